# Optimizing a Trainium2 kernel written in Bass

```python
import jax
import jax.numpy as jnp
from jax import lax
import numpy as np

D_MODEL = 4096
BATCH = 1
SEQ = 16384
DEPTH = 2

HEAD_DIM = 128
ROPE_THETA = 10000.0
NORM_EPS = 1e-6
Q_BLOCK = 128
NEG = -1e30
FORCED = 1e9

NSA_HEADS = D_MODEL // (2 * HEAD_DIM)
NSA_KV_HEADS = NSA_HEADS // 4
NSA_REP = NSA_HEADS // NSA_KV_HEADS
CMP_LEN = 32
CMP_STRIDE = 16
SLC_LEN = 64
SLC_TOPK = 16
WINDOW = 512
PHI_HIDDEN = 512

SGU_WIDTH = D_MODEL // 4
SGU_GROUPS = 8
SGU_GROUP_DIM = SGU_WIDTH // SGU_GROUPS
SGU_CHUNK = 128

MOBA_HEADS = D_MODEL // (4 * HEAD_DIM)
MOBA_BLOCK = 256
MOBA_TOPK = 3

D_FF = 4 * D_MODEL

NSA_W = NSA_HEADS * HEAD_DIM
NSA_KV_W = NSA_KV_HEADS * HEAD_DIM
MOBA_W = MOBA_HEADS * HEAD_DIM
IN_SIZES = (NSA_W, NSA_KV_W, NSA_KV_W, NSA_KV_W, NSA_KV_W, NSA_KV_W, NSA_KV_W, 3 * NSA_HEADS,
            SGU_WIDTH, SGU_WIDTH, MOBA_W, MOBA_W, MOBA_W, D_MODEL, D_MODEL, D_MODEL)
IN_OFFSETS = tuple(int(o) for o in np.cumsum(IN_SIZES)[:-1])
IN_WIDTH = sum(IN_SIZES)

kernel_name = 'hybrid_nsa_sgu_moba_block'


def rms_norm(x, g):
    xf = x.astype(jnp.float32)
    y = xf * lax.rsqrt(jnp.mean(xf * xf, axis=-1, keepdims=True) + NORM_EPS)
    return (y * g.astype(jnp.float32)).astype(x.dtype)


def rope_tables(positions):
    inv = ROPE_THETA ** (-jnp.arange(0, HEAD_DIM, 2, dtype=jnp.float32) / HEAD_DIM)
    ang = positions.astype(jnp.float32)[..., None] * inv
    return jnp.cos(ang), jnp.sin(ang)


def apply_rope(x, cos, sin):
    xf = x.astype(jnp.float32)
    x1, x2 = jnp.split(xf, 2, axis=-1)
    c = cos[:, :, None, :]
    s = sin[:, :, None, :]
    return jnp.concatenate([x1 * c - x2 * s, x2 * c + x1 * s], axis=-1).astype(x.dtype)


def masked_softmax(s, mask):
    s = jnp.where(mask, s, NEG)
    e = jnp.where(mask, jnp.exp(s - jnp.max(s, axis=-1, keepdims=True)), 0.0)
    return e / jnp.maximum(jnp.sum(e, axis=-1, keepdims=True), 1e-30)


def nsa_compress(k_raw, pe, w1, w2):
    b, s = k_raw.shape[:2]
    n_cmp = (s - CMP_LEN) // CMP_STRIDE + 1
    idx = jnp.arange(n_cmp)[:, None] * CMP_STRIDE + jnp.arange(CMP_LEN)[None, :]
    blk = k_raw[:, idx] + pe[None, None, :, None, :]
    blk = jnp.moveaxis(blk, 3, 2).reshape(b, n_cmp, NSA_KV_HEADS, CMP_LEN * HEAD_DIM)
    return jax.nn.gelu(blk @ w1) @ w2


def nsa_attend(q, q_rot, k_cmp, v_cmp, k_slc, v_slc, k_win, v_win, gates):
    s_len = q.shape[0]
    n_cmp = k_cmp.shape[0]
    n_slc = s_len // SLC_LEN
    k_sel = min(SLC_TOPK, n_slc)
    scale = HEAD_DIM ** -0.5
    cmp_start = jnp.arange(n_cmp) * CMP_STRIDE
    cmp_end = cmp_start + CMP_LEN - 1
    blk_id = jnp.arange(n_slc)
    overlap = ((cmp_start[:, None] <= blk_id[None, :] * SLC_LEN + SLC_LEN - 1)
               & (cmp_end[:, None] >= blk_id[None, :] * SLC_LEN)).astype(jnp.float32)
    ks_blocks = k_slc.reshape(n_slc, SLC_LEN, NSA_KV_HEADS, HEAD_DIM).transpose(2, 0, 1, 3)
    vs_blocks = v_slc.reshape(n_slc, SLC_LEN, NSA_KV_HEADS, HEAD_DIM).transpose(2, 0, 1, 3)
    kw_pad = jnp.pad(k_win, ((WINDOW, 0), (0, 0), (0, 0)))
    vw_pad = jnp.pad(v_win, ((WINDOW, 0), (0, 0), (0, 0)))
    g_idx = jnp.arange(NSA_KV_HEADS)[:, None, None]

    def block(i):
        q0 = i * Q_BLOCK
        t = q0 + jnp.arange(Q_BLOCK)
        qc = lax.dynamic_slice_in_dim(q, q0, Q_BLOCK, 0).reshape(Q_BLOCK, NSA_KV_HEADS, NSA_REP, HEAD_DIM)
        qr = lax.dynamic_slice_in_dim(q_rot, q0, Q_BLOCK, 0).reshape(Q_BLOCK, NSA_KV_HEADS, NSA_REP, HEAD_DIM)
        g = lax.dynamic_slice_in_dim(gates, q0, Q_BLOCK, 0).reshape(Q_BLOCK, 3, NSA_KV_HEADS, NSA_REP)
        s_c = jnp.einsum('tgrd,ngd->grtn', qc, k_cmp, preferred_element_type=jnp.float32) * scale
        p_c = masked_softmax(s_c, (cmp_end[None, :] <= t[:, None])[None, None])
        o_c = jnp.einsum('grtn,ngd->tgrd', p_c.astype(v_cmp.dtype), v_cmp)
        imp = jnp.einsum('grtn,nj->gtj', p_c, overlap)
        cur = t // SLC_LEN
        j = blk_id[None, :]
        forced = (j == 0) | (j == cur[:, None]) | (j == cur[:, None] - 1)
        allowed = j <= cur[:, None]
        score = jnp.where(allowed, jnp.where(forced, FORCED, imp), NEG)
        _, idx = lax.top_k(score, k_sel)
        ks_sel = ks_blocks[g_idx, idx].reshape(NSA_KV_HEADS, Q_BLOCK, k_sel * SLC_LEN, HEAD_DIM)
        vs_sel = vs_blocks[g_idx, idx].reshape(NSA_KV_HEADS, Q_BLOCK, k_sel * SLC_LEN, HEAD_DIM)
        pos = idx[..., None] * SLC_LEN + jnp.arange(SLC_LEN)
        m_s = (idx[..., None] <= cur[None, :, None, None]) & (pos <= t[None, :, None, None])
        m_s = m_s.reshape(NSA_KV_HEADS, Q_BLOCK, k_sel * SLC_LEN)[:, None]
        s_s = jnp.einsum('tgrd,gtnd->grtn', qr, ks_sel, preferred_element_type=jnp.float32) * scale
        p_s = masked_softmax(s_s, m_s)
        o_s = jnp.einsum('grtn,gtnd->tgrd', p_s.astype(vs_sel.dtype), vs_sel)
        kw = lax.dynamic_slice_in_dim(kw_pad, q0, Q_BLOCK + WINDOW, 0)
        vw = lax.dynamic_slice_in_dim(vw_pad, q0, Q_BLOCK + WINDOW, 0)
        pos_w = q0 - WINDOW + jnp.arange(Q_BLOCK + WINDOW)
        m_w = ((pos_w[None, :] <= t[:, None]) & (pos_w[None, :] > t[:, None] - WINDOW)
               & (pos_w[None, :] >= 0))
        s_w = jnp.einsum('tgrd,ngd->grtn', qr, kw, preferred_element_type=jnp.float32) * scale
        p_w = masked_softmax(s_w, m_w[None, None])
        o_w = jnp.einsum('grtn,ngd->tgrd', p_w.astype(vw.dtype), vw)
        o = g[:, 0, :, :, None] * o_c + g[:, 1, :, :, None] * o_s + g[:, 2, :, :, None] * o_w
        return o.reshape(Q_BLOCK, NSA_W)

    out = lax.map(block, jnp.arange(s_len // Q_BLOCK))
    return out.reshape(s_len, NSA_W)


def moba_attend(q, k, v):
    s_len = q.shape[0]
    n_blk = -(-s_len // MOBA_BLOCK)
    k_top = min(MOBA_TOPK, n_blk)
    pad = n_blk * MOBA_BLOCK - s_len
    scale = HEAD_DIM ** -0.5
    k_pad = jnp.pad(k, ((0, pad), (0, 0), (0, 0)))
    v_pad = jnp.pad(v, ((0, pad), (0, 0), (0, 0)))
    kb = k_pad.reshape(n_blk, MOBA_BLOCK, MOBA_HEADS, HEAD_DIM).transpose(2, 0, 1, 3)
    vb = v_pad.reshape(n_blk, MOBA_BLOCK, MOBA_HEADS, HEAD_DIM).transpose(2, 0, 1, 3)
    k_mean = jnp.mean(kb.astype(jnp.float32), axis=2).astype(k.dtype)
    h_idx = jnp.arange(MOBA_HEADS)[:, None, None]
    blk_id = jnp.arange(n_blk)
    n_sel = k_top * MOBA_BLOCK

    def block(i):
        q0 = i * Q_BLOCK
        t = q0 + jnp.arange(Q_BLOCK)
        cur = q0 // MOBA_BLOCK
        qb = lax.dynamic_slice_in_dim(q, q0, Q_BLOCK, 0)
        s_g = jnp.einsum('thd,hnd->htn', qb, k_mean, preferred_element_type=jnp.float32)
        _, idx = lax.top_k(jnp.where(blk_id < cur, s_g, NEG), k_top)
        k_sel = kb[h_idx, idx].reshape(MOBA_HEADS, Q_BLOCK, n_sel, HEAD_DIM)
        v_sel = vb[h_idx, idx].reshape(MOBA_HEADS, Q_BLOCK, n_sel, HEAD_DIM)
        m_sel = jnp.repeat(idx < cur, MOBA_BLOCK, axis=-1)
        start = cur * MOBA_BLOCK
        k_own = lax.dynamic_slice_in_dim(k_pad, start, MOBA_BLOCK, 0)
        v_own = lax.dynamic_slice_in_dim(v_pad, start, MOBA_BLOCK, 0)
        m_own = (start + jnp.arange(MOBA_BLOCK))[None, :] <= t[:, None]
        s_sel = jnp.einsum('thd,htnd->htn', qb, k_sel, preferred_element_type=jnp.float32)
        s_own = jnp.einsum('thd,nhd->htn', qb, k_own, preferred_element_type=jnp.float32)
        s = jnp.concatenate([s_sel, s_own], axis=-1) * scale
        m = jnp.concatenate([m_sel, jnp.broadcast_to(m_own[None], (MOBA_HEADS, Q_BLOCK, MOBA_BLOCK))], axis=-1)
        p = masked_softmax(s, m).astype(v.dtype)
        o = (jnp.einsum('htn,htnd->thd', p[..., :n_sel], v_sel)
             + jnp.einsum('htn,nhd->thd', p[..., n_sel:], v_own))
        return o.reshape(Q_BLOCK, MOBA_W)

    out = lax.map(block, jnp.arange(s_len // Q_BLOCK))
    return out.reshape(s_len, MOBA_W)


def sgu_mix(u, v, gain, w_s, b_s):
    b, s = v.shape[:2]
    nc = s // SGU_CHUNK
    v = rms_norm(v.reshape(b, s, SGU_GROUPS, SGU_GROUP_DIM), gain.reshape(SGU_GROUPS, SGU_GROUP_DIM))
    w = jnp.where(jnp.tril(jnp.ones((SGU_CHUNK, SGU_CHUNK), dtype=bool)), w_s, 0.0)
    v = v.reshape(b, nc, SGU_CHUNK, SGU_GROUPS, SGU_GROUP_DIM)
    mixed = jnp.einsum('gts,bcsgd->bctgd', w, v) + b_s.T[None, None, :, :, None]
    return u * mixed.reshape(b, s, SGU_WIDTH)


def token_mixers(h, cos, sin, w_in, nsa_gate_b, nsa_q_norm, nsa_kc_norm, nsa_ks_norm, nsa_kw_norm,
                 phi_pe_k, phi_w1_k, phi_w2_k, phi_pe_v, phi_w1_v, phi_w2_v,
                 sgu_norm, sgu_w, sgu_b, moba_q_norm, moba_k_norm, proj_a, proj_b, proj_c, w_out):
    b, s, _ = h.shape
    z = h @ w_in
    (qa, kc, vc, ks, vs, kw, vw, ga, ub, vb, qc, kcm, vcm, gm_a, gm_b, gm_c) = jnp.split(z, IN_OFFSETS, axis=-1)
    qa = rms_norm(qa.reshape(b, s, NSA_HEADS, HEAD_DIM), nsa_q_norm)
    qa_rot = apply_rope(qa, cos, sin)
    kc = rms_norm(nsa_compress(kc.reshape(b, s, NSA_KV_HEADS, HEAD_DIM), phi_pe_k, phi_w1_k, phi_w2_k), nsa_kc_norm)
    vc = nsa_compress(vc.reshape(b, s, NSA_KV_HEADS, HEAD_DIM), phi_pe_v, phi_w1_v, phi_w2_v)
    ks = apply_rope(rms_norm(ks.reshape(b, s, NSA_KV_HEADS, HEAD_DIM), nsa_ks_norm), cos, sin)
    vs = vs.reshape(b, s, NSA_KV_HEADS, HEAD_DIM)
    kw = apply_rope(rms_norm(kw.reshape(b, s, NSA_KV_HEADS, HEAD_DIM), nsa_kw_norm), cos, sin)
    vw = vw.reshape(b, s, NSA_KV_HEADS, HEAD_DIM)
    ga = jax.nn.sigmoid(ga + nsa_gate_b).reshape(b, s, 3, NSA_HEADS)
    o_a = jax.vmap(nsa_attend)(qa, qa_rot, kc, vc, ks, vs, kw, vw, ga)
    o_b = sgu_mix(jax.nn.gelu(ub), jax.nn.gelu(vb), sgu_norm, sgu_w, sgu_b)
    qc = apply_rope(rms_norm(qc.reshape(b, s, MOBA_HEADS, HEAD_DIM), moba_q_norm), cos, sin)
    kcm = apply_rope(rms_norm(kcm.reshape(b, s, MOBA_HEADS, HEAD_DIM), moba_k_norm), cos, sin)
    vcm = vcm.reshape(b, s, MOBA_HEADS, HEAD_DIM)
    o_c = jax.vmap(moba_attend)(qc, kcm, vcm)
    y = (jax.nn.sigmoid(gm_a) * (o_a @ proj_a)
         + jax.nn.sigmoid(gm_b) * (o_b @ proj_b)
         + jax.nn.sigmoid(gm_c) * (o_c @ proj_c))
    return y @ w_out


def sq_relu_mlp(h, w1, w2):
    return jnp.square(jax.nn.relu(h @ w1)) @ w2


def setup_inputs(seed: int = 0) -> dict:
    key = jax.random.key(seed)
    k = jax.random.split(key, 27)
    L = DEPTH

    def nrm(kk, shape, scale):
        return jax.random.normal(kk, shape, jnp.float32) * scale

    def gain(kk, shape):
        return 1.0 + 0.02 * jax.random.normal(kk, shape, jnp.float32)

    return {
        'x': nrm(k[0], (BATCH, SEQ, D_MODEL), 1.0),
        'positions': jnp.broadcast_to(jnp.arange(SEQ, dtype=jnp.int32)[None, :], (BATCH, SEQ)),
        'norm_mix': gain(k[1], (L, D_MODEL)),
        'norm_mlp': gain(k[2], (L, D_MODEL)),
        'w_in': nrm(k[3], (L, D_MODEL, IN_WIDTH), D_MODEL ** -0.5),
        'nsa_gate_b': nrm(k[4], (L, 3 * NSA_HEADS), 0.1),
        'nsa_q_norm': gain(k[5], (L, HEAD_DIM)),
        'nsa_kc_norm': gain(k[6], (L, HEAD_DIM)),
        'nsa_ks_norm': gain(k[7], (L, HEAD_DIM)),
        'nsa_kw_norm': gain(k[8], (L, HEAD_DIM)),
        'phi_pe_k': nrm(k[9], (L, CMP_LEN, HEAD_DIM), 0.1),
        'phi_w1_k': nrm(k[10], (L, CMP_LEN * HEAD_DIM, PHI_HIDDEN), (CMP_LEN * HEAD_DIM) ** -0.5),
        'phi_w2_k': nrm(k[11], (L, PHI_HIDDEN, HEAD_DIM), PHI_HIDDEN ** -0.5),
        'phi_pe_v': nrm(k[12], (L, CMP_LEN, HEAD_DIM), 0.1),
        'phi_w1_v': nrm(k[13], (L, CMP_LEN * HEAD_DIM, PHI_HIDDEN), (CMP_LEN * HEAD_DIM) ** -0.5),
        'phi_w2_v': nrm(k[14], (L, PHI_HIDDEN, HEAD_DIM), PHI_HIDDEN ** -0.5),
        'sgu_norm': gain(k[15], (L, SGU_WIDTH)),
        'sgu_w': nrm(k[16], (L, SGU_GROUPS, SGU_CHUNK, SGU_CHUNK), 0.5 * SGU_CHUNK ** -0.5),
        'sgu_b': 1.0 + nrm(k[17], (L, SGU_GROUPS, SGU_CHUNK), 0.02),
        'moba_q_norm': gain(k[18], (L, HEAD_DIM)),
        'moba_k_norm': gain(k[19], (L, HEAD_DIM)),
        'proj_a': nrm(k[20], (L, NSA_W, D_MODEL), NSA_W ** -0.5),
        'proj_b': nrm(k[21], (L, SGU_WIDTH, D_MODEL), SGU_WIDTH ** -0.5),
        'proj_c': nrm(k[22], (L, MOBA_W, D_MODEL), MOBA_W ** -0.5),
        'w_out': nrm(k[23], (L, D_MODEL, D_MODEL), D_MODEL ** -0.5),
        'mlp_w1': nrm(k[24], (L, D_MODEL, D_FF), D_MODEL ** -0.5),
        'mlp_w2': nrm(k[25], (L, D_FF, D_MODEL), D_FF ** -0.5),
    }


def reference(x, positions, norm_mix, norm_mlp, w_in, nsa_gate_b, nsa_q_norm, nsa_kc_norm, nsa_ks_norm,
              nsa_kw_norm, phi_pe_k, phi_w1_k, phi_w2_k, phi_pe_v, phi_w1_v, phi_w2_v, sgu_norm, sgu_w, sgu_b,
              moba_q_norm, moba_k_norm, proj_a, proj_b, proj_c, w_out, mlp_w1, mlp_w2):
    cos, sin = rope_tables(positions)
    for l in range(DEPTH):
        h = rms_norm(x, norm_mix[l])
        x = x + token_mixers(h, cos, sin, w_in[l], nsa_gate_b[l], nsa_q_norm[l], nsa_kc_norm[l],
                             nsa_ks_norm[l], nsa_kw_norm[l], phi_pe_k[l], phi_w1_k[l], phi_w2_k[l],
                             phi_pe_v[l], phi_w1_v[l], phi_w2_v[l], sgu_norm[l], sgu_w[l], sgu_b[l],
                             moba_q_norm[l], moba_k_norm[l], proj_a[l], proj_b[l], proj_c[l], w_out[l])
        x = x + sq_relu_mlp(rms_norm(x, norm_mlp[l]), mlp_w1[l], mlp_w2[l])
    return x
```

```python
import math
import numpy as np
import ml_dtypes
from contextlib import ExitStack
import concourse.bass as bass
import concourse.mybir as mybir
from concourse.bass_utils import run_bass_kernel_spmd

F32 = mybir.dt.float32
BF16 = mybir.dt.bfloat16
I32 = mybir.dt.int32
AF = mybir.ActivationFunctionType
ALU = mybir.AluOpType
AX = mybir.AxisListType
EPS = 1e-6
NEGBIG = -1e30
TWO_PI = 2.0 * math.pi


class Cfg:
    def __init__(s, S=16384, L=2, DFF=16384, NC=8):
        s.D = 4096
        s.S, s.L, s.DFF, s.NC = S, L, DFF, NC
        s.HD = 128
        s.NH, s.NG, s.MH, s.SW = 16, 4, 8, 1024
        s.KC = s.D // 128
        s.NT = S // 128
        s.NTO = s.NT // NC
        s.NCMP = (S - 32) // 16 + 1
        s.NCC = (s.NCMP + 127) // 128
        s.NSLC = S // 64
        s.NMB = S // 256
        s.TG = 4
        o = {}
        off = 0
        for name, w in (("qa", 2048), ("kc", 512), ("vc", 512), ("ks", 512), ("vs", 512), ("kw", 512),
                        ("vw", 512), ("ga", 48), ("ub", 1024), ("vb", 1024), ("qm", 1024), ("km", 1024),
                        ("vm", 1024), ("gma", 4096), ("gmb", 4096), ("gmc", 4096)):
            o[name] = off
            off += w
        s.off = o
        s.INW = off
        assert s.INW == 22576
        assert s.NT % (NC * 1) == 0 and s.NMB >= 8 and s.NSLC >= 16


class Track:
    __slots__ = ("w", "r")

    def __init__(s):
        s.w = None
        s.r = {}


class Tl:
    def __init__(s, h):
        s.h = h
        s.tr = Track()

    def __getitem__(s, i):
        return s.h[i]


NDS = 8
SAME_ENGINE_SYNC = True


class PB:
    def __init__(s, nc, es):
        s.nc, s.es = nc, es
        s.E = {"pe": nc.tensor, "act": nc.scalar, "dve": nc.vector, "pool": nc.gpsimd, "sp": nc.sync}
        s.sems = []
        s.cidx, s.ccnt = {}, {}
        for e in ("pe", "act", "dve", "pool"):
            s.cidx[e] = len(s.sems)
            s.sems.append(es.enter_context(nc.semaphore("cs_" + e)))
            s.ccnt[e] = 0
        s.dq = {}
        for q in ("sp", "pool", "act"):
            idx = []
            for i in range(NDS):
                idx.append(len(s.sems))
                s.sems.append(es.enter_context(nc.semaphore(f"ds_{q}{i}")))
            s.dq[q] = {"idx": idx, "val": [0] * NDS, "nxt": 0}
        s.seen = {}
        s.nins = 0

    def wait(s, e, tok):
        k = (e, tok[0])
        if e == "pe" and tok[0] == s.cidx["pe"]:
            return
        if not SAME_ENGINE_SYNC and e in s.cidx and tok[0] == s.cidx[e]:
            return
        if s.seen.get(k, 0) >= tok[1]:
            return
        s.E[e].wait_ge(s.sems[tok[0]], tok[1])
        s.seen[k] = tok[1]

    def _deps(s, e, reads, writes):
        for t in reads:
            if t.w is not None:
                s.wait(e, t.w)
        for t in writes:
            if t.w is not None:
                s.wait(e, t.w)
            for tok in t.r.values():
                s.wait(e, tok)

    def _mark(s, tok, reads, writes):
        for t in reads:
            t.r[tok[0]] = tok
        for t in writes:
            t.w = tok
            t.r = {}

    @staticmethod
    def _tr(lst):
        return [x.tr if isinstance(x, Tl) else x for x in lst]

    def op(s, e, fns, reads=(), writes=()):
        reads, writes = s._tr(reads), s._tr(writes)
        s._deps(e, reads, writes)
        if not isinstance(fns, (list, tuple)):
            fns = [fns]
        ins = None
        for f in fns:
            ins = f(s.E[e])
            s.nins += 1
        s.ccnt[e] += 1
        tok = (s.cidx[e], s.ccnt[e])
        ins.then_inc(s.sems[tok[0]], 1)
        s._mark(tok, reads, writes)
        return tok

    def dma(s, q, out, in_, reads=(), writes=(), slow=False):
        reads, writes = s._tr(reads), s._tr(writes)
        d = s.dq[q]
        i = d["nxt"]
        d["nxt"] = (i + 1) % NDS
        si = d["idx"][i]
        if d["val"][i] > 0:
            s.wait(q, (si, d["val"][i]))
        s._deps(q, reads, writes)
        ins = s.E[q].dma_start(out=out, in_=in_, allow_slow_non_contiguous=True) if slow else s.E[q].dma_start(out=out, in_=in_)
        s.nins += 1
        d["val"][i] += 16
        tok = (si, d["val"][i])
        ins.then_inc(s.sems[si], 16)
        s._mark(tok, reads, writes)
        return tok

    def all_tokens(s):
        toks = [(s.cidx[e], s.ccnt[e]) for e in s.cidx if s.ccnt[e] > 0]
        for q in s.dq.values():
            for si, v in zip(q["idx"], q["val"]):
                if v > 0:
                    toks.append((si, v))
        return toks

    def barrier(s, engines=("pe", "act", "dve", "pool", "sp")):
        toks = s.all_tokens()
        for e in engines:
            for t in toks:
                s.wait(e, t)


class Scope:
    uid = 0

    def __init__(s, pb):
        s.pb = pb
        s.es = ExitStack()

    def __enter__(s):
        s.es.__enter__()
        return s

    def __exit__(s, *a):
        s.pb.barrier()
        return s.es.__exit__(*a)

    def sb(s, name, shape, dt):
        Scope.uid += 1
        return Tl(s.es.enter_context(s.pb.nc.sbuf_tensor(f"s{Scope.uid}_" + name, list(shape), dt)))


def bc(ap, shape):
    return ap.broadcast_to(list(shape))


def build(cfg, debug=()):
    c = cfg
    D, S, L, DFF, NC, KC, TG = c.D, c.S, c.L, c.DFF, c.NC, c.KC, c.TG
    NT, NTO, NG, NH, MH = c.NT, c.NTO, c.NG, c.NH, c.MH
    TOK = TG * 128
    nc = bass.Bass("TRN2", target_bir_lowering=False)

    def din(name, shape, dt=F32):
        return nc.dram_tensor(name, list(shape), dt, kind="ExternalInput").ap()

    def dscr(name, shape, dt):
        return nc.dram_tensor(name, list(shape), dt, kind="Internal").ap()

    x_in = din("x", [S, D])
    pos_in = din("pos", [S, 1], I32)
    ownpos_in = din("ownpos", [NTO * 128, 1], I32)
    onehot_in = din("onehot", [128, NC])
    info_all = din("info_all", [NT * 128, 4])
    info_own = din("info_own", [NTO * 128, 4])
    row_all = din("row_all", [NT, 128])
    row_own = din("row_own", [NTO, 128])
    W = {}
    for nm, shp in (("norm_mix", [L, D]), ("norm_mlp", [L, D]), ("w_in", [L, D, c.INW]), ("nsa_gate_b", [L, 48]),
                    ("nsa_q_norm", [L, 128]), ("nsa_kc_norm", [L, 128]), ("nsa_ks_norm", [L, 128]),
                    ("nsa_kw_norm", [L, 128]), ("phi_pe_k", [L, 32, 128]), ("phi_w1_k", [L, 4096, 512]),
                    ("phi_w2_k", [L, 512, 128]), ("phi_pe_v", [L, 32, 128]), ("phi_w1_v", [L, 4096, 512]),
                    ("phi_w2_v", [L, 512, 128]), ("sgu_norm", [L, 1024]), ("sgu_w", [L, 8, 128, 128]),
                    ("sgu_b", [L, 8, 128]), ("moba_q_norm", [L, 128]), ("moba_k_norm", [L, 128]),
                    ("proj_a", [L, 2048, D]), ("proj_b", [L, 1024, D]), ("proj_c", [L, 1024, D]),
                    ("w_out", [L, D, D]), ("mlp_w1", [L, D, DFF]), ("mlp_w2", [L, DFF, D])):
        W[nm] = din(nm, shp)
    c_identb = din("c_identb", [128, 128], BF16)
    c_identf = din("c_identf", [128, 128])
    c_tril = din("c_tril", [128, 128])
    c_ov = din("c_ov", [c.NCC * 128, 256], BF16)
    c_exps = din("c_exps", [128, 64 * 128], BF16)
    c_expm = din("c_expm", [64, 64 * 128], BF16)
    c_invf = din("c_invf", [1, 64])
    c_jrow = din("c_jrow", [1, 256])
    y_out = nc.dram_tensor("y", [NTO * 128, D], F32, kind="ExternalOutput").ap()
    dbg = {}
    for nm, shp, dt in debug:
        dbg[nm] = nc.dram_tensor("dbg_" + nm, list(shp), dt, kind="ExternalOutput").ap()

    wb = {"w_in": dscr("wb_w_in", [D, c.INW], BF16), "phi_w1_k": dscr("wb_p1k", [4096, 512], BF16),
          "phi_w1_v": dscr("wb_p1v", [4096, 512], BF16), "phi_w2_k": dscr("wb_p2k", [512, 128], BF16),
          "phi_w2_v": dscr("wb_p2v", [512, 128], BF16), "proj_a": dscr("wb_pa", [2048, D], BF16),
          "proj_b": dscr("wb_pb", [1024, D], BF16), "proj_c": dscr("wb_pc", [1024, D], BF16),
          "w_out": dscr("wb_wo", [D, D], BF16), "mlp_w1": dscr("wb_w1", [D, DFF], BF16),
          "mlp_w2": dscr("wb_w2", [DFF, D], BF16)}
    wb_tr = {k: Track() for k in wb}
    xs = [None] + [dscr(f"xs{l}", [S, D], F32) for l in range(1, L)]
    xs_tr = [None] + [Track() for _ in range(1, L)]
    kcrawT = dscr("kcrawT", [128, NG, S], BF16)
    vcrawT = dscr("vcrawT", [128, NG, S], BF16)
    ksT = dscr("ksT", [128, NG, S], BF16)
    kwT = dscr("kwT", [128, NG, S], BF16)
    kmT = dscr("kmT", [128, MH, S], BF16)
    vs_s = dscr("vs_s", [S, NG, 128], BF16)
    vw_s = dscr("vw_s", [S, NG, 128], BF16)
    vm_s = dscr("vm_s", [S, MH, 128], BF16)
    kcT_s = dscr("kcT_s", [128, NG, c.NCC * 128], BF16)
    vc_s = dscr("vc_s", [c.NCC * 128, NG, 128], BF16)
    kmeanT_s = dscr("kmeanT_s", [128, MH, c.NMB], BF16)
    kv_tr = Track()
    hT_s = dscr("hT_s", [128, KC, TOK], BF16)
    qnT_s = dscr("qnT_s", [128, NH, TOK], BF16)
    qrT_s = dscr("qrT_s", [128, NH, TOK], BF16)
    qmT_s = dscr("qmT_s", [128, MH, TOK], BF16)
    oT_s = dscr("oT_s", [128, KC, TOK], BF16)
    xg_s = dscr("xg_s", [TG, 128, D], F32)
    grp_tr = Track()

    es = ExitStack()
    with es:
        pb = PB(nc, es)

        def gsb(name, shape, dt):
            return Tl(es.enter_context(nc.sbuf_tensor("g_" + name, list(shape), dt)))

        PS = [Tl(es.enter_context(nc.psum_tensor(f"ps{i}", [128, 512], F32))) for i in range(8)]
        ps_rr = [0]

        def psn():
            t = PS[ps_rr[0] % 8]
            ps_rr[0] += 1
            return t

        identb = gsb("identb", [128, 128], BF16)
        identf = gsb("identf", [128, 128], F32)
        onesf = gsb("onesf", [128, 128], F32)
        invf = gsb("invf", [128, 64], F32)
        pcol = gsb("pcol", [128, 1], F32)
        onehot = gsb("onehot", [128, NC], F32)
        gains = gsb("gains", [128, 6, 128], F32)
        kcg_col = gsb("kcg_col", [128, 1], F32)
        gbias = gsb("gbias", [128, 48], F32)
        sgng = gsb("sgng", [128, 1024], F32)
        sgwT = gsb("sgwT", [128, 8, 128], BF16)
        sgb = gsb("sgb", [128, 8], F32)
        pek = gsb("pek", [128, 32], F32)
        pev = gsb("pev", [128, 32], F32)
        gmix = gsb("gmix", [128, KC], F32)
        gmlp = gsb("gmlp", [128, KC], F32)
        pb.dma("sp", identb[:], c_identb[:, :], writes=[identb])
        pb.dma("sp", identf[:], c_identf[:, :], writes=[identf])
        pb.dma("sp", invf[:], c_invf[0:1, :].partition_broadcast(128), writes=[invf])
        pb.dma("sp", onehot[:], onehot_in[:, :], writes=[onehot])
        pb.op("dve", lambda e: e.memset(onesf[:], 1.0), writes=[onesf])
        pb.op("pool", lambda e: e.iota(pcol[:], [[0, 1]], base=0, channel_multiplier=1,
                                       allow_small_or_imprecise_dtypes=True), writes=[pcol])

        def rope_tables(sc, pos_ap, tag):
            pi_ = sc.sb("rp_i" + tag, [128, 1], I32)
            pf = sc.sb("rp_f" + tag, [128, 1], F32)
            ang = sc.sb("rp_a" + tag, [128, 2, 64], F32)
            kf = sc.sb("rp_k" + tag, [128, 2, 64], F32)
            ki = sc.sb("rp_ki" + tag, [128, 2, 64], I32)
            cs = sc.sb("rp_cs" + tag, [128, 2, 64], F32)
            pb.dma("pool", pi_[:], pos_ap, writes=[pi_])
            pb.op("dve", lambda e: e.tensor_copy(out=pf[:], in_=pi_[:]), reads=[pi_], writes=[pf])
            pb.op("dve", lambda e: e.tensor_scalar(out=ang[:, 0, :], in0=invf[:], scalar1=pf[:, 0:1], scalar2=None,
                                                   op0=ALU.mult), reads=[invf, pf], writes=[ang])
            pb.op("dve", lambda e: e.tensor_scalar(out=ang[:, 1, :], in0=ang[:, 0, :], scalar1=math.pi / 2,
                                                   scalar2=None, op0=ALU.add), reads=[ang], writes=[ang])
            pb.op("dve", lambda e: e.tensor_scalar(out=kf[:], in0=ang[:], scalar1=1.0 / TWO_PI, scalar2=None,
                                                   op0=ALU.mult), reads=[ang], writes=[kf])
            pb.op("dve", lambda e: e.tensor_copy(out=ki[:], in_=kf[:]), reads=[kf], writes=[ki])
            pb.op("dve", lambda e: e.tensor_copy(out=kf[:], in_=ki[:]), reads=[ki], writes=[kf])
            C1 = 6.28125
            C2 = TWO_PI - C1
            pb.op("dve", lambda e: e.scalar_tensor_tensor(out=ang[:], in0=kf[:], scalar=-C1, in1=ang[:],
                                                          op0=ALU.mult, op1=ALU.add), reads=[kf, ang], writes=[ang])
            pb.op("dve", lambda e: e.scalar_tensor_tensor(out=ang[:], in0=kf[:], scalar=-C2, in1=ang[:],
                                                          op0=ALU.mult, op1=ALU.add), reads=[kf, ang], writes=[ang])
            pb.op("dve", lambda e: e.tensor_scalar(out=kf[:], in0=ang[:], scalar1=0.0, scalar2=TWO_PI,
                                                   op0=ALU.is_lt, op1=ALU.mult), reads=[ang], writes=[kf])
            pb.op("dve", lambda e: e.tensor_tensor(out=ang[:], in0=ang[:], in1=kf[:], op=ALU.add),
                  reads=[ang, kf], writes=[ang])
            pb.op("dve", lambda e: e.tensor_scalar(out=kf[:], in0=ang[:], scalar1=TWO_PI, scalar2=-TWO_PI,
                                                   op0=ALU.is_ge, op1=ALU.mult), reads=[ang], writes=[kf])
            pb.op("dve", lambda e: e.tensor_tensor(out=ang[:], in0=ang[:], in1=kf[:], op=ALU.add),
                  reads=[ang, kf], writes=[ang])
            pb.op("dve", lambda e: e.tensor_scalar(out=ang[:], in0=ang[:], scalar1=-1.0, scalar2=math.pi,
                                                   op0=ALU.mult, op1=ALU.add), reads=[ang], writes=[ang])
            pb.op("act", lambda e: e.activation(out=cs[:], in_=ang[:], func=AF.Sin), reads=[ang], writes=[cs])
            return cs

        def norm_T(sc, xt, hT, t, tag, xap=None):
            sq = sc_tmp["sq"]
            ss = sc.sb("nt_ss" + tag, [128, 1], F32)
            xb = sc_tmp["xb"]
            xap = xt[:] if xap is None else xap
            pb.op("act", lambda e: e.activation(out=sq[:], in_=xap, func=AF.Square), reads=[xt], writes=[sq])
            pb.op("dve", lambda e: e.tensor_reduce(out=ss[:], in_=sq[:], axis=AX.X, op=ALU.add), reads=[sq],
                  writes=[ss])
            rstd_from_ss(ss, ss, 1.0 / D)
            pb.op("act", lambda e: e.activation(out=xb[:], in_=xap, func=AF.Copy, scale=ss[:, 0:1]),
                  reads=[xt, ss], writes=[xb])
            transpose_to(xb, KC, lambda k0, n: hT[:, k0:k0 + n, t * 128:(t + 1) * 128], [hT])

        def rstd_from_ss(out, ss, inv_n):
            pb.op("dve", lambda e: e.tensor_scalar(out=out[:], in0=ss[:], scalar1=inv_n, scalar2=EPS, op0=ALU.mult,
                                                   op1=ALU.add), reads=[ss], writes=[out])
            pb.op("act", lambda e: e.activation(out=out[:], in_=out[:], func=AF.Sqrt), reads=[out], writes=[out])
            pb.op("dve", lambda e: e.reciprocal(out=out[:], in_=out[:]), reads=[out], writes=[out])

        def transpose_to(src, nblk, dst_fn, dst_tiles, src_fn=None, bank=None):
            k0 = 0
            i = 0
            while k0 < nblk:
                n = min(8, nblk - k0)
                pt = psn() if bank is None else bank
                ptb = pt[:].bitcast(BF16)
                fns = []
                for j in range(n):
                    sap = src_fn(k0 + j) if src_fn else src[:, (k0 + j) * 128:(k0 + j + 1) * 128]
                    fns.append(lambda e, j=j, sap=sap: e.transpose(out=ptb[:, j * 128:(j + 1) * 128], in_=sap,
                                                                   identity=identb[:]))
                pb.op("pe", fns, reads=[src, identb], writes=[pt])
                dst = dst_fn(k0, n)
                srcv = ptb[:, 0:n * 128].rearrange("p (n f) -> p n f", f=128)
                eng = "act" if (i % 2 == 0) else "dve"
                if eng == "act":
                    pb.op("act", lambda e: e.copy(out=dst, in_=srcv), reads=[pt], writes=dst_tiles)
                else:
                    pb.op("dve", lambda e: e.tensor_copy(out=dst, in_=srcv), reads=[pt], writes=dst_tiles)
                k0 += n
                i += 1

        wbufs = []
        wb_rr = [0]

        def wload(wname, k0, nk, c0, ncols):
            t = wbufs[wb_rr[0] % len(wbufs)]
            wb_rr[0] += 1
            src = wb[wname][k0 * 128:(k0 + nk) * 128, c0:c0 + ncols].rearrange("(k p) n -> p k n", p=128)
            pb.dma("sp", t[:, 0:nk, 0:ncols], src, reads=[wb_tr[wname]], writes=[t])
            return t

        def gemm(wname, c0, ncols, act, kc0, nkc, ntile, mode, epi, wrow0=0):
            nout = ntile if mode == "tok" else ncols // 128
            pss = [psn() for _ in range(nout)]
            kk = 0
            while kk < nkc:
                nk = min(16, nkc - kk)
                wt = wload(wname, wrow0 + kk, nk, c0, ncols)
                for i in range(nout):
                    fns = []
                    for k in range(nk):
                        if mode == "tok":
                            l_ap = act[:, kc0 + kk + k, i * 128:(i + 1) * 128]
                            r_ap = wt[:, k, 0:ncols]
                            o_ap = pss[i][:, 0:ncols]
                        else:
                            l_ap = wt[:, k, i * 128:(i + 1) * 128]
                            r_ap = act[:, kc0 + kk + k, 0:ntile * 128]
                            o_ap = pss[i][:, 0:ntile * 128]
                        fns.append(lambda e, l_ap=l_ap, r_ap=r_ap, o_ap=o_ap, st=(kk + k == 0),
                                   sp_=(kk + k == nkc - 1): e.matmul(o_ap, lhsT=l_ap, rhs=r_ap, start=st, stop=sp_))
                    pb.op("pe", fns, reads=[act, wt], writes=[pss[i]])
                kk += nk
            for i in range(nout):
                epi(i, pss[i])

        def precast(sc, l):
            st = [sc.sb(f"pc_f{i}", [128, 4096], F32) for i in range(2)]
            sb_ = [sc.sb(f"pc_b{i}", [128, 4096], BF16) for i in range(2)]
            pb.dma("sp", gmix[:], W["norm_mix"][l:l + 1, :].rearrange("o (k p) -> p (o k)", p=128), writes=[gmix], slow=True)
            pb.dma("sp", gmlp[:], W["norm_mlp"][l:l + 1, :].rearrange("o (k p) -> p (o k)", p=128), writes=[gmlp], slow=True)
            i = 0
            for nm in wb:
                src = W[nm][l]
                R, C = src.shape
                g = gmix if nm == "w_in" else (gmlp if nm == "mlp_w1" else None)
                for r in range(R // 128):
                    cc = 0
                    while cc < C:
                        n = min(4096, C - cc)
                        a, b = st[i % 2], sb_[i % 2]
                        pb.dma("sp", a[:, 0:n], src[r * 128:(r + 1) * 128, cc:cc + n], writes=[a])
                        eng = ("dve", "act", "pool")[i % 3] if g is None else ("dve", "pool")[i % 2]
                        if g is None:
                            if eng == "act":
                                pb.op("act", lambda e, a=a, b=b, n=n: e.copy(out=b[:, 0:n], in_=a[:, 0:n]),
                                      reads=[a], writes=[b])
                            else:
                                pb.op(eng, lambda e, a=a, b=b, n=n: e.tensor_copy(out=b[:, 0:n], in_=a[:, 0:n]),
                                      reads=[a], writes=[b])
                        else:
                            pb.op(eng, lambda e, a=a, b=b, n=n, r=r, g=g: e.tensor_scalar(
                                out=b[:, 0:n], in0=a[:, 0:n], scalar1=g[:, r:r + 1], scalar2=None, op0=ALU.mult),
                                reads=[a, g], writes=[b])
                        pb.dma("pool", wb[nm][r * 128:(r + 1) * 128, cc:cc + n], b[:, 0:n], reads=[b],
                               writes=[wb_tr[nm]])
                        cc += n
                        i += 1

        def load_layer_params(sc, l):
            for gi, nm in enumerate(("nsa_q_norm", "nsa_kc_norm", "nsa_ks_norm", "nsa_kw_norm", "moba_q_norm",
                                     "moba_k_norm")):
                pb.dma("pool", gains[:, gi, :], W[nm][l:l + 1, :].partition_broadcast(128), writes=[gains])
            pb.op("dve", lambda e: e.tensor_scalar(out=gains[:, 0, :], in0=gains[:, 0, :], scalar1=128 ** -0.5,
                                                   scalar2=None, op0=ALU.mult), reads=[gains], writes=[gains])
            pb.op("dve", lambda e: e.tensor_scalar(out=gains[:, 4, :], in0=gains[:, 4, :], scalar1=128 ** -0.5,
                                                   scalar2=None, op0=ALU.mult), reads=[gains], writes=[gains])
            pb.dma("pool", kcg_col[:], W["nsa_kc_norm"][l:l + 1, :].rearrange("o p -> p o"), writes=[kcg_col], slow=True)
            pb.dma("pool", gbias[:], W["nsa_gate_b"][l:l + 1, :].partition_broadcast(128), writes=[gbias])
            pb.dma("pool", sgng[:], W["sgu_norm"][l:l + 1, :].partition_broadcast(128), writes=[sgng])
            pb.dma("pool", sgb[:], W["sgu_b"][l].rearrange("g t -> t g"), writes=[sgb], slow=True)
            pb.dma("pool", pek[:], W["phi_pe_k"][l].rearrange("l d -> d l"), writes=[pek], slow=True)
            pb.dma("pool", pev[:], W["phi_pe_v"][l].rearrange("l d -> d l"), writes=[pev], slow=True)
            tril = sc.sb("lp_tril", [128, 128], F32)
            wtmp = sc.sb("lp_w", [128, 128], F32)
            pb.dma("pool", tril[:], c_tril[:, :], writes=[tril])
            for g in range(8):
                pb.dma("pool", wtmp[:], W["sgu_w"][l, g], writes=[wtmp])
                pb.op("dve", lambda e: e.tensor_tensor(out=wtmp[:], in0=wtmp[:], in1=tril[:], op=ALU.mult),
                      reads=[wtmp, tril], writes=[wtmp])
                pt = psn()
                pb.op("pe", lambda e: e.transpose(out=pt[:, 0:128], in_=wtmp[:], identity=identf[:]),
                      reads=[wtmp, identf], writes=[pt])
                pb.op("dve", lambda e, g=g: e.tensor_copy(out=sgwT[:, g, :], in_=pt[:, 0:128]), reads=[pt],
                      writes=[sgwT])

        sc_tmp = {}

        def headnorm_rope(sc, ps, nheads, gain_idx, cs, out_rot, out_plain, tag):
            W_ = nheads * 128
            sq = sc_tmp["hn_sq"]
            xn = sc_tmp["hn_xn"]
            ss = sc.sb("hn_ss" + tag, [128, 4], F32)
            pb.op("act", lambda e: e.activation(out=sq[:, 0:W_], in_=ps[:, 0:W_], func=AF.Square), reads=[ps],
                  writes=[sq])
            pb.op("dve", lambda e: e.tensor_reduce(out=ss[:, 0:nheads],
                                                   in_=sq[:, 0:W_].rearrange("p (h d) -> p h d", d=128),
                                                   axis=AX.X, op=ALU.add), reads=[sq], writes=[ss])
            rstd_from_ss(ss, ss, 1.0 / 128)
            xn3 = xn[:, 0:W_].rearrange("p (h d) -> p h d", d=128)
            pb.op("dve", lambda e: e.tensor_tensor(out=xn3, in0=ps[:, 0:W_].rearrange("p (h d) -> p h d", d=128),
                                                   in1=bc(ss[:, 0:nheads].unsqueeze(2), [128, nheads, 128]),
                                                   op=ALU.mult), reads=[ps, ss], writes=[xn])
            gr = bc(gains[:, gain_idx:gain_idx + 1, :], [128, nheads, 128])
            if out_plain is not None:
                pb.op("pool", lambda e: e.tensor_tensor(out=out_plain, in0=xn3, in1=gr, op=ALU.mult),
                      reads=[xn, gains], writes=[sc_tmp["hn_outp"]])
            if out_rot is not None:
                pb.op("dve", lambda e: e.tensor_tensor(out=xn3, in0=xn3, in1=gr, op=ALU.mult), reads=[xn, gains],
                      writes=[xn])
                t1 = sc_tmp["hn_t1"]
                t2 = sc_tmp["hn_t2"]
                x1 = xn3[:, :, 0:64]
                x2 = xn3[:, :, 64:128]
                cosb = bc(cs[:, 1:2, :], [128, nheads, 64])
                sinb = bc(cs[:, 0:1, :], [128, nheads, 64])
                t1v = t1[:, 0:nheads * 64].rearrange("p (h d) -> p h d", d=64)
                t2v = t2[:, 0:nheads * 64].rearrange("p (h d) -> p h d", d=64)
                pb.op("dve", lambda e: e.tensor_tensor(out=t1v, in0=x1, in1=cosb, op=ALU.mult), reads=[xn, cs],
                      writes=[t1])
                pb.op("pool", lambda e: e.tensor_tensor(out=t2v, in0=x2, in1=sinb, op=ALU.mult), reads=[xn, cs],
                      writes=[t2])
                pb.op("dve", lambda e: e.tensor_tensor(out=out_rot[:, :, 0:64], in0=t1v, in1=t2v, op=ALU.subtract),
                      reads=[t1, t2], writes=[sc_tmp["hn_outr"]])
                pb.op("dve", lambda e: e.tensor_tensor(out=t1v, in0=x2, in1=cosb, op=ALU.mult), reads=[xn, cs],
                      writes=[t1])
                pb.op("pool", lambda e: e.tensor_tensor(out=t2v, in0=x1, in1=sinb, op=ALU.mult), reads=[xn, cs],
                      writes=[t2])
                pb.op("dve", lambda e: e.tensor_tensor(out=out_rot[:, :, 64:128], in0=t1v, in1=t2v, op=ALU.add),
                      reads=[t1, t2], writes=[sc_tmp["hn_outr"]])

        def alloc_common_tmps(sc):
            sc_tmp["xb"] = sc.sb("t_xb", [128, D], BF16)
            sc_tmp["sq"] = sc_tmp["xb"]
            sc_tmp["hn_sq"] = sc.sb("t_hnsq", [128, 512], F32)
            sc_tmp["hn_xn"] = sc.sb("t_hnxn", [128, 512], F32)
            sc_tmp["hn_t1"] = sc.sb("t_hnt1", [128, 256], F32)
            sc_tmp["hn_t2"] = sc.sb("t_hnt2", [128, 256], F32)
            sc_tmp["hn_outp"] = sc.sb("t_hnop", [128, 512], BF16)
            sc_tmp["hn_outr"] = sc.sb("t_hnor", [128, 512], BF16)
            wbufs.clear()
            for i in range(2):
                wbufs.append(sc.sb(f"wbuf{i}", [128, 16, 512], BF16))

        def load_x_tile(sc, l, xt, tile_idx):
            if l == 0:
                pb.dma("sp", xt[:], x_in[tile_idx * 128:(tile_idx + 1) * 128, :], writes=[xt])
            else:
                pb.dma("sp", xt[:], xs[l][tile_idx * 128:(tile_idx + 1) * 128, :], reads=[xs_tr[l]], writes=[xt])

        def kv_pass(l):
            with Scope(pb) as sc:
                alloc_common_tmps(sc)
                hT = sc.sb("kv_hT", [128, KC, TOK], BF16)
                xts = [sc.sb(f"kv_x{i}", [128, D], F32) for i in range(2)]
                stage = [sc.sb(f"kv_st{i}", [128, 4, 128], BF16) for i in range(2)]
                st_i = [0]
                for g0 in range(0, NT, TG):
                    sg_ = Scope(pb)
                    sg_.__enter__()
                    css = []
                    for t in range(TG):
                        xt = xts[t % 2]
                        load_x_tile(sc, l, xt, g0 + t)
                        norm_T(sg_, xt, hT, t, f"kv{t % 2}")
                        css.append(rope_tables(sg_, pos_in[(g0 + t) * 128:(g0 + t + 1) * 128, :], f"kv{t}"))

                    def epi_factory(kind, dstT, dstV, h0, gain_idx):
                        def epi(t, ps):
                            tok0 = (g0 + t) * 128
                            if kind == "v":
                                sg = stage[st_i[0] % 2]
                                st_i[0] += 1
                                pb.op("act", lambda e: e.copy(out=sg[:].rearrange("p h d -> p (h d)"), in_=ps[:]),
                                      reads=[ps], writes=[sg])
                                pb.dma("pool", dstV[tok0:tok0 + 128, h0:h0 + 4, :], sg[:], reads=[sg],
                                       writes=[kv_tr])
                                return
                            src = sc_tmp["hn_outr"]
                            if kind == "kraw":
                                pb.op("act", lambda e: e.copy(out=src[:], in_=ps[:]), reads=[ps], writes=[src])
                            else:
                                headnorm_rope(sg_, ps, 4, gain_idx, css[t],
                                              src[:].rearrange("p (h d) -> p h d", d=128), None, "kv")
                            sg = stage[st_i[0] % 2]
                            st_i[0] += 1
                            transpose_to(src, 4, lambda k0, n: sg[:, k0:k0 + n, :], [sg])
                            pb.dma("pool", dstT[:, h0:h0 + 4, tok0:tok0 + 128], sg[:], reads=[sg], writes=[kv_tr])
                        return epi

                    o = c.off
                    plan = [("kraw", o["kc"], kcrawT, None, 0, 0), ("kraw", o["vc"], vcrawT, None, 0, 0),
                            ("knr", o["ks"], ksT, None, 0, 2), ("v", o["vs"], None, vs_s, 0, 0),
                            ("knr", o["kw"], kwT, None, 0, 3), ("v", o["vw"], None, vw_s, 0, 0),
                            ("knr", o["km"], kmT, None, 0, 5), ("knr", o["km"] + 512, kmT, None, 4, 5),
                            ("v", o["vm"], None, vm_s, 0, 0), ("v", o["vm"] + 512, None, vm_s, 4, 0)]
                    for kind, c0, dT, dV, h0, gi in plan:
                        gemm("w_in", c0, 512, hT, 0, KC, TG, "tok", epi_factory(kind, dT, dV, h0, gi))
                    sg_.__exit__(None, None, None)

        def dump(name, src_ap, tracks):
            if name in dbg:
                pb.dma("pool", dbg[name], src_ap, reads=tracks)

        ga_sb = gsb("ga_sb", [128, TG, 48], F32)
        p16col = gsb("p16col", [128, 1], F32)
        pb.op("dve", lambda e: e.tensor_scalar(out=p16col[:], in0=pcol[:], scalar1=16.0, scalar2=None, op0=ALU.mult),
              reads=[pcol], writes=[p16col])
        yT_s = dscr("yT_s", [128, KC, TOK], BF16)

        def alloc_wbufs(sc):
            wbufs.clear()
            for i in range(2):
                wbufs.append(sc.sb(f"wbuf{i}", [128, 16, 512], BF16))

        def gelu_from_psum(ps, W_, out_ap, out_tiles, tmp):
            x2, inner, sg = tmp
            pb.op("act", lambda e: e.activation(out=x2[:, 0:W_], in_=ps[:, 0:W_], func=AF.Square), reads=[ps],
                  writes=[x2])
            pb.op("dve", lambda e: e.tensor_scalar(out=inner[:, 0:W_], in0=x2[:, 0:W_], scalar1=0.044715, scalar2=1.0,
                                                   op0=ALU.mult, op1=ALU.add), reads=[x2], writes=[inner])
            pb.op("dve", lambda e: e.tensor_tensor(out=inner[:, 0:W_], in0=inner[:, 0:W_], in1=ps[:, 0:W_],
                                                   op=ALU.mult), reads=[inner, ps], writes=[inner])
            pb.op("act", lambda e: e.activation(out=sg[:, 0:W_], in_=inner[:, 0:W_], func=AF.Sigmoid,
                                                scale=1.5957691216057308), reads=[inner], writes=[sg])
            pb.op("dve", lambda e: e.tensor_tensor(out=out_ap, in0=sg[:, 0:W_], in1=ps[:, 0:W_], op=ALU.mult),
                  reads=[sg, ps], writes=out_tiles)

        def compress(l):
            CW = min(512, c.NCC * 128)
            with Scope(pb) as sc:
                kr = sc.sb("cp_kr", [128, S], BF16)
                klT = sc.sb("cp_kl", [128, 32, CW], BF16)
                w1 = sc.sb("cp_w1", [128, 32, 512], BF16)
                w2 = sc.sb("cp_w2", [128, 4, 128], BF16)
                g1T = sc.sb("cp_g1", [128, 4, CW], BF16)
                tmp = [sc.sb(f"cp_t{i}", [128, 512], F32) for i in range(3)]
                kst = sc.sb("cp_kst", [128, CW], BF16)
                vst = sc.sb("cp_vst", [128, 4, 128], BF16)
                pb.op("pool", lambda e: e.memset(klT[:], 0.0), writes=[klT])
                for which in ("k", "v"):
                    pe = pek if which == "k" else pev
                    n1, n2 = "phi_w1_" + which, "phi_w2_" + which
                    pb.dma("sp", w1[:], wb[n1].rearrange("(l d) h -> d l h", d=128), reads=[wb_tr[n1]], writes=[w1])
                    pb.dma("sp", w2[:], wb[n2].rearrange("(c h) d -> h c d", h=128), reads=[wb_tr[n2]], writes=[w2])
                    raw = kcrawT if which == "k" else vcrawT
                    for g in range(NG):
                        pb.dma("sp", kr[:], raw[:, g, :], reads=[kv_tr], writes=[kr])
                        for n0 in range(0, c.NCC * 128, CW):
                            nn = min(CW, c.NCMP - n0)
                            if nn < CW:
                                pb.op("pool", lambda e: e.memset(klT[:], 0.0), writes=[klT])
                            for lq in range(32):
                                src = kr[:, 16 * n0 + lq: 16 * n0 + lq + 16 * (nn - 1) + 1: 16]
                                pb.op(("dve", "pool")[lq % 2], lambda e, lq=lq, src=src: e.tensor_scalar(
                                    out=klT[:, lq, 0:nn], in0=src, scalar1=pe[:, lq:lq + 1], scalar2=None,
                                    op0=ALU.add), reads=[kr, pe], writes=[klT])
                            for hc in range(4):
                                ps = psn()
                                pb.op("pe", [lambda e, lq=lq, hc=hc, ps=ps: e.matmul(
                                    ps[:, 0:CW], lhsT=w1[:, lq, hc * 128:(hc + 1) * 128], rhs=klT[:, lq, :],
                                    start=(lq == 0), stop=(lq == 31)) for lq in range(32)], reads=[w1, klT], writes=[ps])
                                gelu_from_psum(ps, CW, g1T[:, hc, :], [g1T], tmp)
                            if which == "k":
                                ps = psn()
                                pb.op("pe", [lambda e, hc=hc, ps=ps: e.matmul(ps[:, 0:CW], lhsT=w2[:, hc, :],
                                                                             rhs=g1T[:, hc, :], start=(hc == 0),
                                                                             stop=(hc == 3)) for hc in range(4)],
                                      reads=[w2, g1T], writes=[ps])
                                x2, inner, _ = tmp
                                pb.op("act", lambda e: e.activation(out=x2[:, 0:CW], in_=ps[:, 0:CW], func=AF.Square),
                                      reads=[ps], writes=[x2])
                                ps2 = psn()
                                pb.op("pe", lambda e: e.matmul(ps2[:, 0:CW], lhsT=onesf[:], rhs=x2[:, 0:CW], start=True,
                                                               stop=True), reads=[onesf, x2], writes=[ps2])
                                rstd_from_ss(inner, ps2, 1.0 / 128)
                                pb.op("dve", lambda e: e.scalar_tensor_tensor(
                                    out=kst[:], in0=ps[:, 0:CW], scalar=kcg_col[:, 0:1], in1=inner[:, 0:CW],
                                    op0=ALU.mult, op1=ALU.mult), reads=[ps, kcg_col, inner], writes=[kst])
                                pb.dma("pool", kcT_s[:, g, n0:n0 + CW], kst[:], reads=[kst], writes=[kv_tr])
                            else:
                                for j in range(CW // 128):
                                    ps = psn()
                                    pb.op("pe", [lambda e, hc=hc, ps=ps, j=j: e.matmul(
                                        ps[:, 0:128], lhsT=g1T[:, hc, j * 128:(j + 1) * 128], rhs=w2[:, hc, :],
                                        start=(hc == 0), stop=(hc == 3)) for hc in range(4)], reads=[w2, g1T],
                                        writes=[ps])
                                    pb.op("act", lambda e, j=j, ps=ps: e.copy(out=vst[:, j, :], in_=ps[:, 0:128]),
                                          reads=[ps], writes=[vst])
                                pb.dma("pool", vc_s[n0:n0 + CW, g, :].rearrange("(j p) d -> p j d", p=128),
                                       vst[:, 0:CW // 128, :], reads=[vst], writes=[kv_tr])

        def kmean(l):
            with Scope(pb) as sc:
                kr = sc.sb("km_kr", [128, S], BF16)
                acc = sc.sb("km_acc", [128, c.NMB], F32)
                st = sc.sb("km_st", [128, MH, c.NMB], BF16)
                for h in range(MH):
                    pb.dma("sp", kr[:], kmT[:, h, :], reads=[kv_tr], writes=[kr])
                    pb.op("dve", lambda e: e.tensor_reduce(out=acc[:], in_=kr[:].rearrange("p (b k) -> p b k", k=256),
                                                           axis=AX.X, op=ALU.add), reads=[kr], writes=[acc])
                    pb.op("dve", lambda e, h=h: e.tensor_scalar(out=st[:, h, :], in0=acc[:], scalar1=1.0 / 256,
                                                                scalar2=None, op0=ALU.mult), reads=[acc], writes=[st])
                pb.dma("pool", kmeanT_s[:, :, :], st[:], reads=[st], writes=[kv_tr])

        def phase_a(l, last, qtiles):
            ntile = len(qtiles)
            src_x = x_in if l == 0 else xs[l]
            src_tr = [] if l == 0 else [xs_tr[l]]
            o = c.off
            with Scope(pb) as sc:
                alloc_common_tmps(sc)
                hT = sc.sb("a_hT", [128, KC, TOK], BF16)
                xt = sc.sb("a_x", [128, D], F32)
                xtmp = sc.sb("a_xtmp", [128, D], F32) if last else None
                gu = sc.sb("a_gu", [128, TG, 1024], BF16)
                gvn = sc.sb("a_gvn", [128, TG, 1024], BF16)
                gv = sc.sb("a_gv", [128, 512], F32)
                ob = sc.sb("a_ob", [128, 1024], BF16)
                tmp = [sc.sb(f"a_t{i}", [128, 512], F32) for i in range(3)]
                stg = [sc.sb(f"a_stg{i}", [128, 8, 128], BF16) for i in range(2)]
                ss4 = sc.sb("a_ss4", [128, 4], F32)
                stg_i = [0]
                css = []
                for t, qi in enumerate(qtiles):
                    if not last:
                        pb.dma("sp", xt[:], src_x[qi * 128:(qi + 1) * 128, :], reads=src_tr, writes=[xt])
                    else:
                        for m in range(NC):
                            ti = NC * qi + m
                            pb.dma("sp", xtmp[:], src_x[ti * 128:(ti + 1) * 128, :], reads=src_tr, writes=[xtmp])
                            if m == 0:
                                pb.op("dve", lambda e: e.tensor_scalar(out=xt[:], in0=xtmp[:], scalar1=onehot[:, 0:1],
                                                                       scalar2=None, op0=ALU.mult),
                                      reads=[xtmp, onehot], writes=[xt])
                            else:
                                pb.op("dve", lambda e, m=m: e.scalar_tensor_tensor(
                                    out=xt[:], in0=xtmp[:], scalar=onehot[:, m:m + 1], in1=xt[:], op0=ALU.mult,
                                    op1=ALU.add), reads=[xtmp, onehot, xt], writes=[xt])
                    pb.dma("pool", xg_s[t], xt[:], reads=[xt], writes=[grp_tr])
                    norm_T(sc, xt, hT, t, f"a{t}")
                    pos_ap = (ownpos_in if last else pos_in)[qi * 128:(qi + 1) * 128, :]
                    css.append(rope_tables(sc, pos_ap, f"a{t}"))
                pb.dma("pool", hT_s[:, :, :], hT[:], reads=[hT], writes=[grp_tr])

                def q_epi(dst_plain, dst_rot, h0, gain_idx):
                    def epi(t, ps):
                        outp = sc_tmp["hn_outp"]
                        outr = sc_tmp["hn_outr"]
                        headnorm_rope(sc, ps, 4, gain_idx, css[t], outr[:].rearrange("p (h d) -> p h d", d=128),
                                      outp[:].rearrange("p (h d) -> p h d", d=128) if dst_plain is not None else None,
                                      "a")
                        for srct, dst in ((outp, dst_plain), (outr, dst_rot)):
                            if dst is None:
                                continue
                            sg = stg[stg_i[0] % 2]
                            stg_i[0] += 1
                            transpose_to(srct, 4, lambda k0, n, sg=sg: sg[:, k0:k0 + n, :], [sg])
                            pb.dma("pool", dst[:, h0:h0 + 4, t * 128:(t + 1) * 128], sg[:, 0:4, :], reads=[sg],
                                   writes=[grp_tr])
                    return epi

                for cq in range(4):
                    gemm("w_in", o["qa"] + cq * 512, 512, hT, 0, KC, ntile, "tok", q_epi(qnT_s, qrT_s, cq * 4, 0))
                for cq in range(2):
                    gemm("w_in", o["qm"] + cq * 512, 512, hT, 0, KC, ntile, "tok", q_epi(None, qmT_s, cq * 4, 4))

                def ga_epi(t, ps):
                    pb.op("dve", lambda e: e.tensor_tensor(out=ga_sb[:, t, :], in0=ps[:, 0:48], in1=gbias[:],
                                                           op=ALU.add), reads=[ps, gbias], writes=[ga_sb])
                    pb.op("act", lambda e: e.activation(out=ga_sb[:, t, :], in_=ga_sb[:, t, :], func=AF.Sigmoid),
                          reads=[ga_sb], writes=[ga_sb])
                gemm("w_in", o["ga"], 48, hT, 0, KC, ntile, "tok", ga_epi)

                def u_epi(cu):
                    def epi(t, ps):
                        gelu_from_psum(ps, 512, gu[:, t, cu * 512:(cu + 1) * 512], [gu], tmp)
                    return epi

                def v_epi(cv):
                    def epi(t, ps):
                        gelu_from_psum(ps, 512, gv[:], [gv], tmp)
                        x2 = tmp[0]
                        pb.op("act", lambda e: e.activation(out=x2[:], in_=gv[:], func=AF.Square), reads=[gv],
                              writes=[x2])
                        pb.op("dve", lambda e: e.tensor_reduce(out=ss4[:], in_=x2[:].rearrange("p (h d) -> p h d", d=128),
                                                               axis=AX.X, op=ALU.add), reads=[x2], writes=[ss4])
                        rstd_from_ss(ss4, ss4, 1.0 / 128)
                        gv3 = gv[:].rearrange("p (h d) -> p h d", d=128)
                        pb.op("dve", lambda e: e.tensor_tensor(out=gv3, in0=gv3,
                                                               in1=bc(ss4[:, 0:4].unsqueeze(2), [128, 4, 128]),
                                                               op=ALU.mult), reads=[gv, ss4], writes=[gv])
                        pb.op("pool", lambda e: e.tensor_tensor(out=gvn[:, t, cv * 512:(cv + 1) * 512], in0=gv[:],
                                                                in1=sgng[:, cv * 512:(cv + 1) * 512], op=ALU.mult),
                              reads=[gv, sgng], writes=[gvn])
                    return epi

                for cu in range(2):
                    gemm("w_in", o["ub"] + cu * 512, 512, hT, 0, KC, ntile, "tok", u_epi(cu))
                for cv in range(2):
                    gemm("w_in", o["vb"] + cv * 512, 512, hT, 0, KC, ntile, "tok", v_epi(cv))
                for t in range(ntile):
                    for half in range(2):
                        ps = psn()
                        pb.op("pe", [lambda e, g=g, ps=ps: e.matmul(
                            ps[:, (g % 4) * 128:(g % 4 + 1) * 128], lhsT=sgwT[:, g, :],
                            rhs=gvn[:, t, g * 128:(g + 1) * 128], start=True, stop=True)
                            for g in range(half * 4, half * 4 + 4)], reads=[sgwT, gvn], writes=[ps])
                        for g in range(half * 4, half * 4 + 4):
                            pb.op("dve", lambda e, g=g, ps=ps: e.scalar_tensor_tensor(
                                out=ob[:, g * 128:(g + 1) * 128], in0=ps[:, (g % 4) * 128:(g % 4 + 1) * 128],
                                scalar=sgb[:, g:g + 1], in1=gu[:, t, g * 128:(g + 1) * 128], op0=ALU.add,
                                op1=ALU.mult), reads=[ps, sgb, gu], writes=[ob])
                    sg = stg[stg_i[0] % 2]
                    stg_i[0] += 1
                    transpose_to(ob, 8, lambda k0, n, sg=sg: sg[:, k0:k0 + n, :], [sg])
                    pb.dma("pool", oT_s[:, 16:24, t * 128:(t + 1) * 128], sg[:], reads=[sg], writes=[grp_tr])
        def phase_b(l, last, qtiles):
            NS = c.NSLC
            NMB = c.NMB
            NCC = c.NCC
            SB = [PS[0], PS[1]]
            MB = PS[2]
            OB = [PS[3], PS[4], PS[5], PS[6]]
            XB = PS[7]
            with Scope(pb) as sc:
                exps = sc.sb("b_exps", [128, 64, 128], BF16)
                expm = sc.sb("b_expm", [64, 64, 128], BF16)
                ov = sc.sb("b_ov", [128, NCC, 256], BF16)
                kcT = sc.sb("b_kcT", [128, NG, NCC * 128], BF16)
                vcP = sc.sb("b_vcP", [128, NCC, NG, 129], BF16)
                kmeanT = sc.sb("b_kmean", [128, MH, NMB], BF16)
                jrow = sc.sb("b_jrow", [128, 256], F32)
                pb.dma("sp", exps[:].rearrange("p a b -> p (a b)"), c_exps[:, :], writes=[exps])
                pb.dma("sp", expm[:].rearrange("p a b -> p (a b)"), c_expm[:, :], writes=[expm])
                pb.dma("sp", ov[:], c_ov.rearrange("(c p) j -> p c j", p=128), writes=[ov])
                pb.dma("sp", kcT[:], kcT_s[:, :, :], reads=[kv_tr], writes=[kcT])
                pb.op("dve", lambda e: e.memset(vcP[:], 1.0), writes=[vcP])
                for cc in range(NCC):
                    pb.dma("sp", vcP[:, cc, :, 0:128], vc_s[cc * 128:(cc + 1) * 128, :, :], reads=[kv_tr], writes=[vcP])
                pb.dma("sp", kmeanT[:], kmeanT_s[:, :, :], reads=[kv_tr], writes=[kmeanT])
                pb.dma("sp", jrow[:], c_jrow[0:1, :].partition_broadcast(128), writes=[jrow])
                qn = sc.sb("b_qn", [128, NH * 128], BF16)
                qr = sc.sb("b_qr", [128, NH * 128], BF16)
                qm = sc.sb("b_qm", [128, MH, 128], BF16)
                info = sc.sb("b_info", [128, 4], F32)
                trow = sc.sb("b_trow", [128, 128], F32)
                kbuf = [sc.sb(f"b_k{i}", [128, 4, 512], BF16) for i in range(2)]
                vbuf = [sc.sb(f"b_v{i}", [128, 4, 4, 129], BF16) for i in range(2)]
                for vb_ in vbuf:
                    pb.op("dve", lambda e, vb_=vb_: e.memset(vb_[:], 1.0), writes=[vb_])
                e_sb = [sc.sb(f"b_e{i}", [128, 512], F32) for i in range(2)]
                em = [sc.sb(f"b_em{i}", [128, 512], BF16) for i in range(2)]
                cmd = sc.sb("b_cmd", [128, 8, 128], F32)
                msk = sc.sb("b_msk", [128, 512], F32)
                wm = sc.sb("b_wm", [128, 128], F32)
                wm2 = sc.sb("b_wm2", [128, 128], F32)
                ecmp = sc.sb("b_ecmp", [128, NCC, 512], BF16)
                imp = sc.sb("b_imp", [128, 256], F32)
                dd = sc.sb("b_dd", [128, 256], F32)
                ff = sc.sb("b_ff", [128, 256], F32)
                al = sc.sb("b_al", [128, 256], F32)
                pen = sc.sb("b_pen", [128, 256], F32)
                score = sc.sb("b_score", [128, 256], F32)
                sc2 = sc.sb("b_sc2", [128, 256], F32)
                m8a = sc.sb("b_m8a", [128, 8], F32)
                m8b = sc.sb("b_m8b", [128, 8], F32)
                sel = sc.sb("b_sel", [128, 256], BF16)
                selT = sc.sb("b_selT", [128, 2, 128], BF16)
                pastm = sc.sb("b_pastm", [128, NMB], F32)
                ownm = sc.sb("b_ownm", [128, NMB], F32)
                penm = sc.sb("b_penm", [128, NMB], F32)
                msc = sc.sb("b_msc", [128, MH, NMB], F32)
                m8m = sc.sb("b_m8m", [128, MH, 8], F32)
                selm = sc.sb("b_selm", [128, MH, NMB], F32)
                selmb = sc.sb("b_selmb", [128, MH, 64], BF16)
                selTm = sc.sb("b_selTm", [64, MH, 128], BF16)
                oa = sc.sb("b_oa", [128, NH * 128], F32)
                oab = sc.sb("b_oab", [128, NH * 128], BF16)
                om = sc.sb("b_om", [128, MH * 128], BF16)
                sm = sc.sb("b_sm", [128, 4], F32)
                ostg = sc.sb("b_ostg", [128, 8, 128], BF16)
                pb.op("pool", lambda e: e.memset(sel[:], 0.0), writes=[sel])
                pb.op("pool", lambda e: e.memset(selmb[:], 0.0), writes=[selmb])
                ctr = [0]

                def finish_head(O, ga_col, dst_ap, dst_tile, first):
                    pb.op("dve", lambda e: e.tensor_scalar(out=sm[:, 0:1], in0=O[:, 128:129], scalar1=1e-30,
                                                           scalar2=None, op0=ALU.max), reads=[O], writes=[sm])
                    pb.op("dve", lambda e: e.reciprocal(out=sm[:, 1:2], in_=sm[:, 0:1]), reads=[sm], writes=[sm])
                    if ga_col is not None:
                        pb.op("dve", lambda e: e.tensor_tensor(out=sm[:, 2:3], in0=sm[:, 1:2], in1=ga_col, op=ALU.mult),
                              reads=[sm, ga_sb], writes=[sm])
                        coef = sm[:, 2:3]
                    else:
                        coef = sm[:, 1:2]
                    if first:
                        pb.op("dve", lambda e: e.tensor_scalar(out=dst_ap, in0=O[:, 0:128], scalar1=coef, scalar2=None,
                                                               op0=ALU.mult), reads=[O, sm], writes=[dst_tile])
                    else:
                        pb.op("dve", lambda e: e.scalar_tensor_tensor(out=dst_ap, in0=O[:, 0:128], scalar=coef,
                                                                      in1=dst_ap, op0=ALU.mult, op1=ALU.add),
                              reads=[O, sm, dst_tile], writes=[dst_tile])

                for t, qi in enumerate(qtiles):
                    if not last:
                        imax = imin = qi
                        ndiag = 1
                    else:
                        imax = NC * qi + NC - 1
                        imin = NC * qi
                        ndiag = NC
                    nk = imax + 1
                    w0 = max(0, imin - 4)
                    ncc = min(NCC, (8 * imax + 7 + 127) // 128)
                    inf_src = info_own if last else info_all
                    row_src = row_own if last else row_all
                    pb.dma("sp", qn[:].rearrange("p (h q) -> p h q", q=128), qnT_s[:, :, t * 128:(t + 1) * 128],
                           reads=[grp_tr], writes=[qn])
                    pb.dma("sp", qr[:].rearrange("p (h q) -> p h q", q=128), qrT_s[:, :, t * 128:(t + 1) * 128],
                           reads=[grp_tr], writes=[qr])
                    pb.dma("sp", qm[:], qmT_s[:, :, t * 128:(t + 1) * 128], reads=[grp_tr], writes=[qm])
                    pb.dma("pool", info[:], inf_src[qi * 128:(qi + 1) * 128, :], writes=[info])
                    pb.dma("pool", trow[:], row_src[qi:qi + 1, :].partition_broadcast(128), writes=[trow])
                    for i in range(ndiag):
                        kt = nk - ndiag + i
                        pb.op("pool", lambda e, i=i, kt=kt: e.tensor_scalar(
                            out=cmd[:, i, :], in0=trow[:], scalar1=float(-kt * 128), scalar2=pcol[:, 0:1], op0=ALU.add,
                            op1=ALU.is_ge), reads=[trow, pcol], writes=[cmd])

                    for g in range(NG):
                        qng = qn[:, g * 512:(g + 1) * 512]
                        qrg = qr[:, g * 512:(g + 1) * 512]
                        for cc in range(ncc):
                            Sb = SB[ctr[0] % 2]
                            eb = e_sb[ctr[0] % 2]
                            ctr[0] += 1
                            pb.op("pe", lambda e, cc=cc, Sb=Sb: e.matmul(Sb[:, 0:512], lhsT=kcT[:, g, cc * 128:(cc + 1) * 128],
                                                                       rhs=qng, start=True, stop=True),
                                  reads=[kcT, qn], writes=[Sb])
                            pb.op("act", lambda e, Sb=Sb, eb=eb: e.activation(out=eb[:], in_=Sb[:], func=AF.Exp),
                                  reads=[Sb], writes=[eb])
                            pb.op("pool", lambda e, cc=cc: e.tensor_scalar(
                                out=wm[:], in0=trow[:], scalar1=float(-(31 + 2048 * cc)), scalar2=p16col[:, 0:1],
                                op0=ALU.add, op1=ALU.is_ge), reads=[trow, p16col], writes=[wm])
                            pb.op("dve", lambda e, cc=cc, eb=eb: e.tensor_tensor(
                                out=ecmp[:, cc, :].rearrange("p (r q) -> p r q", q=128),
                                in0=eb[:].rearrange("p (r q) -> p r q", q=128),
                                in1=bc(wm[:].unsqueeze(1), [128, 4, 128]), op=ALU.mult), reads=[eb, wm], writes=[ecmp])
                        for r in range(4):
                            head = g * 4 + r
                            A = OB[r]
                            Bk = MB if r % 2 == 0 else XB
                            pb.op("pe", [lambda e, cc=cc, A=A: e.matmul(
                                A[:, 0:129], lhsT=ecmp[:, cc, r * 128:(r + 1) * 128], rhs=vcP[:, cc, g, :],
                                start=(cc == 0), stop=(cc == ncc - 1)) for cc in range(ncc)], reads=[ecmp, vcP],
                                writes=[A])
                            pb.op("pe", [lambda e, cc=cc, Bk=Bk: e.matmul(
                                Bk[:, 0:256], lhsT=ecmp[:, cc, r * 128:(r + 1) * 128], rhs=ov[:, cc, :],
                                start=(cc == 0), stop=(cc == ncc - 1)) for cc in range(ncc)], reads=[ecmp, ov],
                                writes=[Bk])
                            finish_head(A, ga_sb[:, t, head:head + 1], oa[:, head * 128:(head + 1) * 128], oa, True)
                            if r == 0:
                                pb.op("dve", lambda e, Bk=Bk: e.tensor_scalar(out=imp[:], in0=Bk[:, 0:256],
                                                                            scalar1=sm[:, 1:2], scalar2=None,
                                                                            op0=ALU.mult), reads=[Bk, sm], writes=[imp])
                            else:
                                pb.op("dve", lambda e, Bk=Bk: e.scalar_tensor_tensor(
                                    out=imp[:], in0=Bk[:, 0:256], scalar=sm[:, 1:2], in1=imp[:], op0=ALU.mult,
                                    op1=ALU.add), reads=[Bk, sm, imp], writes=[imp])
                        pb.op("dve", lambda e: e.tensor_scalar(out=dd[:, 0:NS], in0=jrow[:, 0:NS], scalar1=info[:, 1:2],
                                                               scalar2=None, op0=ALU.subtract), reads=[jrow, info],
                              writes=[dd])
                        pb.op("dve", lambda e: e.tensor_scalar(out=ff[:, 0:NS], in0=dd[:, 0:NS], scalar1=-1.0,
                                                               scalar2=None, op0=ALU.is_ge), reads=[dd], writes=[ff])
                        pb.op("dve", lambda e: e.scalar_tensor_tensor(out=ff[:, 0:NS], in0=dd[:, 0:NS], scalar=0.0,
                                                                      in1=ff[:, 0:NS], op0=ALU.is_le, op1=ALU.mult),
                              reads=[dd, ff], writes=[ff])
                        pb.op("dve", lambda e: e.memset(ff[:, 0:1], 1.0), reads=[ff], writes=[ff])
                        pb.op("dve", lambda e: e.tensor_single_scalar(out=al[:, 0:NS], in_=dd[:, 0:NS], scalar=0.0,
                                                                      op=ALU.is_le), reads=[dd], writes=[al])
                        pb.op("dve", lambda e: e.scalar_tensor_tensor(out=score[:, 0:NS], in0=ff[:, 0:NS], scalar=1e9,
                                                                      in1=imp[:, 0:NS], op0=ALU.mult, op1=ALU.add),
                              reads=[ff, imp], writes=[score])
                        pb.op("dve", lambda e: e.tensor_scalar(out=pen[:, 0:NS], in0=al[:, 0:NS], scalar1=1.0,
                                                               scalar2=1e30, op0=ALU.subtract, op1=ALU.mult),
                              reads=[al], writes=[pen])
                        pb.op("dve", lambda e: e.tensor_tensor(out=score[:, 0:NS], in0=score[:, 0:NS], in1=al[:, 0:NS],
                                                               op=ALU.mult), reads=[score, al], writes=[score])
                        pb.op("dve", lambda e: e.tensor_tensor(out=score[:, 0:NS], in0=score[:, 0:NS], in1=pen[:, 0:NS],
                                                               op=ALU.add), reads=[score, pen], writes=[score])
                        pb.op("dve", lambda e: e.max(out=m8a[:], in_=score[:, 0:NS]), reads=[score], writes=[m8a])
                        pb.op("dve", lambda e: e.match_replace(out=sc2[:, 0:NS], in_to_replace=m8a[:],
                                                               in_values=score[:, 0:NS], imm_value=-3.0e38),
                              reads=[score, m8a], writes=[sc2])
                        pb.op("dve", lambda e: e.max(out=m8b[:], in_=sc2[:, 0:NS]), reads=[sc2], writes=[m8b])
                        pb.op("dve", lambda e: e.scalar_tensor_tensor(out=sel[:, 0:NS], in0=score[:, 0:NS],
                                                                      scalar=m8b[:, 7:8], in1=al[:, 0:NS],
                                                                      op0=ALU.is_ge, op1=ALU.mult),
                              reads=[score, m8b, al], writes=[sel])
                        transpose_to(sel, 2, lambda k0, n: selT[:, k0:k0 + n, :], [selT], bank=XB)
                        for br in ("slc", "win"):
                            kT_src, v_src = (ksT, vs_s) if br == "slc" else (kwT, vw_s)
                            k_lo = 0 if br == "slc" else w0
                            for ktb in range(k_lo, nk, 4):
                                nkb = min(4, nk - ktb)
                                kb = kbuf[ctr[0] % 2]
                                vb = vbuf[ctr[0] % 2]
                                pb.dma("sp", kb[:, 0, 0:nkb * 128], kT_src[:, g, ktb * 128:(ktb + nkb) * 128],
                                       reads=[kv_tr], writes=[kb])
                                pb.dma("sp", vb[:, 0:nkb, 0, 0:128],
                                       v_src[ktb * 128:(ktb + nkb) * 128, g, :].rearrange("(k p) d -> p k d", p=128),
                                       reads=[kv_tr], writes=[vb])
                                for kk in range(nkb):
                                    kt = ktb + kk
                                    Sb = SB[ctr[0] % 2]
                                    eb = e_sb[ctr[0] % 2]
                                    emb = em[ctr[0] % 2]
                                    ctr[0] += 1
                                    pb.op("pe", lambda e, kk=kk, Sb=Sb, kb=kb: e.matmul(
                                        Sb[:, 0:512], lhsT=kb[:, 0, kk * 128:(kk + 1) * 128], rhs=qrg, start=True,
                                        stop=True), reads=[kb, qr], writes=[Sb])
                                    pb.op("act", lambda e, Sb=Sb, eb=eb: e.activation(out=eb[:], in_=Sb[:], func=AF.Exp),
                                          reads=[Sb], writes=[eb])
                                    if br == "slc":
                                        pb.op("pe", lambda e, kt=kt: e.matmul(MB[:, 0:128], lhsT=exps[:, kt % 64, :],
                                                                              rhs=selT[:, kt // 64, :], start=True,
                                                                              stop=True), reads=[exps, selT], writes=[MB])
                                        if kt >= nk - ndiag:
                                            di = kt - (nk - ndiag)
                                            pb.op("dve", lambda e, di=di: e.tensor_tensor(
                                                out=wm[:], in0=cmd[:, di, :], in1=MB[:, 0:128], op=ALU.mult),
                                                reads=[cmd, MB], writes=[wm])
                                            mask_ap, mask_t = wm[:], wm
                                        else:
                                            mask_ap, mask_t = MB[:, 0:128], MB
                                    else:
                                        pb.op("pool", lambda e, kt=kt: e.tensor_scalar(
                                            out=wm[:], in0=trow[:], scalar1=float(-kt * 128), scalar2=pcol[:, 0:1],
                                            op0=ALU.add, op1=ALU.subtract), reads=[trow, pcol], writes=[wm])
                                        pb.op("pool", lambda e: e.tensor_scalar(
                                            out=wm2[:], in0=wm[:], scalar1=0.0, scalar2=None, op0=ALU.is_ge),
                                            reads=[wm], writes=[wm2])
                                        pb.op("pool", lambda e: e.tensor_scalar(
                                            out=wm[:], in0=wm[:], scalar1=511.0, scalar2=None, op0=ALU.is_le),
                                            reads=[wm], writes=[wm])
                                        pb.op("pool", lambda e: e.tensor_tensor(
                                            out=wm[:], in0=wm[:], in1=wm2[:], op=ALU.mult), reads=[wm, wm2],
                                            writes=[wm])
                                        mask_ap, mask_t = wm[:], wm
                                    pb.op("dve", lambda e, eb=eb, emb=emb, mask_ap=mask_ap: e.tensor_tensor(
                                        out=emb[:].rearrange("p (r q) -> p r q", q=128),
                                        in0=eb[:].rearrange("p (r q) -> p r q", q=128),
                                        in1=bc(mask_ap.unsqueeze(1), [128, 4, 128]), op=ALU.mult),
                                        reads=[eb, mask_t], writes=[emb])
                                    for r in range(4):
                                        pb.op("pe", lambda e, r=r, emb=emb, vb=vb, kk=kk, kt=kt: e.matmul(
                                            OB[r][:, 0:129], lhsT=emb[:, r * 128:(r + 1) * 128], rhs=vb[:, kk, 0, :],
                                            start=(kt == k_lo), stop=(kt == nk - 1)), reads=[emb, vb], writes=[OB[r]])
                            gofs = 16 if br == "slc" else 32
                            for r in range(4):
                                head = g * 4 + r
                                finish_head(OB[r], ga_sb[:, t, gofs + head:gofs + head + 1],
                                            oa[:, head * 128:(head + 1) * 128], oa, False)

                    pb.op("pe", [lambda e, h=h: e.matmul(XB[:, h * NMB:(h + 1) * NMB], lhsT=qm[:, h, :],
                                                         rhs=kmeanT[:, h, :], start=True, stop=True)
                                 for h in range(MH)], reads=[qm, kmeanT], writes=[XB])
                    pb.op("dve", lambda e: e.tensor_scalar(out=pastm[:], in0=jrow[:, 0:NMB], scalar1=info[:, 2:3],
                                                           scalar2=None, op0=ALU.is_lt), reads=[jrow, info],
                          writes=[pastm])
                    pb.op("dve", lambda e: e.tensor_scalar(out=ownm[:], in0=jrow[:, 0:NMB], scalar1=info[:, 2:3],
                                                           scalar2=None, op0=ALU.is_equal), reads=[jrow, info],
                          writes=[ownm])
                    pb.op("dve", lambda e: e.tensor_scalar(out=penm[:], in0=pastm[:], scalar1=1.0, scalar2=1e30,
                                                           op0=ALU.subtract, op1=ALU.mult), reads=[pastm], writes=[penm])
                    pb.op("dve", lambda e: e.tensor_tensor(out=msc[:], in0=XB[:, 0:MH * NMB].rearrange(
                        "p (h b) -> p h b", b=NMB), in1=bc(pastm[:].unsqueeze(1), [128, MH, NMB]), op=ALU.mult),
                        reads=[XB, pastm], writes=[msc])
                    pb.op("dve", lambda e: e.tensor_tensor(out=msc[:], in0=msc[:],
                                                           in1=bc(penm[:].unsqueeze(1), [128, MH, NMB]), op=ALU.add),
                          reads=[msc, penm], writes=[msc])
                    for h in range(MH):
                        pb.op("dve", lambda e, h=h: e.max(out=m8m[:, h, :], in_=msc[:, h, :]), reads=[msc],
                              writes=[m8m])
                    for h in range(MH):
                        pb.op("dve", lambda e, h=h: e.scalar_tensor_tensor(
                            out=selm[:, h, :], in0=msc[:, h, :], scalar=m8m[:, h, 2:3], in1=pastm[:], op0=ALU.is_ge,
                            op1=ALU.mult), reads=[msc, m8m, pastm], writes=[selm])
                    pb.op("dve", lambda e: e.tensor_tensor(out=selmb[:, :, 0:NMB], in0=selm[:],
                                                           in1=bc(ownm[:].unsqueeze(1), [128, MH, NMB]), op=ALU.add),
                          reads=[selm, ownm], writes=[selmb])
                    xbb = XB[:].bitcast(BF16)
                    pb.op("pe", [lambda e, h=h: e.transpose(out=xbb[0:64, h * 128:(h + 1) * 128], in_=selmb[:, h, :],
                                                            identity=identb[:]) for h in range(MH)],
                          reads=[selmb, identb], writes=[XB])
                    pb.op("act", lambda e: e.copy(out=selTm[:].rearrange("p h q -> p (h q)"), in_=xbb[0:64, 0:MH * 128]),
                          reads=[XB], writes=[selTm])
                    for hb in range(MH // 4):
                        for ktb in range(0, nk, 4):
                            nkb = min(4, nk - ktb)
                            kb = kbuf[ctr[0] % 2]
                            vb = vbuf[ctr[0] % 2]
                            pb.dma("sp", kb[:, :, 0:nkb * 128], kmT[:, hb * 4:(hb + 1) * 4, ktb * 128:(ktb + nkb) * 128],
                                   reads=[kv_tr], writes=[kb])
                            for kk in range(nkb):
                                pb.dma("sp", vb[:, kk, :, 0:128],
                                       vm_s[(ktb + kk) * 128:(ktb + kk + 1) * 128, hb * 4:(hb + 1) * 4, :],
                                       reads=[kv_tr], writes=[vb])
                            for kk in range(nkb):
                                kt = ktb + kk
                                Sb = SB[ctr[0] % 2]
                                eb = e_sb[ctr[0] % 2]
                                emb = em[ctr[0] % 2]
                                ctr[0] += 1
                                pb.op("pe", [lambda e, h=h, kk=kk, Sb=Sb, kb=kb: e.matmul(
                                    Sb[:, h * 128:(h + 1) * 128], lhsT=kb[:, h, kk * 128:(kk + 1) * 128],
                                    rhs=qm[:, hb * 4 + h, :], start=True, stop=True) for h in range(4)],
                                    reads=[kb, qm], writes=[Sb])
                                pb.op("pe", [lambda e, h=h, kt=kt: e.matmul(
                                    MB[:, h * 128:(h + 1) * 128], lhsT=expm[:, kt // 2, :], rhs=selTm[:, hb * 4 + h, :],
                                    start=True, stop=True) for h in range(4)], reads=[expm, selTm], writes=[MB])
                                pb.op("act", lambda e, Sb=Sb, eb=eb: e.activation(out=eb[:], in_=Sb[:], func=AF.Exp),
                                      reads=[Sb], writes=[eb])
                                if kt >= nk - ndiag:
                                    di = kt - (nk - ndiag)
                                    pb.op("dve", lambda e, di=di: e.tensor_tensor(
                                        out=msk[:].rearrange("p (r q) -> p r q", q=128),
                                        in0=MB[:].rearrange("p (r q) -> p r q", q=128),
                                        in1=bc(cmd[:, di:di + 1, :], [128, 4, 128]), op=ALU.mult), reads=[MB, cmd],
                                        writes=[msk])
                                    mask_ap, mask_t = msk[:], msk
                                else:
                                    mask_ap, mask_t = MB[:], MB
                                pb.op("dve", lambda e, eb=eb, emb=emb, mask_ap=mask_ap: e.tensor_tensor(
                                    out=emb[:], in0=eb[:], in1=mask_ap, op=ALU.mult), reads=[eb, mask_t], writes=[emb])
                                for h in range(4):
                                    pb.op("pe", lambda e, h=h, emb=emb, vb=vb, kk=kk, kt=kt: e.matmul(
                                        OB[h][:, 0:129], lhsT=emb[:, h * 128:(h + 1) * 128], rhs=vb[:, kk, h, :],
                                        start=(kt == 0), stop=(kt == nk - 1)), reads=[emb, vb], writes=[OB[h]])
                        for h in range(4):
                            hh = hb * 4 + h
                            finish_head(OB[h], None, om[:, hh * 128:(hh + 1) * 128], om, True)
                    pb.op("act", lambda e: e.copy(out=oab[:], in_=oa[:]), reads=[oa], writes=[oab])
                    for half in range(2):
                        transpose_to(oab, 8, lambda k0, n: ostg[:, k0:k0 + n, :], [ostg],
                                     src_fn=lambda k, half=half: oab[:, (half * 8 + k) * 128:(half * 8 + k + 1) * 128],
                                     bank=XB)
                        pb.dma("pool", oT_s[:, half * 8:(half + 1) * 8, t * 128:(t + 1) * 128], ostg[:], reads=[ostg],
                               writes=[grp_tr])
                    transpose_to(om, 8, lambda k0, n: ostg[:, k0:k0 + n, :], [ostg], bank=XB)
                    pb.dma("pool", oT_s[:, 24:32, t * 128:(t + 1) * 128], ostg[:], reads=[ostg], writes=[grp_tr])

        def phase_c1(l, ntile):
            o = c.off
            NTK = ntile * 128
            with Scope(pb) as sc:
                alloc_wbufs(sc)
                hT = sc.sb("c_hT", [128, KC, TOK], BF16)
                oT = sc.sb("c_oT", [128, KC, TOK], BF16)
                sig = [sc.sb(f"c_sig{j}", [128, TOK], F32) for j in range(4)]
                yacc = [sc.sb(f"c_yacc{j}", [128, TOK], F32) for j in range(4)]
                tmp = sc.sb("c_tmp", [128, TOK], F32)
                ystg = sc.sb("c_ystg", [128, 4, TOK], BF16)
                pb.dma("sp", hT[:], hT_s[:, :, :], reads=[grp_tr], writes=[hT])
                pb.dma("sp", oT[:], oT_s[:, :, :], reads=[grp_tr], writes=[oT])
                for cc4 in range(D // 512):
                    col0 = cc4 * 512
                    for bi, (gname, pname, kofs, nkp) in enumerate((("gma", "proj_a", 0, 16), ("gmb", "proj_b", 16, 8),
                                                                    ("gmc", "proj_c", 24, 8))):
                        def epi_gate(j, ps):
                            pb.op("act", lambda e: e.activation(out=sig[j][:, 0:NTK], in_=ps[:, 0:NTK], func=AF.Sigmoid),
                                  reads=[ps], writes=[sig[j]])

                        def epi_proj(j, ps, bi=bi):
                            if bi == 0:
                                pb.op("dve", lambda e: e.tensor_tensor(out=yacc[j][:, 0:NTK], in0=sig[j][:, 0:NTK],
                                                                       in1=ps[:, 0:NTK], op=ALU.mult),
                                      reads=[sig[j], ps], writes=[yacc[j]])
                                return
                            pb.op("dve", lambda e: e.tensor_tensor(out=tmp[:, 0:NTK], in0=sig[j][:, 0:NTK],
                                                                   in1=ps[:, 0:NTK], op=ALU.mult), reads=[sig[j], ps],
                                  writes=[tmp])
                            if bi == 1:
                                pb.op("pool", lambda e: e.tensor_tensor(out=yacc[j][:, 0:NTK], in0=yacc[j][:, 0:NTK],
                                                                        in1=tmp[:, 0:NTK], op=ALU.add),
                                      reads=[yacc[j], tmp], writes=[yacc[j]])
                            else:
                                pb.op("pool", lambda e: e.tensor_tensor(out=ystg[:, j, 0:NTK], in0=yacc[j][:, 0:NTK],
                                                                        in1=tmp[:, 0:NTK], op=ALU.add),
                                      reads=[yacc[j], tmp], writes=[ystg])
                        gemm("w_in", o[gname] + col0, 512, hT, 0, KC, ntile, "feat", epi_gate)
                        gemm(pname, col0, 512, oT, kofs, nkp, ntile, "feat", epi_proj)
                    pb.dma("pool", yT_s[:, cc4 * 4:(cc4 + 1) * 4, :], ystg[:], reads=[ystg], writes=[grp_tr])

        def phase_c2d(l, last, qtiles):
            ntile = len(qtiles)
            NTK = ntile * 128
            with Scope(pb) as scx:
                xacc = scx.sb("x_acc", [128, TG, D], F32)
                for t in range(ntile):
                    pb.dma("sp", xacc[:, t, :], xg_s[t], reads=[grp_tr], writes=[xacc])
                with Scope(pb) as sc:
                    alloc_wbufs(sc)
                    yT = sc.sb("c2_yT", [128, KC, TOK], BF16)
                    pb.dma("sp", yT[:], yT_s[:, :, :], reads=[grp_tr], writes=[yT])
                    for oc in range(D // 512):
                        def epi(t, ps, oc=oc):
                            pb.op("dve", lambda e: e.tensor_tensor(out=xacc[:, t, oc * 512:(oc + 1) * 512],
                                                                   in0=xacc[:, t, oc * 512:(oc + 1) * 512], in1=ps[:],
                                                                   op=ALU.add), reads=[xacc, ps], writes=[xacc])
                        gemm("w_out", oc * 512, 512, yT, 0, KC, ntile, "tok", epi)
                h2T = scx.sb("d_h2T", [128, KC, TOK], BF16)
                with Scope(pb) as sc:
                    sc_tmp["xb"] = sc.sb("t_xb", [128, D], BF16)
                    sc_tmp["sq"] = sc_tmp["xb"]
                    for t in range(ntile):
                        norm_T(sc, xacc, h2T, t, f"d{t}", xap=xacc[:, t, :])
                with Scope(pb) as sc:
                    alloc_wbufs(sc)
                    hid = sc.sb("d_hid", [128, 16, TOK], BF16)
                    rl = [sc.sb(f"d_rl{i}", [128, TOK], F32) for i in range(2)]
                    rli = [0]
                    for part in range(DFF // 2048):
                        for c4 in range(4):
                            def epi1(j, ps, c4=c4):
                                r_ = rl[rli[0] % 2]
                                rli[0] += 1
                                pb.op("act", lambda e: e.activation(out=r_[:, 0:NTK], in_=ps[:, 0:NTK], func=AF.Relu),
                                      reads=[ps], writes=[r_])
                                pb.op("pool", lambda e: e.tensor_tensor(out=hid[:, c4 * 4 + j, 0:NTK], in0=r_[:, 0:NTK],
                                                                        in1=r_[:, 0:NTK], op=ALU.mult), reads=[r_],
                                      writes=[hid])
                            gemm("mlp_w1", part * 2048 + c4 * 512, 512, h2T, 0, KC, ntile, "feat", epi1)
                        for oc in range(D // 512):
                            def epi2(t, ps, oc=oc):
                                pb.op("dve", lambda e: e.tensor_tensor(out=xacc[:, t, oc * 512:(oc + 1) * 512],
                                                                       in0=xacc[:, t, oc * 512:(oc + 1) * 512],
                                                                       in1=ps[:], op=ALU.add), reads=[xacc, ps],
                                      writes=[xacc])
                            gemm("mlp_w2", oc * 512, 512, hid, 0, 16, ntile, "tok", epi2, wrow0=part * 16)
                for t, qi in enumerate(qtiles):
                    if last:
                        pb.dma("pool", y_out[qi * 128:(qi + 1) * 128, :], xacc[:, t, :], reads=[xacc])
                    else:
                        pb.dma("pool", xs[l + 1][qi * 128:(qi + 1) * 128, :], xacc[:, t, :], reads=[xacc],
                               writes=[xs_tr[l + 1]])
                if "xmid" in dbg and False:
                    pass

        for l in range(L):
            last = (l == L - 1)
            with Scope(pb) as sc:
                precast(sc, l)
            with Scope(pb) as sc:
                load_layer_params(sc, l)
            kv_pass(l)
            compress(l)
            kmean(l)
            ntl = NTO if last else NT
            assert ntl % TG == 0
            for g0 in range(0, ntl, TG):
                qtiles = list(range(g0, g0 + TG))
                phase_a(l, last, qtiles)
                phase_b(l, last, qtiles)
                phase_c1(l, TG)
                phase_c2d(l, last, qtiles)
        pb.barrier(engines=("sp",))
        print("instructions:", pb.nins)
    return nc


def host_constants(cfg):
    c = cfg
    bf = ml_dtypes.bfloat16
    ident = np.eye(128, dtype=np.float32)
    tril = np.tril(np.ones((128, 128), np.float32))
    n = np.arange(c.NCC * 128)
    j = np.arange(256)
    ov = ((n[:, None] * 16 <= j[None, :] * 64 + 63) & (n[:, None] * 16 + 31 >= j[None, :] * 64)).astype(np.float32)
    ov[c.NCMP:, :] = 0
    exps = np.zeros((128, 64, 128), np.float32)
    for m in range(64):
        for k in range(128):
            exps[2 * m + k // 64, m, k] = 1.0
    expm = np.zeros((64, 64, 128), np.float32)
    for m in range(64):
        expm[m, m, :] = 1.0
    invf = (10000.0 ** (-np.arange(0, 128, 2, dtype=np.float32) / 128)).astype(np.float32)[None, :]
    jrow = np.arange(256, dtype=np.float32)[None, :]
    return {"c_identb": ident.astype(bf), "c_identf": ident, "c_tril": tril, "c_ov": ov.astype(bf),
            "c_exps": exps.reshape(128, -1).astype(bf), "c_expm": expm.reshape(64, -1).astype(bf), "c_invf": invf,
            "c_jrow": jrow}


def host_inputs(cfg, inputs):
    c = cfg
    consts = host_constants(c)
    x = np.ascontiguousarray(np.asarray(inputs["x"], np.float32).reshape(c.S, c.D))
    pos = np.ascontiguousarray(np.asarray(inputs["positions"], np.int32).reshape(c.S, 1))
    idx_all = np.arange(c.S)
    info_all = np.stack([idx_all, idx_all // 64, idx_all // 256, np.zeros_like(idx_all)], 1).astype(np.float32)
    row_all = idx_all.reshape(c.NT, 128).astype(np.float32)
    maps = []
    for core in range(c.NC):
        tiles = [c.NC * j + core for j in range(c.NTO)]
        own = np.concatenate([np.arange(t * 128, (t + 1) * 128) for t in tiles])
        m = dict(consts)
        m["x"] = x
        m["pos"] = pos
        m["ownpos"] = np.ascontiguousarray(pos[own])
        oh = np.zeros((128, c.NC), np.float32)
        oh[:, core] = 1.0
        m["onehot"] = oh
        m["info_all"] = info_all
        m["info_own"] = np.ascontiguousarray(info_all[own])
        m["row_all"] = row_all
        m["row_own"] = np.ascontiguousarray(own.reshape(c.NTO, 128).astype(np.float32))
        for k, v in inputs.items():
            if k in ("x", "positions"):
                continue
            m[k] = np.ascontiguousarray(np.asarray(v, np.float32))
        maps.append(m)
    return maps


def run(cfg, inputs, debug=(), trace=False):
    nc = build(cfg, debug)
    maps = host_inputs(cfg, inputs)
    res = run_bass_kernel_spmd(nc, maps, core_ids=list(range(cfg.NC)), trace=trace)
    return res


N_CORES_USED = 4


def kernel(**inputs):
    cfg = Cfg(L=1, NC=N_CORES_USED)
    depth = int(np.asarray(inputs["w_in"]).shape[0])
    nc = build(cfg)
    x = np.ascontiguousarray(np.asarray(inputs["x"], np.float32).reshape(cfg.S, cfg.D))
    for l in range(depth):
        inp = {}
        for k, v in inputs.items():
            if k == "x":
                inp[k] = x
            elif k == "positions":
                inp[k] = v
            else:
                inp[k] = np.asarray(v)[l:l + 1]
        maps = host_inputs(cfg, inp)
        res = run_bass_kernel_spmd(nc, maps, core_ids=list(range(cfg.NC)))
        out = np.empty((cfg.S, cfg.D), np.float32)
        for core in range(cfg.NC):
            y = res.results[core]["y"]
            for j in range(cfg.NTO):
                t = cfg.NC * j + core
                out[t * 128:(t + 1) * 128] = y[j * 128:(j + 1) * 128]
        x = out
    return x.reshape(1, cfg.S, cfg.D)
```

```python
import math
import numpy as np
import ml_dtypes
from contextlib import ExitStack
import concourse.bass as bass
import concourse.mybir as mybir
from concourse.bass_utils import run_bass_kernel_spmd

F32 = mybir.dt.float32
BF16 = mybir.dt.bfloat16
I32 = mybir.dt.int32
AF = mybir.ActivationFunctionType
ALU = mybir.AluOpType
AX = mybir.AxisListType
EPS = 1e-6
NEGBIG = -1e30
TWO_PI = 2.0 * math.pi


class Cfg:
    def __init__(s, S=16384, L=2, DFF=16384, NC=8):
        s.D = 4096
        s.S, s.L, s.DFF, s.NC = S, L, DFF, NC
        s.HD = 128
        s.NH, s.NG, s.MH, s.SW = 16, 4, 8, 1024
        s.KC = s.D // 128
        s.NT = S // 128
        s.NTO = s.NT // NC
        s.NCMP = (S - 32) // 16 + 1
        s.NCC = (s.NCMP + 127) // 128
        s.NSLC = S // 64
        s.NMB = S // 256
        s.TG = 4
        o = {}
        off = 0
        for name, w in (("qa", 2048), ("kc", 512), ("vc", 512), ("ks", 512), ("vs", 512), ("kw", 512),
                        ("vw", 512), ("ga", 48), ("ub", 1024), ("vb", 1024), ("qm", 1024), ("km", 1024),
                        ("vm", 1024), ("gma", 4096), ("gmb", 4096), ("gmc", 4096)):
            o[name] = off
            off += w
        s.off = o
        s.INW = off
        assert s.INW == 22576
        assert s.NT % (NC * 1) == 0 and s.NMB >= 8 and s.NSLC >= 16


class Track:
    __slots__ = ("w", "r")

    def __init__(s):
        s.w = None
        s.r = {}


class Tl:
    def __init__(s, h):
        s.h = h
        s.tr = Track()

    def __getitem__(s, i):
        return s.h[i]


NDS = 8
NWBUF = 3
NPC = 3
SAME_ENGINE_SYNC = True


class PB:
    def __init__(s, nc, es):
        s.nc, s.es = nc, es
        s.E = {"pe": nc.tensor, "act": nc.scalar, "dve": nc.vector, "pool": nc.gpsimd, "sp": nc.sync}
        s.sems = []
        s.cidx, s.ccnt = {}, {}
        for e in ("pe", "act", "dve", "pool"):
            s.cidx[e] = len(s.sems)
            s.sems.append(es.enter_context(nc.semaphore("cs_" + e)))
            s.ccnt[e] = 0
        s.dq = {}
        for q in ("sp", "pool", "act"):
            idx = []
            for i in range(NDS):
                idx.append(len(s.sems))
                s.sems.append(es.enter_context(nc.semaphore(f"ds_{q}{i}")))
            s.dq[q] = {"idx": idx, "val": [0] * NDS, "nxt": 0}
        s.seen = {}
        s.nins = 0

    def wait(s, e, tok):
        k = (e, tok[0])
        if e == "pe" and tok[0] == s.cidx["pe"]:
            return
        if not SAME_ENGINE_SYNC and e in s.cidx and tok[0] == s.cidx[e]:
            return
        if s.seen.get(k, 0) >= tok[1]:
            return
        s.E[e].wait_ge(s.sems[tok[0]], tok[1])
        s.seen[k] = tok[1]

    def _deps(s, e, reads, writes):
        for t in reads:
            if t.w is not None:
                s.wait(e, t.w)
        for t in writes:
            if t.w is not None:
                s.wait(e, t.w)
            for tok in t.r.values():
                s.wait(e, tok)

    def _mark(s, tok, reads, writes):
        for t in reads:
            t.r[tok[0]] = tok
        for t in writes:
            t.w = tok
            t.r = {}

    @staticmethod
    def _tr(lst):
        return [x.tr if isinstance(x, Tl) else x for x in lst]

    def op(s, e, fns, reads=(), writes=()):
        reads, writes = s._tr(reads), s._tr(writes)
        s._deps(e, reads, writes)
        if not isinstance(fns, (list, tuple)):
            fns = [fns]
        ins = None
        for f in fns:
            ins = f(s.E[e])
            s.nins += 1
        s.ccnt[e] += 1
        tok = (s.cidx[e], s.ccnt[e])
        ins.then_inc(s.sems[tok[0]], 1)
        s._mark(tok, reads, writes)
        return tok

    def dma(s, q, out, in_, reads=(), writes=(), slow=False):
        reads, writes = s._tr(reads), s._tr(writes)
        d = s.dq[q]
        i = d["nxt"]
        d["nxt"] = (i + 1) % NDS
        si = d["idx"][i]
        if d["val"][i] > 0:
            s.wait(q, (si, d["val"][i]))
        s._deps(q, reads, writes)
        ins = s.E[q].dma_start(out=out, in_=in_, allow_slow_non_contiguous=True) if slow else s.E[q].dma_start(out=out, in_=in_)
        s.nins += 1
        d["val"][i] += 16
        tok = (si, d["val"][i])
        ins.then_inc(s.sems[si], 16)
        s._mark(tok, reads, writes)
        return tok

    def all_tokens(s):
        toks = [(s.cidx[e], s.ccnt[e]) for e in s.cidx if s.ccnt[e] > 0]
        for q in s.dq.values():
            for si, v in zip(q["idx"], q["val"]):
                if v > 0:
                    toks.append((si, v))
        return toks

    def barrier(s, engines=("pe", "act", "dve", "pool", "sp")):
        toks = s.all_tokens()
        for e in engines:
            for t in toks:
                s.wait(e, t)


class Scope:
    uid = 0

    def __init__(s, pb):
        s.pb = pb
        s.es = ExitStack()

    def __enter__(s):
        s.es.__enter__()
        return s

    def __exit__(s, *a):
        s.pb.barrier()
        return s.es.__exit__(*a)

    def sb(s, name, shape, dt):
        Scope.uid += 1
        return Tl(s.es.enter_context(s.pb.nc.sbuf_tensor(f"s{Scope.uid}_" + name, list(shape), dt)))


def bc(ap, shape):
    return ap.broadcast_to(list(shape))


def build(cfg, debug=()):
    c = cfg
    D, S, L, DFF, NC, KC, TG = c.D, c.S, c.L, c.DFF, c.NC, c.KC, c.TG
    NT, NTO, NG, NH, MH = c.NT, c.NTO, c.NG, c.NH, c.MH
    TOK = TG * 128
    nc = bass.Bass("TRN2", target_bir_lowering=False)

    def din(name, shape, dt=F32):
        return nc.dram_tensor(name, list(shape), dt, kind="ExternalInput").ap()

    def dscr(name, shape, dt):
        return nc.dram_tensor(name, list(shape), dt, kind="Internal").ap()

    x_in = din("x", [S, D])
    pos_in = din("pos", [S, 1], I32)
    ownpos_in = din("ownpos", [NTO * 128, 1], I32)
    onehot_in = din("onehot", [128, NC])
    info_all = din("info_all", [NT * 128, 4])
    info_own = din("info_own", [NTO * 128, 4])
    row_all = din("row_all", [NT, 128])
    row_own = din("row_own", [NTO, 128])
    W = {}
    for nm, shp in (("norm_mix", [L, D]), ("norm_mlp", [L, D]), ("w_in", [L, D, c.INW]), ("nsa_gate_b", [L, 48]),
                    ("nsa_q_norm", [L, 128]), ("nsa_kc_norm", [L, 128]), ("nsa_ks_norm", [L, 128]),
                    ("nsa_kw_norm", [L, 128]), ("phi_pe_k", [L, 32, 128]), ("phi_w1_k", [L, 4096, 512]),
                    ("phi_w2_k", [L, 512, 128]), ("phi_pe_v", [L, 32, 128]), ("phi_w1_v", [L, 4096, 512]),
                    ("phi_w2_v", [L, 512, 128]), ("sgu_norm", [L, 1024]), ("sgu_w", [L, 8, 128, 128]),
                    ("sgu_b", [L, 8, 128]), ("moba_q_norm", [L, 128]), ("moba_k_norm", [L, 128]),
                    ("proj_a", [L, 2048, D]), ("proj_b", [L, 1024, D]), ("proj_c", [L, 1024, D]),
                    ("w_out", [L, D, D]), ("mlp_w1", [L, D, DFF]), ("mlp_w2", [L, DFF, D])):
        W[nm] = din(nm, shp)
    c_identb = din("c_identb", [128, 128], BF16)
    c_identf = din("c_identf", [128, 128])
    c_tril = din("c_tril", [128, 128])
    c_ov = din("c_ov", [c.NCC * 128, 256], BF16)
    c_exps = din("c_exps", [128, 64 * 128], BF16)
    c_expm = din("c_expm", [64, 64 * 128], BF16)
    c_invf = din("c_invf", [1, 64])
    c_jrow = din("c_jrow", [1, 256])
    y_out = nc.dram_tensor("y", [NTO * 128, D], F32, kind="ExternalOutput").ap()
    dbg = {}
    for nm, shp, dt in debug:
        dbg[nm] = nc.dram_tensor("dbg_" + nm, list(shp), dt, kind="ExternalOutput").ap()

    wb = {"w_in": dscr("wb_w_in", [D, c.INW], BF16), "phi_w1_k": dscr("wb_p1k", [4096, 512], BF16),
          "phi_w1_v": dscr("wb_p1v", [4096, 512], BF16), "phi_w2_k": dscr("wb_p2k", [512, 128], BF16),
          "phi_w2_v": dscr("wb_p2v", [512, 128], BF16), "proj_a": dscr("wb_pa", [2048, D], BF16),
          "proj_b": dscr("wb_pb", [1024, D], BF16), "proj_c": dscr("wb_pc", [1024, D], BF16),
          "w_out": dscr("wb_wo", [D, D], BF16), "mlp_w1": dscr("wb_w1", [D, DFF], BF16),
          "mlp_w2": dscr("wb_w2", [DFF, D], BF16)}
    wb_tr = {k: Track() for k in wb}
    xs = [None] + [dscr(f"xs{l}", [S, D], F32) for l in range(1, L)]
    xs_tr = [None] + [Track() for _ in range(1, L)]
    kcrawT = dscr("kcrawT", [128, NG, S], BF16)
    vcrawT = dscr("vcrawT", [128, NG, S], BF16)
    ksT = dscr("ksT", [128, NG, S], BF16)
    kwT = dscr("kwT", [128, NG, S], BF16)
    kmT = dscr("kmT", [128, MH, S], BF16)
    vs_s = dscr("vs_s", [S, NG, 128], BF16)
    vw_s = dscr("vw_s", [S, NG, 128], BF16)
    vm_s = dscr("vm_s", [S, MH, 128], BF16)
    kcT_s = dscr("kcT_s", [128, NG, c.NCC * 128], BF16)
    vc_s = dscr("vc_s", [c.NCC * 128, NG, 128], BF16)
    kmeanT_s = dscr("kmeanT_s", [128, MH, c.NMB], BF16)
    kv_tr = Track()
    hT_s = dscr("hT_s", [128, KC, TOK], BF16)
    qnT_s = dscr("qnT_s", [128, NH, TOK], BF16)
    qrT_s = dscr("qrT_s", [128, NH, TOK], BF16)
    qmT_s = dscr("qmT_s", [128, MH, TOK], BF16)
    oT_s = dscr("oT_s", [128, KC, TOK], BF16)
    xg_s = dscr("xg_s", [TG, 128, D], F32)
    grp_tr = Track()

    es = ExitStack()
    with es:
        pb = PB(nc, es)

        def gsb(name, shape, dt):
            return Tl(es.enter_context(nc.sbuf_tensor("g_" + name, list(shape), dt)))

        PS = [Tl(es.enter_context(nc.psum_tensor(f"ps{i}", [128, 512], F32))) for i in range(8)]
        ps_rr = [0]

        def psn():
            t = PS[ps_rr[0] % 8]
            ps_rr[0] += 1
            return t

        identb = gsb("identb", [128, 128], BF16)
        identf = gsb("identf", [128, 128], F32)
        onesf = gsb("onesf", [128, 128], F32)
        invf = gsb("invf", [128, 64], F32)
        pcol = gsb("pcol", [128, 1], F32)
        onehot = gsb("onehot", [128, NC], F32)
        gains = gsb("gains", [128, 6, 128], F32)
        kcg_col = gsb("kcg_col", [128, 1], F32)
        gbias = gsb("gbias", [128, 48], F32)
        sgng = gsb("sgng", [128, 1024], F32)
        sgwT = gsb("sgwT", [128, 8, 128], BF16)
        sgb = gsb("sgb", [128, 8], F32)
        pek = gsb("pek", [128, 32], F32)
        pev = gsb("pev", [128, 32], F32)
        gmix = gsb("gmix", [128, KC], F32)
        gmlp = gsb("gmlp", [128, KC], F32)
        pb.dma("sp", identb[:], c_identb[:, :], writes=[identb])
        pb.dma("sp", identf[:], c_identf[:, :], writes=[identf])
        pb.dma("sp", invf[:], c_invf[0:1, :].partition_broadcast(128), writes=[invf])
        pb.dma("sp", onehot[:], onehot_in[:, :], writes=[onehot])
        pb.op("dve", lambda e: e.memset(onesf[:], 1.0), writes=[onesf])
        pb.op("pool", lambda e: e.iota(pcol[:], [[0, 1]], base=0, channel_multiplier=1,
                                       allow_small_or_imprecise_dtypes=True), writes=[pcol])

        def rope_tables(sc, pos_ap, tag):
            pi_ = sc.sb("rp_i" + tag, [128, 1], I32)
            pf = sc.sb("rp_f" + tag, [128, 1], F32)
            ang = sc.sb("rp_a" + tag, [128, 2, 64], F32)
            kf = sc.sb("rp_k" + tag, [128, 2, 64], F32)
            ki = sc.sb("rp_ki" + tag, [128, 2, 64], I32)
            cs = sc.sb("rp_cs" + tag, [128, 2, 64], F32)
            pb.dma("pool", pi_[:], pos_ap, writes=[pi_])
            pb.op("dve", lambda e: e.tensor_copy(out=pf[:], in_=pi_[:]), reads=[pi_], writes=[pf])
            pb.op("dve", lambda e: e.tensor_scalar(out=ang[:, 0, :], in0=invf[:], scalar1=pf[:, 0:1], scalar2=None,
                                                   op0=ALU.mult), reads=[invf, pf], writes=[ang])
            pb.op("dve", lambda e: e.tensor_scalar(out=ang[:, 1, :], in0=ang[:, 0, :], scalar1=math.pi / 2,
                                                   scalar2=None, op0=ALU.add), reads=[ang], writes=[ang])
            pb.op("dve", lambda e: e.tensor_scalar(out=kf[:], in0=ang[:], scalar1=1.0 / TWO_PI, scalar2=None,
                                                   op0=ALU.mult), reads=[ang], writes=[kf])
            pb.op("dve", lambda e: e.tensor_copy(out=ki[:], in_=kf[:]), reads=[kf], writes=[ki])
            pb.op("dve", lambda e: e.tensor_copy(out=kf[:], in_=ki[:]), reads=[ki], writes=[kf])
            C1 = 6.28125
            C2 = TWO_PI - C1
            pb.op("dve", lambda e: e.scalar_tensor_tensor(out=ang[:], in0=kf[:], scalar=-C1, in1=ang[:],
                                                          op0=ALU.mult, op1=ALU.add), reads=[kf, ang], writes=[ang])
            pb.op("dve", lambda e: e.scalar_tensor_tensor(out=ang[:], in0=kf[:], scalar=-C2, in1=ang[:],
                                                          op0=ALU.mult, op1=ALU.add), reads=[kf, ang], writes=[ang])
            pb.op("dve", lambda e: e.tensor_scalar(out=kf[:], in0=ang[:], scalar1=0.0, scalar2=TWO_PI,
                                                   op0=ALU.is_lt, op1=ALU.mult), reads=[ang], writes=[kf])
            pb.op("dve", lambda e: e.tensor_tensor(out=ang[:], in0=ang[:], in1=kf[:], op=ALU.add),
                  reads=[ang, kf], writes=[ang])
            pb.op("dve", lambda e: e.tensor_scalar(out=kf[:], in0=ang[:], scalar1=TWO_PI, scalar2=-TWO_PI,
                                                   op0=ALU.is_ge, op1=ALU.mult), reads=[ang], writes=[kf])
            pb.op("dve", lambda e: e.tensor_tensor(out=ang[:], in0=ang[:], in1=kf[:], op=ALU.add),
                  reads=[ang, kf], writes=[ang])
            pb.op("dve", lambda e: e.tensor_scalar(out=ang[:], in0=ang[:], scalar1=-1.0, scalar2=math.pi,
                                                   op0=ALU.mult, op1=ALU.add), reads=[ang], writes=[ang])
            pb.op("act", lambda e: e.activation(out=cs[:], in_=ang[:], func=AF.Sin), reads=[ang], writes=[cs])
            return cs

        def norm_T(sc, xt, hT, t, tag, xap=None):
            sq = sc_tmp["sq"]
            ss = sc.sb("nt_ss" + tag, [128, 1], F32)
            xb = sc_tmp["xb"]
            xap = xt[:] if xap is None else xap
            pb.op("act", lambda e: e.activation(out=sq[:], in_=xap, func=AF.Square), reads=[xt], writes=[sq])
            pb.op("dve", lambda e: e.tensor_reduce(out=ss[:], in_=sq[:], axis=AX.X, op=ALU.add), reads=[sq],
                  writes=[ss])
            rstd_from_ss(ss, ss, 1.0 / D)
            pb.op("act", lambda e: e.activation(out=xb[:], in_=xap, func=AF.Copy, scale=ss[:, 0:1]),
                  reads=[xt, ss], writes=[xb])
            transpose_to(xb, KC, lambda k0, n: hT[:, k0:k0 + n, t * 128:(t + 1) * 128], [hT])

        def rstd_from_ss(out, ss, inv_n):
            pb.op("dve", lambda e: e.tensor_scalar(out=out[:], in0=ss[:], scalar1=inv_n, scalar2=EPS, op0=ALU.mult,
                                                   op1=ALU.add), reads=[ss], writes=[out])
            pb.op("act", lambda e: e.activation(out=out[:], in_=out[:], func=AF.Sqrt), reads=[out], writes=[out])
            pb.op("dve", lambda e: e.reciprocal(out=out[:], in_=out[:]), reads=[out], writes=[out])

        def transpose_to(src, nblk, dst_fn, dst_tiles, src_fn=None, bank=None):
            k0 = 0
            i = 0
            while k0 < nblk:
                n = min(8, nblk - k0)
                pt = psn() if bank is None else bank
                ptb = pt[:].bitcast(BF16)
                fns = []
                for j in range(n):
                    sap = src_fn(k0 + j) if src_fn else src[:, (k0 + j) * 128:(k0 + j + 1) * 128]
                    fns.append(lambda e, j=j, sap=sap: e.transpose(out=ptb[:, j * 128:(j + 1) * 128], in_=sap,
                                                                   identity=identb[:]))
                pb.op("pe", fns, reads=[src, identb], writes=[pt])
                dst = dst_fn(k0, n)
                srcv = ptb[:, 0:n * 128].rearrange("p (n f) -> p n f", f=128)
                eng = "act" if (i % 2 == 0) else "dve"
                if eng == "act":
                    pb.op("act", lambda e: e.copy(out=dst, in_=srcv), reads=[pt], writes=dst_tiles)
                else:
                    pb.op("dve", lambda e: e.tensor_copy(out=dst, in_=srcv), reads=[pt], writes=dst_tiles)
                k0 += n
                i += 1

        wbufs = []
        wb_rr = [0]

        def wload(wname, k0, nk, c0, ncols):
            t = wbufs[wb_rr[0] % len(wbufs)]
            wb_rr[0] += 1
            src = wb[wname][k0 * 128:(k0 + nk) * 128, c0:c0 + ncols].rearrange("(k p) n -> p k n", p=128)
            pb.dma("sp", t[:, 0:nk, 0:ncols], src, reads=[wb_tr[wname]], writes=[t])
            return t

        def gemm(wname, c0, ncols, act, kc0, nkc, ntile, mode, epi, wrow0=0):
            nout = ntile if mode == "tok" else ncols // 128
            pss = [psn() for _ in range(nout)]
            kk = 0
            while kk < nkc:
                nk = min(16, nkc - kk)
                wt = wload(wname, wrow0 + kk, nk, c0, ncols)
                for i in range(nout):
                    fns = []
                    for k in range(nk):
                        if mode == "tok":
                            l_ap = act[:, kc0 + kk + k, i * 128:(i + 1) * 128]
                            r_ap = wt[:, k, 0:ncols]
                            o_ap = pss[i][:, 0:ncols]
                        else:
                            l_ap = wt[:, k, i * 128:(i + 1) * 128]
                            r_ap = act[:, kc0 + kk + k, 0:ntile * 128]
                            o_ap = pss[i][:, 0:ntile * 128]
                        fns.append(lambda e, l_ap=l_ap, r_ap=r_ap, o_ap=o_ap, st=(kk + k == 0),
                                   sp_=(kk + k == nkc - 1): e.matmul(o_ap, lhsT=l_ap, rhs=r_ap, start=st, stop=sp_))
                    pb.op("pe", fns, reads=[act, wt], writes=[pss[i]])
                kk += nk
            for i in range(nout):
                epi(i, pss[i])

        def precast(sc, l):
            st = [sc.sb(f"pc_f{i}", [128, 4096], F32) for i in range(NPC)]
            sb_ = [sc.sb(f"pc_b{i}", [128, 4096], BF16) for i in range(NPC)]
            pb.dma("sp", gmix[:], W["norm_mix"][l:l + 1, :].rearrange("o (k p) -> p (o k)", p=128), writes=[gmix], slow=True)
            pb.dma("sp", gmlp[:], W["norm_mlp"][l:l + 1, :].rearrange("o (k p) -> p (o k)", p=128), writes=[gmlp], slow=True)
            i = 0
            for nm in wb:
                src = W[nm][l]
                R, C = src.shape
                g = gmix if nm == "w_in" else (gmlp if nm == "mlp_w1" else None)
                for r in range(R // 128):
                    cc = 0
                    while cc < C:
                        n = min(4096, C - cc)
                        a, b = st[i % NPC], sb_[i % NPC]
                        pb.dma("sp", a[:, 0:n], src[r * 128:(r + 1) * 128, cc:cc + n], writes=[a])
                        eng = ("dve", "act", "pool")[i % 3] if g is None else ("dve", "pool")[i % 2]
                        if g is None:
                            if eng == "act":
                                pb.op("act", lambda e, a=a, b=b, n=n: e.copy(out=b[:, 0:n], in_=a[:, 0:n]),
                                      reads=[a], writes=[b])
                            else:
                                pb.op(eng, lambda e, a=a, b=b, n=n: e.tensor_copy(out=b[:, 0:n], in_=a[:, 0:n]),
                                      reads=[a], writes=[b])
                        else:
                            pb.op(eng, lambda e, a=a, b=b, n=n, r=r, g=g: e.tensor_scalar(
                                out=b[:, 0:n], in0=a[:, 0:n], scalar1=g[:, r:r + 1], scalar2=None, op0=ALU.mult),
                                reads=[a, g], writes=[b])
                        pb.dma("pool", wb[nm][r * 128:(r + 1) * 128, cc:cc + n], b[:, 0:n], reads=[b],
                               writes=[wb_tr[nm]])
                        cc += n
                        i += 1

        def load_layer_params(sc, l):
            for gi, nm in enumerate(("nsa_q_norm", "nsa_kc_norm", "nsa_ks_norm", "nsa_kw_norm", "moba_q_norm",
                                     "moba_k_norm")):
                pb.dma("pool", gains[:, gi, :], W[nm][l:l + 1, :].partition_broadcast(128), writes=[gains])
            pb.op("dve", lambda e: e.tensor_scalar(out=gains[:, 0, :], in0=gains[:, 0, :], scalar1=128 ** -0.5,
                                                   scalar2=None, op0=ALU.mult), reads=[gains], writes=[gains])
            pb.op("dve", lambda e: e.tensor_scalar(out=gains[:, 4, :], in0=gains[:, 4, :], scalar1=128 ** -0.5,
                                                   scalar2=None, op0=ALU.mult), reads=[gains], writes=[gains])
            pb.dma("pool", kcg_col[:], W["nsa_kc_norm"][l:l + 1, :].rearrange("o p -> p o"), writes=[kcg_col], slow=True)
            pb.dma("pool", gbias[:], W["nsa_gate_b"][l:l + 1, :].partition_broadcast(128), writes=[gbias])
            pb.dma("pool", sgng[:], W["sgu_norm"][l:l + 1, :].partition_broadcast(128), writes=[sgng])
            pb.dma("pool", sgb[:], W["sgu_b"][l].rearrange("g t -> t g"), writes=[sgb], slow=True)
            pb.dma("pool", pek[:], W["phi_pe_k"][l].rearrange("l d -> d l"), writes=[pek], slow=True)
            pb.dma("pool", pev[:], W["phi_pe_v"][l].rearrange("l d -> d l"), writes=[pev], slow=True)
            tril = sc.sb("lp_tril", [128, 128], F32)
            wtmp = sc.sb("lp_w", [128, 128], F32)
            pb.dma("pool", tril[:], c_tril[:, :], writes=[tril])
            for g in range(8):
                pb.dma("pool", wtmp[:], W["sgu_w"][l, g], writes=[wtmp])
                pb.op("dve", lambda e: e.tensor_tensor(out=wtmp[:], in0=wtmp[:], in1=tril[:], op=ALU.mult),
                      reads=[wtmp, tril], writes=[wtmp])
                pt = psn()
                pb.op("pe", lambda e: e.transpose(out=pt[:, 0:128], in_=wtmp[:], identity=identf[:]),
                      reads=[wtmp, identf], writes=[pt])
                pb.op("dve", lambda e, g=g: e.tensor_copy(out=sgwT[:, g, :], in_=pt[:, 0:128]), reads=[pt],
                      writes=[sgwT])

        sc_tmp = {}

        def headnorm_rope(sc, ps, nheads, gain_idx, cs, out_rot, out_plain, tag):
            W_ = nheads * 128
            sq = sc_tmp["hn_sq"]
            xn = sc_tmp["hn_xn"]
            ss = sc.sb("hn_ss" + tag, [128, 4], F32)
            pb.op("act", lambda e: e.activation(out=sq[:, 0:W_], in_=ps[:, 0:W_], func=AF.Square), reads=[ps],
                  writes=[sq])
            pb.op("dve", lambda e: e.tensor_reduce(out=ss[:, 0:nheads],
                                                   in_=sq[:, 0:W_].rearrange("p (h d) -> p h d", d=128),
                                                   axis=AX.X, op=ALU.add), reads=[sq], writes=[ss])
            rstd_from_ss(ss, ss, 1.0 / 128)
            xn3 = xn[:, 0:W_].rearrange("p (h d) -> p h d", d=128)
            pb.op("dve", lambda e: e.tensor_tensor(out=xn3, in0=ps[:, 0:W_].rearrange("p (h d) -> p h d", d=128),
                                                   in1=bc(ss[:, 0:nheads].unsqueeze(2), [128, nheads, 128]),
                                                   op=ALU.mult), reads=[ps, ss], writes=[xn])
            gr = bc(gains[:, gain_idx:gain_idx + 1, :], [128, nheads, 128])
            if out_plain is not None:
                pb.op("pool", lambda e: e.tensor_tensor(out=out_plain, in0=xn3, in1=gr, op=ALU.mult),
                      reads=[xn, gains], writes=[sc_tmp["hn_outp"]])
            if out_rot is not None:
                pb.op("dve", lambda e: e.tensor_tensor(out=xn3, in0=xn3, in1=gr, op=ALU.mult), reads=[xn, gains],
                      writes=[xn])
                t1 = sc_tmp["hn_t1"]
                t2 = sc_tmp["hn_t2"]
                x1 = xn3[:, :, 0:64]
                x2 = xn3[:, :, 64:128]
                cosb = bc(cs[:, 1:2, :], [128, nheads, 64])
                sinb = bc(cs[:, 0:1, :], [128, nheads, 64])
                t1v = t1[:, 0:nheads * 64].rearrange("p (h d) -> p h d", d=64)
                t2v = t2[:, 0:nheads * 64].rearrange("p (h d) -> p h d", d=64)
                pb.op("dve", lambda e: e.tensor_tensor(out=t1v, in0=x1, in1=cosb, op=ALU.mult), reads=[xn, cs],
                      writes=[t1])
                pb.op("pool", lambda e: e.tensor_tensor(out=t2v, in0=x2, in1=sinb, op=ALU.mult), reads=[xn, cs],
                      writes=[t2])
                pb.op("dve", lambda e: e.tensor_tensor(out=out_rot[:, :, 0:64], in0=t1v, in1=t2v, op=ALU.subtract),
                      reads=[t1, t2], writes=[sc_tmp["hn_outr"]])
                pb.op("dve", lambda e: e.tensor_tensor(out=t1v, in0=x2, in1=cosb, op=ALU.mult), reads=[xn, cs],
                      writes=[t1])
                pb.op("pool", lambda e: e.tensor_tensor(out=t2v, in0=x1, in1=sinb, op=ALU.mult), reads=[xn, cs],
                      writes=[t2])
                pb.op("dve", lambda e: e.tensor_tensor(out=out_rot[:, :, 64:128], in0=t1v, in1=t2v, op=ALU.add),
                      reads=[t1, t2], writes=[sc_tmp["hn_outr"]])

        def alloc_common_tmps(sc):
            sc_tmp["xb"] = sc.sb("t_xb", [128, D], BF16)
            sc_tmp["sq"] = sc_tmp["xb"]
            sc_tmp["hn_sq"] = sc.sb("t_hnsq", [128, 512], F32)
            sc_tmp["hn_xn"] = sc.sb("t_hnxn", [128, 512], F32)
            sc_tmp["hn_t1"] = sc.sb("t_hnt1", [128, 256], F32)
            sc_tmp["hn_t2"] = sc.sb("t_hnt2", [128, 256], F32)
            sc_tmp["hn_outp"] = sc.sb("t_hnop", [128, 512], BF16)
            sc_tmp["hn_outr"] = sc.sb("t_hnor", [128, 512], BF16)
            wbufs.clear()
            for i in range(NWBUF):
                wbufs.append(sc.sb(f"wbuf{i}", [128, 16, 512], BF16))

        def load_x_tile(sc, l, xt, tile_idx):
            if l == 0:
                pb.dma("sp", xt[:], x_in[tile_idx * 128:(tile_idx + 1) * 128, :], writes=[xt])
            else:
                pb.dma("sp", xt[:], xs[l][tile_idx * 128:(tile_idx + 1) * 128, :], reads=[xs_tr[l]], writes=[xt])

        def kv_pass(l):
            with Scope(pb) as sc:
                alloc_common_tmps(sc)
                hT = sc.sb("kv_hT", [128, KC, TOK], BF16)
                xts = [sc.sb(f"kv_x{i}", [128, D], F32) for i in range(2)]
                stage = [sc.sb(f"kv_st{i}", [128, 4, 128], BF16) for i in range(2)]
                st_i = [0]
                for g0 in range(0, NT, TG):
                    sg_ = Scope(pb)
                    sg_.__enter__()
                    css = []
                    for t in range(TG):
                        xt = xts[t % 2]
                        load_x_tile(sc, l, xt, g0 + t)
                        norm_T(sg_, xt, hT, t, f"kv{t % 2}")
                        css.append(rope_tables(sg_, pos_in[(g0 + t) * 128:(g0 + t + 1) * 128, :], f"kv{t}"))

                    def epi_factory(kind, dstT, dstV, h0, gain_idx):
                        def epi(t, ps):
                            tok0 = (g0 + t) * 128
                            if kind == "v":
                                sg = stage[st_i[0] % 2]
                                st_i[0] += 1
                                pb.op("act", lambda e: e.copy(out=sg[:].rearrange("p h d -> p (h d)"), in_=ps[:]),
                                      reads=[ps], writes=[sg])
                                pb.dma("pool", dstV[tok0:tok0 + 128, h0:h0 + 4, :], sg[:], reads=[sg],
                                       writes=[kv_tr])
                                return
                            src = sc_tmp["hn_outr"]
                            if kind == "kraw":
                                pb.op("act", lambda e: e.copy(out=src[:], in_=ps[:]), reads=[ps], writes=[src])
                            else:
                                headnorm_rope(sg_, ps, 4, gain_idx, css[t],
                                              src[:].rearrange("p (h d) -> p h d", d=128), None, "kv")
                            sg = stage[st_i[0] % 2]
                            st_i[0] += 1
                            transpose_to(src, 4, lambda k0, n: sg[:, k0:k0 + n, :], [sg])
                            pb.dma("pool", dstT[:, h0:h0 + 4, tok0:tok0 + 128], sg[:], reads=[sg], writes=[kv_tr])
                        return epi

                    o = c.off
                    plan = [("kraw", o["kc"], kcrawT, None, 0, 0), ("kraw", o["vc"], vcrawT, None, 0, 0),
                            ("knr", o["ks"], ksT, None, 0, 2), ("v", o["vs"], None, vs_s, 0, 0),
                            ("knr", o["kw"], kwT, None, 0, 3), ("v", o["vw"], None, vw_s, 0, 0),
                            ("knr", o["km"], kmT, None, 0, 5), ("knr", o["km"] + 512, kmT, None, 4, 5),
                            ("v", o["vm"], None, vm_s, 0, 0), ("v", o["vm"] + 512, None, vm_s, 4, 0)]
                    for kind, c0, dT, dV, h0, gi in plan:
                        gemm("w_in", c0, 512, hT, 0, KC, TG, "tok", epi_factory(kind, dT, dV, h0, gi))
                    sg_.__exit__(None, None, None)

        def dump(name, src_ap, tracks):
            if name in dbg:
                pb.dma("pool", dbg[name], src_ap, reads=tracks)

        ga_sb = gsb("ga_sb", [128, TG, 48], F32)
        p16col = gsb("p16col", [128, 1], F32)
        pb.op("dve", lambda e: e.tensor_scalar(out=p16col[:], in0=pcol[:], scalar1=16.0, scalar2=None, op0=ALU.mult),
              reads=[pcol], writes=[p16col])
        yT_s = dscr("yT_s", [128, KC, TOK], BF16)

        def alloc_wbufs(sc):
            wbufs.clear()
            for i in range(NWBUF):
                wbufs.append(sc.sb(f"wbuf{i}", [128, 16, 512], BF16))

        def gelu_from_psum(ps, W_, out_ap, out_tiles, tmp):
            x2, inner, sg = tmp
            pb.op("act", lambda e: e.activation(out=x2[:, 0:W_], in_=ps[:, 0:W_], func=AF.Square), reads=[ps],
                  writes=[x2])
            pb.op("dve", lambda e: e.tensor_scalar(out=inner[:, 0:W_], in0=x2[:, 0:W_], scalar1=0.044715, scalar2=1.0,
                                                   op0=ALU.mult, op1=ALU.add), reads=[x2], writes=[inner])
            pb.op("dve", lambda e: e.tensor_tensor(out=inner[:, 0:W_], in0=inner[:, 0:W_], in1=ps[:, 0:W_],
                                                   op=ALU.mult), reads=[inner, ps], writes=[inner])
            pb.op("act", lambda e: e.activation(out=sg[:, 0:W_], in_=inner[:, 0:W_], func=AF.Sigmoid,
                                                scale=1.5957691216057308), reads=[inner], writes=[sg])
            pb.op("dve", lambda e: e.tensor_tensor(out=out_ap, in0=sg[:, 0:W_], in1=ps[:, 0:W_], op=ALU.mult),
                  reads=[sg, ps], writes=out_tiles)

        def compress(l):
            CW = min(512, c.NCC * 128)
            with Scope(pb) as sc:
                kr = sc.sb("cp_kr", [128, S], BF16)
                klT = sc.sb("cp_kl", [128, 32, CW], BF16)
                w1 = sc.sb("cp_w1", [128, 32, 512], BF16)
                w2 = sc.sb("cp_w2", [128, 4, 128], BF16)
                g1T = sc.sb("cp_g1", [128, 4, CW], BF16)
                tmp = [sc.sb(f"cp_t{i}", [128, 512], F32) for i in range(3)]
                kst = sc.sb("cp_kst", [128, CW], BF16)
                vst = sc.sb("cp_vst", [128, 4, 128], BF16)
                pb.op("pool", lambda e: e.memset(klT[:], 0.0), writes=[klT])
                for which in ("k", "v"):
                    pe = pek if which == "k" else pev
                    n1, n2 = "phi_w1_" + which, "phi_w2_" + which
                    pb.dma("sp", w1[:], wb[n1].rearrange("(l d) h -> d l h", d=128), reads=[wb_tr[n1]], writes=[w1])
                    pb.dma("sp", w2[:], wb[n2].rearrange("(c h) d -> h c d", h=128), reads=[wb_tr[n2]], writes=[w2])
                    raw = kcrawT if which == "k" else vcrawT
                    for g in range(NG):
                        pb.dma("sp", kr[:], raw[:, g, :], reads=[kv_tr], writes=[kr])
                        for n0 in range(0, c.NCC * 128, CW):
                            nn = min(CW, c.NCMP - n0)
                            if nn < CW:
                                pb.op("pool", lambda e: e.memset(klT[:], 0.0), writes=[klT])
                            for lq in range(32):
                                src = kr[:, 16 * n0 + lq: 16 * n0 + lq + 16 * (nn - 1) + 1: 16]
                                pb.op(("dve", "pool")[lq % 2], lambda e, lq=lq, src=src: e.tensor_scalar(
                                    out=klT[:, lq, 0:nn], in0=src, scalar1=pe[:, lq:lq + 1], scalar2=None,
                                    op0=ALU.add), reads=[kr, pe], writes=[klT])
                            for hc in range(4):
                                ps = psn()
                                pb.op("pe", [lambda e, lq=lq, hc=hc, ps=ps: e.matmul(
                                    ps[:, 0:CW], lhsT=w1[:, lq, hc * 128:(hc + 1) * 128], rhs=klT[:, lq, :],
                                    start=(lq == 0), stop=(lq == 31)) for lq in range(32)], reads=[w1, klT], writes=[ps])
                                gelu_from_psum(ps, CW, g1T[:, hc, :], [g1T], tmp)
                            if which == "k":
                                ps = psn()
                                pb.op("pe", [lambda e, hc=hc, ps=ps: e.matmul(ps[:, 0:CW], lhsT=w2[:, hc, :],
                                                                             rhs=g1T[:, hc, :], start=(hc == 0),
                                                                             stop=(hc == 3)) for hc in range(4)],
                                      reads=[w2, g1T], writes=[ps])
                                x2, inner, _ = tmp
                                pb.op("act", lambda e: e.activation(out=x2[:, 0:CW], in_=ps[:, 0:CW], func=AF.Square),
                                      reads=[ps], writes=[x2])
                                ps2 = psn()
                                pb.op("pe", lambda e: e.matmul(ps2[:, 0:CW], lhsT=onesf[:], rhs=x2[:, 0:CW], start=True,
                                                               stop=True), reads=[onesf, x2], writes=[ps2])
                                rstd_from_ss(inner, ps2, 1.0 / 128)
                                pb.op("dve", lambda e: e.scalar_tensor_tensor(
                                    out=kst[:], in0=ps[:, 0:CW], scalar=kcg_col[:, 0:1], in1=inner[:, 0:CW],
                                    op0=ALU.mult, op1=ALU.mult), reads=[ps, kcg_col, inner], writes=[kst])
                                pb.dma("pool", kcT_s[:, g, n0:n0 + CW], kst[:], reads=[kst], writes=[kv_tr])
                            else:
                                for j in range(CW // 128):
                                    ps = psn()
                                    pb.op("pe", [lambda e, hc=hc, ps=ps, j=j: e.matmul(
                                        ps[:, 0:128], lhsT=g1T[:, hc, j * 128:(j + 1) * 128], rhs=w2[:, hc, :],
                                        start=(hc == 0), stop=(hc == 3)) for hc in range(4)], reads=[w2, g1T],
                                        writes=[ps])
                                    pb.op("act", lambda e, j=j, ps=ps: e.copy(out=vst[:, j, :], in_=ps[:, 0:128]),
                                          reads=[ps], writes=[vst])
                                pb.dma("pool", vc_s[n0:n0 + CW, g, :].rearrange("(j p) d -> p j d", p=128),
                                       vst[:, 0:CW // 128, :], reads=[vst], writes=[kv_tr])

        def kmean(l):
            with Scope(pb) as sc:
                kr = sc.sb("km_kr", [128, S], BF16)
                acc = sc.sb("km_acc", [128, c.NMB], F32)
                st = sc.sb("km_st", [128, MH, c.NMB], BF16)
                for h in range(MH):
                    pb.dma("sp", kr[:], kmT[:, h, :], reads=[kv_tr], writes=[kr])
                    pb.op("dve", lambda e: e.tensor_reduce(out=acc[:], in_=kr[:].rearrange("p (b k) -> p b k", k=256),
                                                           axis=AX.X, op=ALU.add), reads=[kr], writes=[acc])
                    pb.op("dve", lambda e, h=h: e.tensor_scalar(out=st[:, h, :], in0=acc[:], scalar1=1.0 / 256,
                                                                scalar2=None, op0=ALU.mult), reads=[acc], writes=[st])
                pb.dma("pool", kmeanT_s[:, :, :], st[:], reads=[st], writes=[kv_tr])

        def phase_a(l, last, qtiles):
            ntile = len(qtiles)
            src_x = x_in if l == 0 else xs[l]
            src_tr = [] if l == 0 else [xs_tr[l]]
            o = c.off
            with Scope(pb) as sc:
                alloc_common_tmps(sc)
                hT = sc.sb("a_hT", [128, KC, TOK], BF16)
                xt = sc.sb("a_x", [128, D], F32)
                xtmp = sc.sb("a_xtmp", [128, D], F32) if last else None
                gu = sc.sb("a_gu", [128, TG, 1024], BF16)
                gvn = sc.sb("a_gvn", [128, TG, 1024], BF16)
                gv = sc.sb("a_gv", [128, 512], F32)
                ob = sc.sb("a_ob", [128, 1024], BF16)
                tmp = [sc.sb(f"a_t{i}", [128, 512], F32) for i in range(3)]
                stg = [sc.sb(f"a_stg{i}", [128, 8, 128], BF16) for i in range(2)]
                ss4 = sc.sb("a_ss4", [128, 4], F32)
                stg_i = [0]
                css = []
                for t, qi in enumerate(qtiles):
                    if not last:
                        pb.dma("sp", xt[:], src_x[qi * 128:(qi + 1) * 128, :], reads=src_tr, writes=[xt])
                    else:
                        for m in range(NC):
                            ti = NC * qi + m
                            pb.dma("sp", xtmp[:], src_x[ti * 128:(ti + 1) * 128, :], reads=src_tr, writes=[xtmp])
                            if m == 0:
                                pb.op("dve", lambda e: e.tensor_scalar(out=xt[:], in0=xtmp[:], scalar1=onehot[:, 0:1],
                                                                       scalar2=None, op0=ALU.mult),
                                      reads=[xtmp, onehot], writes=[xt])
                            else:
                                pb.op("dve", lambda e, m=m: e.scalar_tensor_tensor(
                                    out=xt[:], in0=xtmp[:], scalar=onehot[:, m:m + 1], in1=xt[:], op0=ALU.mult,
                                    op1=ALU.add), reads=[xtmp, onehot, xt], writes=[xt])
                    pb.dma("pool", xg_s[t], xt[:], reads=[xt], writes=[grp_tr])
                    norm_T(sc, xt, hT, t, f"a{t}")
                    pos_ap = (ownpos_in if last else pos_in)[qi * 128:(qi + 1) * 128, :]
                    css.append(rope_tables(sc, pos_ap, f"a{t}"))
                pb.dma("pool", hT_s[:, :, :], hT[:], reads=[hT], writes=[grp_tr])

                def q_epi(dst_plain, dst_rot, h0, gain_idx):
                    def epi(t, ps):
                        outp = sc_tmp["hn_outp"]
                        outr = sc_tmp["hn_outr"]
                        headnorm_rope(sc, ps, 4, gain_idx, css[t], outr[:].rearrange("p (h d) -> p h d", d=128),
                                      outp[:].rearrange("p (h d) -> p h d", d=128) if dst_plain is not None else None,
                                      "a")
                        for srct, dst in ((outp, dst_plain), (outr, dst_rot)):
                            if dst is None:
                                continue
                            sg = stg[stg_i[0] % 2]
                            stg_i[0] += 1
                            transpose_to(srct, 4, lambda k0, n, sg=sg: sg[:, k0:k0 + n, :], [sg])
                            pb.dma("pool", dst[:, h0:h0 + 4, t * 128:(t + 1) * 128], sg[:, 0:4, :], reads=[sg],
                                   writes=[grp_tr])
                    return epi

                for cq in range(4):
                    gemm("w_in", o["qa"] + cq * 512, 512, hT, 0, KC, ntile, "tok", q_epi(qnT_s, qrT_s, cq * 4, 0))
                for cq in range(2):
                    gemm("w_in", o["qm"] + cq * 512, 512, hT, 0, KC, ntile, "tok", q_epi(None, qmT_s, cq * 4, 4))

                def ga_epi(t, ps):
                    pb.op("dve", lambda e: e.tensor_tensor(out=ga_sb[:, t, :], in0=ps[:, 0:48], in1=gbias[:],
                                                           op=ALU.add), reads=[ps, gbias], writes=[ga_sb])
                    pb.op("act", lambda e: e.activation(out=ga_sb[:, t, :], in_=ga_sb[:, t, :], func=AF.Sigmoid),
                          reads=[ga_sb], writes=[ga_sb])
                gemm("w_in", o["ga"], 48, hT, 0, KC, ntile, "tok", ga_epi)

                def u_epi(cu):
                    def epi(t, ps):
                        gelu_from_psum(ps, 512, gu[:, t, cu * 512:(cu + 1) * 512], [gu], tmp)
                    return epi

                def v_epi(cv):
                    def epi(t, ps):
                        gelu_from_psum(ps, 512, gv[:], [gv], tmp)
                        x2 = tmp[0]
                        pb.op("act", lambda e: e.activation(out=x2[:], in_=gv[:], func=AF.Square), reads=[gv],
                              writes=[x2])
                        pb.op("dve", lambda e: e.tensor_reduce(out=ss4[:], in_=x2[:].rearrange("p (h d) -> p h d", d=128),
                                                               axis=AX.X, op=ALU.add), reads=[x2], writes=[ss4])
                        rstd_from_ss(ss4, ss4, 1.0 / 128)
                        gv3 = gv[:].rearrange("p (h d) -> p h d", d=128)
                        pb.op("dve", lambda e: e.tensor_tensor(out=gv3, in0=gv3,
                                                               in1=bc(ss4[:, 0:4].unsqueeze(2), [128, 4, 128]),
                                                               op=ALU.mult), reads=[gv, ss4], writes=[gv])
                        pb.op("pool", lambda e: e.tensor_tensor(out=gvn[:, t, cv * 512:(cv + 1) * 512], in0=gv[:],
                                                                in1=sgng[:, cv * 512:(cv + 1) * 512], op=ALU.mult),
                              reads=[gv, sgng], writes=[gvn])
                    return epi

                for cu in range(2):
                    gemm("w_in", o["ub"] + cu * 512, 512, hT, 0, KC, ntile, "tok", u_epi(cu))
                for cv in range(2):
                    gemm("w_in", o["vb"] + cv * 512, 512, hT, 0, KC, ntile, "tok", v_epi(cv))
                for t in range(ntile):
                    for half in range(2):
                        ps = psn()
                        pb.op("pe", [lambda e, g=g, ps=ps: e.matmul(
                            ps[:, (g % 4) * 128:(g % 4 + 1) * 128], lhsT=sgwT[:, g, :],
                            rhs=gvn[:, t, g * 128:(g + 1) * 128], start=True, stop=True)
                            for g in range(half * 4, half * 4 + 4)], reads=[sgwT, gvn], writes=[ps])
                        for g in range(half * 4, half * 4 + 4):
                            pb.op("dve", lambda e, g=g, ps=ps: e.scalar_tensor_tensor(
                                out=ob[:, g * 128:(g + 1) * 128], in0=ps[:, (g % 4) * 128:(g % 4 + 1) * 128],
                                scalar=sgb[:, g:g + 1], in1=gu[:, t, g * 128:(g + 1) * 128], op0=ALU.add,
                                op1=ALU.mult), reads=[ps, sgb, gu], writes=[ob])
                    sg = stg[stg_i[0] % 2]
                    stg_i[0] += 1
                    transpose_to(ob, 8, lambda k0, n, sg=sg: sg[:, k0:k0 + n, :], [sg])
                    pb.dma("pool", oT_s[:, 16:24, t * 128:(t + 1) * 128], sg[:], reads=[sg], writes=[grp_tr])
        def phase_b(l, last, qtiles):
            NS = c.NSLC
            NMB = c.NMB
            NCC = c.NCC
            SB = [PS[0], PS[1]]
            MB = PS[2]
            OB = [PS[3], PS[4], PS[5], PS[6]]
            XB = PS[7]
            with Scope(pb) as sc:
                exps = sc.sb("b_exps", [128, 64, 128], BF16)
                expm = sc.sb("b_expm", [64, 64, 128], BF16)
                ov = sc.sb("b_ov", [128, NCC, 256], BF16)
                kcT = sc.sb("b_kcT", [128, NG, NCC * 128], BF16)
                vcP = sc.sb("b_vcP", [128, NCC, NG, 129], BF16)
                kmeanT = sc.sb("b_kmean", [128, MH, NMB], BF16)
                jrow = sc.sb("b_jrow", [128, 256], F32)
                pb.dma("sp", exps[:].rearrange("p a b -> p (a b)"), c_exps[:, :], writes=[exps])
                pb.dma("sp", expm[:].rearrange("p a b -> p (a b)"), c_expm[:, :], writes=[expm])
                pb.dma("sp", ov[:], c_ov.rearrange("(c p) j -> p c j", p=128), writes=[ov])
                pb.dma("sp", kcT[:], kcT_s[:, :, :], reads=[kv_tr], writes=[kcT])
                pb.op("dve", lambda e: e.memset(vcP[:], 1.0), writes=[vcP])
                for cc in range(NCC):
                    pb.dma("sp", vcP[:, cc, :, 0:128], vc_s[cc * 128:(cc + 1) * 128, :, :], reads=[kv_tr], writes=[vcP])
                pb.dma("sp", kmeanT[:], kmeanT_s[:, :, :], reads=[kv_tr], writes=[kmeanT])
                pb.dma("sp", jrow[:], c_jrow[0:1, :].partition_broadcast(128), writes=[jrow])
                qn = sc.sb("b_qn", [128, NH * 128], BF16)
                qr = sc.sb("b_qr", [128, NH * 128], BF16)
                qm = sc.sb("b_qm", [128, MH, 128], BF16)
                info = sc.sb("b_info", [128, 4], F32)
                trow = sc.sb("b_trow", [128, 128], F32)
                kbuf = [sc.sb(f"b_k{i}", [128, 4, 512], BF16) for i in range(2)]
                vbuf = [sc.sb(f"b_v{i}", [128, 4, 4, 129], BF16) for i in range(2)]
                for vb_ in vbuf:
                    pb.op("dve", lambda e, vb_=vb_: e.memset(vb_[:], 1.0), writes=[vb_])
                e_sb = [sc.sb(f"b_e{i}", [128, 512], F32) for i in range(2)]
                em = [sc.sb(f"b_em{i}", [128, 512], BF16) for i in range(2)]
                cmd = sc.sb("b_cmd", [128, 8, 128], F32)
                msk = sc.sb("b_msk", [128, 512], F32)
                wm = sc.sb("b_wm", [128, 128], F32)
                wm2 = sc.sb("b_wm2", [128, 128], F32)
                ecmp = sc.sb("b_ecmp", [128, NCC, 512], BF16)
                imp = sc.sb("b_imp", [128, 256], F32)
                dd = sc.sb("b_dd", [128, 256], F32)
                ff = sc.sb("b_ff", [128, 256], F32)
                al = sc.sb("b_al", [128, 256], F32)
                pen = sc.sb("b_pen", [128, 256], F32)
                score = sc.sb("b_score", [128, 256], F32)
                sc2 = sc.sb("b_sc2", [128, 256], F32)
                m8a = sc.sb("b_m8a", [128, 8], F32)
                m8b = sc.sb("b_m8b", [128, 8], F32)
                sel = sc.sb("b_sel", [128, 256], BF16)
                selT = sc.sb("b_selT", [128, 2, 128], BF16)
                pastm = sc.sb("b_pastm", [128, NMB], F32)
                ownm = sc.sb("b_ownm", [128, NMB], F32)
                penm = sc.sb("b_penm", [128, NMB], F32)
                msc = sc.sb("b_msc", [128, MH, NMB], F32)
                m8m = sc.sb("b_m8m", [128, MH, 8], F32)
                selm = sc.sb("b_selm", [128, MH, NMB], F32)
                selmb = sc.sb("b_selmb", [128, MH, 64], BF16)
                selTm = sc.sb("b_selTm", [64, MH, 128], BF16)
                oa = sc.sb("b_oa", [128, NH * 128], F32)
                oab = sc.sb("b_oab", [128, NH * 128], BF16)
                om = sc.sb("b_om", [128, MH * 128], BF16)
                sm = sc.sb("b_sm", [128, 4], F32)
                ostg = sc.sb("b_ostg", [128, 8, 128], BF16)
                pb.op("pool", lambda e: e.memset(sel[:], 0.0), writes=[sel])
                pb.op("pool", lambda e: e.memset(selmb[:], 0.0), writes=[selmb])
                ctr = [0]

                def finish_head(O, ga_col, dst_ap, dst_tile, first):
                    pb.op("dve", lambda e: e.tensor_scalar(out=sm[:, 0:1], in0=O[:, 128:129], scalar1=1e-30,
                                                           scalar2=None, op0=ALU.max), reads=[O], writes=[sm])
                    pb.op("dve", lambda e: e.reciprocal(out=sm[:, 1:2], in_=sm[:, 0:1]), reads=[sm], writes=[sm])
                    if ga_col is not None:
                        pb.op("dve", lambda e: e.tensor_tensor(out=sm[:, 2:3], in0=sm[:, 1:2], in1=ga_col, op=ALU.mult),
                              reads=[sm, ga_sb], writes=[sm])
                        coef = sm[:, 2:3]
                    else:
                        coef = sm[:, 1:2]
                    if first:
                        pb.op("dve", lambda e: e.tensor_scalar(out=dst_ap, in0=O[:, 0:128], scalar1=coef, scalar2=None,
                                                               op0=ALU.mult), reads=[O, sm], writes=[dst_tile])
                    else:
                        pb.op("dve", lambda e: e.scalar_tensor_tensor(out=dst_ap, in0=O[:, 0:128], scalar=coef,
                                                                      in1=dst_ap, op0=ALU.mult, op1=ALU.add),
                              reads=[O, sm, dst_tile], writes=[dst_tile])

                for t, qi in enumerate(qtiles):
                    if not last:
                        imax = imin = qi
                        ndiag = 1
                    else:
                        imax = NC * qi + NC - 1
                        imin = NC * qi
                        ndiag = NC
                    nk = imax + 1
                    w0 = max(0, imin - 4)
                    ncc = min(NCC, (8 * imax + 7 + 127) // 128)
                    inf_src = info_own if last else info_all
                    row_src = row_own if last else row_all
                    pb.dma("sp", qn[:].rearrange("p (h q) -> p h q", q=128), qnT_s[:, :, t * 128:(t + 1) * 128],
                           reads=[grp_tr], writes=[qn])
                    pb.dma("sp", qr[:].rearrange("p (h q) -> p h q", q=128), qrT_s[:, :, t * 128:(t + 1) * 128],
                           reads=[grp_tr], writes=[qr])
                    pb.dma("sp", qm[:], qmT_s[:, :, t * 128:(t + 1) * 128], reads=[grp_tr], writes=[qm])
                    pb.dma("pool", info[:], inf_src[qi * 128:(qi + 1) * 128, :], writes=[info])
                    pb.dma("pool", trow[:], row_src[qi:qi + 1, :].partition_broadcast(128), writes=[trow])
                    for i in range(ndiag):
                        kt = nk - ndiag + i
                        pb.op("pool", lambda e, i=i, kt=kt: e.tensor_scalar(
                            out=cmd[:, i, :], in0=trow[:], scalar1=float(-kt * 128), scalar2=pcol[:, 0:1], op0=ALU.add,
                            op1=ALU.is_ge), reads=[trow, pcol], writes=[cmd])

                    for g in range(NG):
                        qng = qn[:, g * 512:(g + 1) * 512]
                        qrg = qr[:, g * 512:(g + 1) * 512]
                        for cc in range(ncc):
                            Sb = SB[ctr[0] % 2]
                            eb = e_sb[ctr[0] % 2]
                            ctr[0] += 1
                            pb.op("pe", lambda e, cc=cc, Sb=Sb: e.matmul(Sb[:, 0:512], lhsT=kcT[:, g, cc * 128:(cc + 1) * 128],
                                                                       rhs=qng, start=True, stop=True),
                                  reads=[kcT, qn], writes=[Sb])
                            pb.op("act", lambda e, Sb=Sb, eb=eb: e.activation(out=eb[:], in_=Sb[:], func=AF.Exp),
                                  reads=[Sb], writes=[eb])
                            pb.op("pool", lambda e, cc=cc: e.tensor_scalar(
                                out=wm[:], in0=trow[:], scalar1=float(-(31 + 2048 * cc)), scalar2=p16col[:, 0:1],
                                op0=ALU.add, op1=ALU.is_ge), reads=[trow, p16col], writes=[wm])
                            pb.op("dve", lambda e, cc=cc, eb=eb: e.tensor_tensor(
                                out=ecmp[:, cc, :].rearrange("p (r q) -> p r q", q=128),
                                in0=eb[:].rearrange("p (r q) -> p r q", q=128),
                                in1=bc(wm[:].unsqueeze(1), [128, 4, 128]), op=ALU.mult), reads=[eb, wm], writes=[ecmp])
                        for r in range(4):
                            head = g * 4 + r
                            A = OB[r]
                            Bk = MB if r % 2 == 0 else XB
                            pb.op("pe", [lambda e, cc=cc, A=A: e.matmul(
                                A[:, 0:129], lhsT=ecmp[:, cc, r * 128:(r + 1) * 128], rhs=vcP[:, cc, g, :],
                                start=(cc == 0), stop=(cc == ncc - 1)) for cc in range(ncc)], reads=[ecmp, vcP],
                                writes=[A])
                            pb.op("pe", [lambda e, cc=cc, Bk=Bk: e.matmul(
                                Bk[:, 0:256], lhsT=ecmp[:, cc, r * 128:(r + 1) * 128], rhs=ov[:, cc, :],
                                start=(cc == 0), stop=(cc == ncc - 1)) for cc in range(ncc)], reads=[ecmp, ov],
                                writes=[Bk])
                            finish_head(A, ga_sb[:, t, head:head + 1], oa[:, head * 128:(head + 1) * 128], oa, True)
                            if r == 0:
                                pb.op("dve", lambda e, Bk=Bk: e.tensor_scalar(out=imp[:], in0=Bk[:, 0:256],
                                                                            scalar1=sm[:, 1:2], scalar2=None,
                                                                            op0=ALU.mult), reads=[Bk, sm], writes=[imp])
                            else:
                                pb.op("dve", lambda e, Bk=Bk: e.scalar_tensor_tensor(
                                    out=imp[:], in0=Bk[:, 0:256], scalar=sm[:, 1:2], in1=imp[:], op0=ALU.mult,
                                    op1=ALU.add), reads=[Bk, sm, imp], writes=[imp])
                        pb.op("dve", lambda e: e.tensor_scalar(out=dd[:, 0:NS], in0=jrow[:, 0:NS], scalar1=info[:, 1:2],
                                                               scalar2=None, op0=ALU.subtract), reads=[jrow, info],
                              writes=[dd])
                        pb.op("dve", lambda e: e.tensor_scalar(out=ff[:, 0:NS], in0=dd[:, 0:NS], scalar1=-1.0,
                                                               scalar2=None, op0=ALU.is_ge), reads=[dd], writes=[ff])
                        pb.op("dve", lambda e: e.scalar_tensor_tensor(out=ff[:, 0:NS], in0=dd[:, 0:NS], scalar=0.0,
                                                                      in1=ff[:, 0:NS], op0=ALU.is_le, op1=ALU.mult),
                              reads=[dd, ff], writes=[ff])
                        pb.op("dve", lambda e: e.memset(ff[:, 0:1], 1.0), reads=[ff], writes=[ff])
                        pb.op("dve", lambda e: e.tensor_single_scalar(out=al[:, 0:NS], in_=dd[:, 0:NS], scalar=0.0,
                                                                      op=ALU.is_le), reads=[dd], writes=[al])
                        pb.op("dve", lambda e: e.scalar_tensor_tensor(out=score[:, 0:NS], in0=ff[:, 0:NS], scalar=1e9,
                                                                      in1=imp[:, 0:NS], op0=ALU.mult, op1=ALU.add),
                              reads=[ff, imp], writes=[score])
                        pb.op("dve", lambda e: e.tensor_scalar(out=pen[:, 0:NS], in0=al[:, 0:NS], scalar1=1.0,
                                                               scalar2=1e30, op0=ALU.subtract, op1=ALU.mult),
                              reads=[al], writes=[pen])
                        pb.op("dve", lambda e: e.tensor_tensor(out=score[:, 0:NS], in0=score[:, 0:NS], in1=al[:, 0:NS],
                                                               op=ALU.mult), reads=[score, al], writes=[score])
                        pb.op("dve", lambda e: e.tensor_tensor(out=score[:, 0:NS], in0=score[:, 0:NS], in1=pen[:, 0:NS],
                                                               op=ALU.add), reads=[score, pen], writes=[score])
                        pb.op("dve", lambda e: e.max(out=m8a[:], in_=score[:, 0:NS]), reads=[score], writes=[m8a])
                        pb.op("dve", lambda e: e.match_replace(out=sc2[:, 0:NS], in_to_replace=m8a[:],
                                                               in_values=score[:, 0:NS], imm_value=-3.0e38),
                              reads=[score, m8a], writes=[sc2])
                        pb.op("dve", lambda e: e.max(out=m8b[:], in_=sc2[:, 0:NS]), reads=[sc2], writes=[m8b])
                        pb.op("dve", lambda e: e.scalar_tensor_tensor(out=sel[:, 0:NS], in0=score[:, 0:NS],
                                                                      scalar=m8b[:, 7:8], in1=al[:, 0:NS],
                                                                      op0=ALU.is_ge, op1=ALU.mult),
                              reads=[score, m8b, al], writes=[sel])
                        transpose_to(sel, 2, lambda k0, n: selT[:, k0:k0 + n, :], [selT], bank=XB)
                        for br in ("slc", "win"):
                            kT_src, v_src = (ksT, vs_s) if br == "slc" else (kwT, vw_s)
                            k_lo = 0 if br == "slc" else w0
                            for ktb in range(k_lo, nk, 4):
                                nkb = min(4, nk - ktb)
                                kb = kbuf[ctr[0] % 2]
                                vb = vbuf[ctr[0] % 2]
                                pb.dma("sp", kb[:, 0, 0:nkb * 128], kT_src[:, g, ktb * 128:(ktb + nkb) * 128],
                                       reads=[kv_tr], writes=[kb])
                                pb.dma("sp", vb[:, 0:nkb, 0, 0:128],
                                       v_src[ktb * 128:(ktb + nkb) * 128, g, :].rearrange("(k p) d -> p k d", p=128),
                                       reads=[kv_tr], writes=[vb])
                                for kk in range(nkb):
                                    kt = ktb + kk
                                    Sb = SB[ctr[0] % 2]
                                    eb = e_sb[ctr[0] % 2]
                                    emb = em[ctr[0] % 2]
                                    ctr[0] += 1
                                    pb.op("pe", lambda e, kk=kk, Sb=Sb, kb=kb: e.matmul(
                                        Sb[:, 0:512], lhsT=kb[:, 0, kk * 128:(kk + 1) * 128], rhs=qrg, start=True,
                                        stop=True), reads=[kb, qr], writes=[Sb])
                                    pb.op("act", lambda e, Sb=Sb, eb=eb: e.activation(out=eb[:], in_=Sb[:], func=AF.Exp),
                                          reads=[Sb], writes=[eb])
                                    if br == "slc":
                                        pb.op("pe", lambda e, kt=kt: e.matmul(MB[:, 0:128], lhsT=exps[:, kt % 64, :],
                                                                              rhs=selT[:, kt // 64, :], start=True,
                                                                              stop=True), reads=[exps, selT], writes=[MB])
                                        if kt >= nk - ndiag:
                                            di = kt - (nk - ndiag)
                                            pb.op("dve", lambda e, di=di: e.tensor_tensor(
                                                out=wm[:], in0=cmd[:, di, :], in1=MB[:, 0:128], op=ALU.mult),
                                                reads=[cmd, MB], writes=[wm])
                                            mask_ap, mask_t = wm[:], wm
                                        else:
                                            mask_ap, mask_t = MB[:, 0:128], MB
                                    else:
                                        pb.op("pool", lambda e, kt=kt: e.tensor_scalar(
                                            out=wm[:], in0=trow[:], scalar1=float(-kt * 128), scalar2=pcol[:, 0:1],
                                            op0=ALU.add, op1=ALU.subtract), reads=[trow, pcol], writes=[wm])
                                        pb.op("pool", lambda e: e.tensor_scalar(
                                            out=wm2[:], in0=wm[:], scalar1=0.0, scalar2=None, op0=ALU.is_ge),
                                            reads=[wm], writes=[wm2])
                                        pb.op("pool", lambda e: e.tensor_scalar(
                                            out=wm[:], in0=wm[:], scalar1=511.0, scalar2=None, op0=ALU.is_le),
                                            reads=[wm], writes=[wm])
                                        pb.op("pool", lambda e: e.tensor_tensor(
                                            out=wm[:], in0=wm[:], in1=wm2[:], op=ALU.mult), reads=[wm, wm2],
                                            writes=[wm])
                                        mask_ap, mask_t = wm[:], wm
                                    pb.op("dve", lambda e, eb=eb, emb=emb, mask_ap=mask_ap: e.tensor_tensor(
                                        out=emb[:].rearrange("p (r q) -> p r q", q=128),
                                        in0=eb[:].rearrange("p (r q) -> p r q", q=128),
                                        in1=bc(mask_ap.unsqueeze(1), [128, 4, 128]), op=ALU.mult),
                                        reads=[eb, mask_t], writes=[emb])
                                    for r in range(4):
                                        pb.op("pe", lambda e, r=r, emb=emb, vb=vb, kk=kk, kt=kt: e.matmul(
                                            OB[r][:, 0:129], lhsT=emb[:, r * 128:(r + 1) * 128], rhs=vb[:, kk, 0, :],
                                            start=(kt == k_lo), stop=(kt == nk - 1)), reads=[emb, vb], writes=[OB[r]])
                            gofs = 16 if br == "slc" else 32
                            for r in range(4):
                                head = g * 4 + r
                                finish_head(OB[r], ga_sb[:, t, gofs + head:gofs + head + 1],
                                            oa[:, head * 128:(head + 1) * 128], oa, False)

                    pb.op("pe", [lambda e, h=h: e.matmul(XB[:, h * NMB:(h + 1) * NMB], lhsT=qm[:, h, :],
                                                         rhs=kmeanT[:, h, :], start=True, stop=True)
                                 for h in range(MH)], reads=[qm, kmeanT], writes=[XB])
                    pb.op("dve", lambda e: e.tensor_scalar(out=pastm[:], in0=jrow[:, 0:NMB], scalar1=info[:, 2:3],
                                                           scalar2=None, op0=ALU.is_lt), reads=[jrow, info],
                          writes=[pastm])
                    pb.op("dve", lambda e: e.tensor_scalar(out=ownm[:], in0=jrow[:, 0:NMB], scalar1=info[:, 2:3],
                                                           scalar2=None, op0=ALU.is_equal), reads=[jrow, info],
                          writes=[ownm])
                    pb.op("dve", lambda e: e.tensor_scalar(out=penm[:], in0=pastm[:], scalar1=1.0, scalar2=1e30,
                                                           op0=ALU.subtract, op1=ALU.mult), reads=[pastm], writes=[penm])
                    pb.op("dve", lambda e: e.tensor_tensor(out=msc[:], in0=XB[:, 0:MH * NMB].rearrange(
                        "p (h b) -> p h b", b=NMB), in1=bc(pastm[:].unsqueeze(1), [128, MH, NMB]), op=ALU.mult),
                        reads=[XB, pastm], writes=[msc])
                    pb.op("dve", lambda e: e.tensor_tensor(out=msc[:], in0=msc[:],
                                                           in1=bc(penm[:].unsqueeze(1), [128, MH, NMB]), op=ALU.add),
                          reads=[msc, penm], writes=[msc])
                    for h in range(MH):
                        pb.op("dve", lambda e, h=h: e.max(out=m8m[:, h, :], in_=msc[:, h, :]), reads=[msc],
                              writes=[m8m])
                    for h in range(MH):
                        pb.op("dve", lambda e, h=h: e.scalar_tensor_tensor(
                            out=selm[:, h, :], in0=msc[:, h, :], scalar=m8m[:, h, 2:3], in1=pastm[:], op0=ALU.is_ge,
                            op1=ALU.mult), reads=[msc, m8m, pastm], writes=[selm])
                    pb.op("dve", lambda e: e.tensor_tensor(out=selmb[:, :, 0:NMB], in0=selm[:],
                                                           in1=bc(ownm[:].unsqueeze(1), [128, MH, NMB]), op=ALU.add),
                          reads=[selm, ownm], writes=[selmb])
                    xbb = XB[:].bitcast(BF16)
                    pb.op("pe", [lambda e, h=h: e.transpose(out=xbb[0:64, h * 128:(h + 1) * 128], in_=selmb[:, h, :],
                                                            identity=identb[:]) for h in range(MH)],
                          reads=[selmb, identb], writes=[XB])
                    pb.op("act", lambda e: e.copy(out=selTm[:].rearrange("p h q -> p (h q)"), in_=xbb[0:64, 0:MH * 128]),
                          reads=[XB], writes=[selTm])
                    for hb in range(MH // 4):
                        for ktb in range(0, nk, 4):
                            nkb = min(4, nk - ktb)
                            kb = kbuf[ctr[0] % 2]
                            vb = vbuf[ctr[0] % 2]
                            pb.dma("sp", kb[:, :, 0:nkb * 128], kmT[:, hb * 4:(hb + 1) * 4, ktb * 128:(ktb + nkb) * 128],
                                   reads=[kv_tr], writes=[kb])
                            for kk in range(nkb):
                                pb.dma("sp", vb[:, kk, :, 0:128],
                                       vm_s[(ktb + kk) * 128:(ktb + kk + 1) * 128, hb * 4:(hb + 1) * 4, :],
                                       reads=[kv_tr], writes=[vb])
                            for kk in range(nkb):
                                kt = ktb + kk
                                Sb = SB[ctr[0] % 2]
                                eb = e_sb[ctr[0] % 2]
                                emb = em[ctr[0] % 2]
                                ctr[0] += 1
                                pb.op("pe", [lambda e, h=h, kk=kk, Sb=Sb, kb=kb: e.matmul(
                                    Sb[:, h * 128:(h + 1) * 128], lhsT=kb[:, h, kk * 128:(kk + 1) * 128],
                                    rhs=qm[:, hb * 4 + h, :], start=True, stop=True) for h in range(4)],
                                    reads=[kb, qm], writes=[Sb])
                                pb.op("pe", [lambda e, h=h, kt=kt: e.matmul(
                                    MB[:, h * 128:(h + 1) * 128], lhsT=expm[:, kt // 2, :], rhs=selTm[:, hb * 4 + h, :],
                                    start=True, stop=True) for h in range(4)], reads=[expm, selTm], writes=[MB])
                                pb.op("act", lambda e, Sb=Sb, eb=eb: e.activation(out=eb[:], in_=Sb[:], func=AF.Exp),
                                      reads=[Sb], writes=[eb])
                                if kt >= nk - ndiag:
                                    di = kt - (nk - ndiag)
                                    pb.op("dve", lambda e, di=di: e.tensor_tensor(
                                        out=msk[:].rearrange("p (r q) -> p r q", q=128),
                                        in0=MB[:].rearrange("p (r q) -> p r q", q=128),
                                        in1=bc(cmd[:, di:di + 1, :], [128, 4, 128]), op=ALU.mult), reads=[MB, cmd],
                                        writes=[msk])
                                    mask_ap, mask_t = msk[:], msk
                                else:
                                    mask_ap, mask_t = MB[:], MB
                                pb.op("dve", lambda e, eb=eb, emb=emb, mask_ap=mask_ap: e.tensor_tensor(
                                    out=emb[:], in0=eb[:], in1=mask_ap, op=ALU.mult), reads=[eb, mask_t], writes=[emb])
                                for h in range(4):
                                    pb.op("pe", lambda e, h=h, emb=emb, vb=vb, kk=kk, kt=kt: e.matmul(
                                        OB[h][:, 0:129], lhsT=emb[:, h * 128:(h + 1) * 128], rhs=vb[:, kk, h, :],
                                        start=(kt == 0), stop=(kt == nk - 1)), reads=[emb, vb], writes=[OB[h]])
                        for h in range(4):
                            hh = hb * 4 + h
                            finish_head(OB[h], None, om[:, hh * 128:(hh + 1) * 128], om, True)
                    pb.op("act", lambda e: e.copy(out=oab[:], in_=oa[:]), reads=[oa], writes=[oab])
                    for half in range(2):
                        transpose_to(oab, 8, lambda k0, n: ostg[:, k0:k0 + n, :], [ostg],
                                     src_fn=lambda k, half=half: oab[:, (half * 8 + k) * 128:(half * 8 + k + 1) * 128],
                                     bank=XB)
                        pb.dma("pool", oT_s[:, half * 8:(half + 1) * 8, t * 128:(t + 1) * 128], ostg[:], reads=[ostg],
                               writes=[grp_tr])
                    transpose_to(om, 8, lambda k0, n: ostg[:, k0:k0 + n, :], [ostg], bank=XB)
                    pb.dma("pool", oT_s[:, 24:32, t * 128:(t + 1) * 128], ostg[:], reads=[ostg], writes=[grp_tr])

        def phase_c1(l, ntile):
            o = c.off
            NTK = ntile * 128
            with Scope(pb) as sc:
                alloc_wbufs(sc)
                hT = sc.sb("c_hT", [128, KC, TOK], BF16)
                oT = sc.sb("c_oT", [128, KC, TOK], BF16)
                sig = [sc.sb(f"c_sig{j}", [128, TOK], F32) for j in range(4)]
                yacc = [sc.sb(f"c_yacc{j}", [128, TOK], F32) for j in range(4)]
                tmp = sc.sb("c_tmp", [128, TOK], F32)
                ystg = sc.sb("c_ystg", [128, 4, TOK], BF16)
                pb.dma("sp", hT[:], hT_s[:, :, :], reads=[grp_tr], writes=[hT])
                pb.dma("sp", oT[:], oT_s[:, :, :], reads=[grp_tr], writes=[oT])
                for cc4 in range(D // 512):
                    col0 = cc4 * 512
                    for bi, (gname, pname, kofs, nkp) in enumerate((("gma", "proj_a", 0, 16), ("gmb", "proj_b", 16, 8),
                                                                    ("gmc", "proj_c", 24, 8))):
                        def epi_gate(j, ps):
                            pb.op("act", lambda e: e.activation(out=sig[j][:, 0:NTK], in_=ps[:, 0:NTK], func=AF.Sigmoid),
                                  reads=[ps], writes=[sig[j]])

                        def epi_proj(j, ps, bi=bi):
                            if bi == 0:
                                pb.op("dve", lambda e: e.tensor_tensor(out=yacc[j][:, 0:NTK], in0=sig[j][:, 0:NTK],
                                                                       in1=ps[:, 0:NTK], op=ALU.mult),
                                      reads=[sig[j], ps], writes=[yacc[j]])
                                return
                            pb.op("dve", lambda e: e.tensor_tensor(out=tmp[:, 0:NTK], in0=sig[j][:, 0:NTK],
                                                                   in1=ps[:, 0:NTK], op=ALU.mult), reads=[sig[j], ps],
                                  writes=[tmp])
                            if bi == 1:
                                pb.op("pool", lambda e: e.tensor_tensor(out=yacc[j][:, 0:NTK], in0=yacc[j][:, 0:NTK],
                                                                        in1=tmp[:, 0:NTK], op=ALU.add),
                                      reads=[yacc[j], tmp], writes=[yacc[j]])
                            else:
                                pb.op("pool", lambda e: e.tensor_tensor(out=ystg[:, j, 0:NTK], in0=yacc[j][:, 0:NTK],
                                                                        in1=tmp[:, 0:NTK], op=ALU.add),
                                      reads=[yacc[j], tmp], writes=[ystg])
                        gemm("w_in", o[gname] + col0, 512, hT, 0, KC, ntile, "feat", epi_gate)
                        gemm(pname, col0, 512, oT, kofs, nkp, ntile, "feat", epi_proj)
                    pb.dma("pool", yT_s[:, cc4 * 4:(cc4 + 1) * 4, :], ystg[:], reads=[ystg], writes=[grp_tr])

        def phase_c2d(l, last, qtiles):
            ntile = len(qtiles)
            NTK = ntile * 128
            with Scope(pb) as scx:
                xacc = scx.sb("x_acc", [128, TG, D], F32)
                for t in range(ntile):
                    pb.dma("sp", xacc[:, t, :], xg_s[t], reads=[grp_tr], writes=[xacc])
                with Scope(pb) as sc:
                    alloc_wbufs(sc)
                    yT = sc.sb("c2_yT", [128, KC, TOK], BF16)
                    pb.dma("sp", yT[:], yT_s[:, :, :], reads=[grp_tr], writes=[yT])
                    for oc in range(D // 512):
                        def epi(t, ps, oc=oc):
                            pb.op("dve", lambda e: e.tensor_tensor(out=xacc[:, t, oc * 512:(oc + 1) * 512],
                                                                   in0=xacc[:, t, oc * 512:(oc + 1) * 512], in1=ps[:],
                                                                   op=ALU.add), reads=[xacc, ps], writes=[xacc])
                        gemm("w_out", oc * 512, 512, yT, 0, KC, ntile, "tok", epi)
                h2T = scx.sb("d_h2T", [128, KC, TOK], BF16)
                with Scope(pb) as sc:
                    sc_tmp["xb"] = sc.sb("t_xb", [128, D], BF16)
                    sc_tmp["sq"] = sc_tmp["xb"]
                    for t in range(ntile):
                        norm_T(sc, xacc, h2T, t, f"d{t}", xap=xacc[:, t, :])
                with Scope(pb) as sc:
                    alloc_wbufs(sc)
                    hid = sc.sb("d_hid", [128, 16, TOK], BF16)
                    rl = [sc.sb(f"d_rl{i}", [128, TOK], F32) for i in range(2)]
                    rli = [0]
                    for part in range(DFF // 2048):
                        for c4 in range(4):
                            def epi1(j, ps, c4=c4):
                                r_ = rl[rli[0] % 2]
                                rli[0] += 1
                                pb.op("act", lambda e: e.activation(out=r_[:, 0:NTK], in_=ps[:, 0:NTK], func=AF.Relu),
                                      reads=[ps], writes=[r_])
                                pb.op("pool", lambda e: e.tensor_tensor(out=hid[:, c4 * 4 + j, 0:NTK], in0=r_[:, 0:NTK],
                                                                        in1=r_[:, 0:NTK], op=ALU.mult), reads=[r_],
                                      writes=[hid])
                            gemm("mlp_w1", part * 2048 + c4 * 512, 512, h2T, 0, KC, ntile, "feat", epi1)
                        for oc in range(D // 512):
                            def epi2(t, ps, oc=oc):
                                pb.op("dve", lambda e: e.tensor_tensor(out=xacc[:, t, oc * 512:(oc + 1) * 512],
                                                                       in0=xacc[:, t, oc * 512:(oc + 1) * 512],
                                                                       in1=ps[:], op=ALU.add), reads=[xacc, ps],
                                      writes=[xacc])
                            gemm("mlp_w2", oc * 512, 512, hid, 0, 16, ntile, "tok", epi2, wrow0=part * 16)
                for t, qi in enumerate(qtiles):
                    if last:
                        pb.dma("pool", y_out[qi * 128:(qi + 1) * 128, :], xacc[:, t, :], reads=[xacc])
                    else:
                        pb.dma("pool", xs[l + 1][qi * 128:(qi + 1) * 128, :], xacc[:, t, :], reads=[xacc],
                               writes=[xs_tr[l + 1]])
                if "xmid" in dbg and False:
                    pass

        for l in range(L):
            last = (l == L - 1)
            with Scope(pb) as sc:
                precast(sc, l)
            with Scope(pb) as sc:
                load_layer_params(sc, l)
            kv_pass(l)
            compress(l)
            kmean(l)
            ntl = NTO if last else NT
            assert ntl % TG == 0
            for g0 in range(0, ntl, TG):
                qtiles = list(range(g0, g0 + TG))
                phase_a(l, last, qtiles)
                phase_b(l, last, qtiles)
                phase_c1(l, TG)
                phase_c2d(l, last, qtiles)
        pb.barrier(engines=("sp",))
        print("instructions:", pb.nins)
    return nc


def host_constants(cfg):
    c = cfg
    bf = ml_dtypes.bfloat16
    ident = np.eye(128, dtype=np.float32)
    tril = np.tril(np.ones((128, 128), np.float32))
    n = np.arange(c.NCC * 128)
    j = np.arange(256)
    ov = ((n[:, None] * 16 <= j[None, :] * 64 + 63) & (n[:, None] * 16 + 31 >= j[None, :] * 64)).astype(np.float32)
    ov[c.NCMP:, :] = 0
    exps = np.zeros((128, 64, 128), np.float32)
    for m in range(64):
        for k in range(128):
            exps[2 * m + k // 64, m, k] = 1.0
    expm = np.zeros((64, 64, 128), np.float32)
    for m in range(64):
        expm[m, m, :] = 1.0
    invf = (10000.0 ** (-np.arange(0, 128, 2, dtype=np.float32) / 128)).astype(np.float32)[None, :]
    jrow = np.arange(256, dtype=np.float32)[None, :]
    return {"c_identb": ident.astype(bf), "c_identf": ident, "c_tril": tril, "c_ov": ov.astype(bf),
            "c_exps": exps.reshape(128, -1).astype(bf), "c_expm": expm.reshape(64, -1).astype(bf), "c_invf": invf,
            "c_jrow": jrow}


def host_inputs(cfg, inputs):
    c = cfg
    consts = host_constants(c)
    x = np.ascontiguousarray(np.asarray(inputs["x"], np.float32).reshape(c.S, c.D))
    pos = np.ascontiguousarray(np.asarray(inputs["positions"], np.int32).reshape(c.S, 1))
    idx_all = np.arange(c.S)
    info_all = np.stack([idx_all, idx_all // 64, idx_all // 256, np.zeros_like(idx_all)], 1).astype(np.float32)
    row_all = idx_all.reshape(c.NT, 128).astype(np.float32)
    maps = []
    for core in range(c.NC):
        tiles = [c.NC * j + core for j in range(c.NTO)]
        own = np.concatenate([np.arange(t * 128, (t + 1) * 128) for t in tiles])
        m = dict(consts)
        m["x"] = x
        m["pos"] = pos
        m["ownpos"] = np.ascontiguousarray(pos[own])
        oh = np.zeros((128, c.NC), np.float32)
        oh[:, core] = 1.0
        m["onehot"] = oh
        m["info_all"] = info_all
        m["info_own"] = np.ascontiguousarray(info_all[own])
        m["row_all"] = row_all
        m["row_own"] = np.ascontiguousarray(own.reshape(c.NTO, 128).astype(np.float32))
        for k, v in inputs.items():
            if k in ("x", "positions"):
                continue
            m[k] = np.ascontiguousarray(np.asarray(v, np.float32))
        maps.append(m)
    return maps


def run(cfg, inputs, debug=(), trace=False):
    nc = build(cfg, debug)
    maps = host_inputs(cfg, inputs)
    res = run_bass_kernel_spmd(nc, maps, core_ids=list(range(cfg.NC)), trace=trace)
    return res


N_CORES_USED = 8


def kernel(**inputs):
    cfg = Cfg(L=1, NC=N_CORES_USED)
    depth = int(np.asarray(inputs["w_in"]).shape[0])
    nc = build(cfg)
    x = np.ascontiguousarray(np.asarray(inputs["x"], np.float32).reshape(cfg.S, cfg.D))
    for l in range(depth):
        inp = {}
        for k, v in inputs.items():
            if k == "x":
                inp[k] = x
            elif k == "positions":
                inp[k] = v
            else:
                inp[k] = np.asarray(v)[l:l + 1]
        maps = host_inputs(cfg, inp)
        res = run_bass_kernel_spmd(nc, maps, core_ids=list(range(cfg.NC)))
        out = np.empty((cfg.S, cfg.D), np.float32)
        for core in range(cfg.NC):
            y = res.results[core]["y"]
            for j in range(cfg.NTO):
                t = cfg.NC * j + core
                out[t * 128:(t + 1) * 128] = y[j * 128:(j + 1) * 128]
        x = out
    return x.reshape(1, cfg.S, cfg.D)
```

```python
import math
import numpy as np
import ml_dtypes
from contextlib import ExitStack
import concourse.bass as bass
import concourse.mybir as mybir
from concourse.bass_utils import run_bass_kernel_spmd

F32 = mybir.dt.float32
BF16 = mybir.dt.bfloat16
I32 = mybir.dt.int32
AF = mybir.ActivationFunctionType
ALU = mybir.AluOpType
AX = mybir.AxisListType
EPS = 1e-6
NEGBIG = -1e30
TWO_PI = 2.0 * math.pi


class Cfg:
    def __init__(s, S=16384, L=2, DFF=16384, NC=8):
        s.D = 4096
        s.S, s.L, s.DFF, s.NC = S, L, DFF, NC
        s.HD = 128
        s.NH, s.NG, s.MH, s.SW = 16, 4, 8, 1024
        s.KC = s.D // 128
        s.NT = S // 128
        s.NTO = s.NT // NC
        s.NCMP = (S - 32) // 16 + 1
        s.NCC = (s.NCMP + 127) // 128
        s.NSLC = S // 64
        s.NMB = S // 256
        s.TG = 4
        o = {}
        off = 0
        for name, w in (("qa", 2048), ("kc", 512), ("vc", 512), ("ks", 512), ("vs", 512), ("kw", 512),
                        ("vw", 512), ("ga", 48), ("ub", 1024), ("vb", 1024), ("qm", 1024), ("km", 1024),
                        ("vm", 1024), ("gma", 4096), ("gmb", 4096), ("gmc", 4096)):
            o[name] = off
            off += w
        s.off = o
        s.INW = off
        assert s.INW == 22576
        assert s.NT % (NC * 1) == 0 and s.NMB >= 8 and s.NSLC >= 16


class Track:
    __slots__ = ("w", "r")

    def __init__(s):
        s.w = None
        s.r = {}


class Tl:
    def __init__(s, h):
        s.h = h
        s.tr = Track()

    def __getitem__(s, i):
        return s.h[i]


NDS = 8
NWBUF = 3
NPC = 5
SAME_ENGINE_SYNC = True


class PB:
    def __init__(s, nc, es):
        s.nc, s.es = nc, es
        s.E = {"pe": nc.tensor, "act": nc.scalar, "dve": nc.vector, "pool": nc.gpsimd, "sp": nc.sync}
        s.sems = []
        s.cidx, s.ccnt = {}, {}
        for e in ("pe", "act", "dve", "pool"):
            s.cidx[e] = len(s.sems)
            s.sems.append(es.enter_context(nc.semaphore("cs_" + e)))
            s.ccnt[e] = 0
        s.dq = {}
        for q in ("sp", "pool", "act"):
            idx = []
            for i in range(NDS):
                idx.append(len(s.sems))
                s.sems.append(es.enter_context(nc.semaphore(f"ds_{q}{i}")))
            s.dq[q] = {"idx": idx, "val": [0] * NDS, "nxt": 0}
        s.seen = {}
        s.nins = 0

    def wait(s, e, tok):
        k = (e, tok[0])
        if e == "pe" and tok[0] == s.cidx["pe"]:
            return
        if not SAME_ENGINE_SYNC and e in s.cidx and tok[0] == s.cidx[e]:
            return
        if s.seen.get(k, 0) >= tok[1]:
            return
        s.E[e].wait_ge(s.sems[tok[0]], tok[1])
        s.seen[k] = tok[1]

    def _deps(s, e, reads, writes):
        for t in reads:
            if t.w is not None:
                s.wait(e, t.w)
        for t in writes:
            if t.w is not None:
                s.wait(e, t.w)
            for tok in t.r.values():
                s.wait(e, tok)

    def _mark(s, tok, reads, writes):
        for t in reads:
            t.r[tok[0]] = tok
        for t in writes:
            t.w = tok
            t.r = {}

    @staticmethod
    def _tr(lst):
        return [x.tr if isinstance(x, Tl) else x for x in lst]

    def op(s, e, fns, reads=(), writes=()):
        reads, writes = s._tr(reads), s._tr(writes)
        s._deps(e, reads, writes)
        if not isinstance(fns, (list, tuple)):
            fns = [fns]
        ins = None
        for f in fns:
            ins = f(s.E[e])
            s.nins += 1
        s.ccnt[e] += 1
        tok = (s.cidx[e], s.ccnt[e])
        ins.then_inc(s.sems[tok[0]], 1)
        s._mark(tok, reads, writes)
        return tok

    def dma(s, q, out, in_, reads=(), writes=(), slow=False):
        reads, writes = s._tr(reads), s._tr(writes)
        d = s.dq[q]
        i = d["nxt"]
        d["nxt"] = (i + 1) % NDS
        si = d["idx"][i]
        if d["val"][i] > 0:
            s.wait(q, (si, d["val"][i]))
        s._deps(q, reads, writes)
        ins = s.E[q].dma_start(out=out, in_=in_, allow_slow_non_contiguous=True) if slow else s.E[q].dma_start(out=out, in_=in_)
        s.nins += 1
        d["val"][i] += 16
        tok = (si, d["val"][i])
        ins.then_inc(s.sems[si], 16)
        s._mark(tok, reads, writes)
        return tok

    def all_tokens(s):
        toks = [(s.cidx[e], s.ccnt[e]) for e in s.cidx if s.ccnt[e] > 0]
        for q in s.dq.values():
            for si, v in zip(q["idx"], q["val"]):
                if v > 0:
                    toks.append((si, v))
        return toks

    def barrier(s, engines=("pe", "act", "dve", "pool", "sp")):
        toks = s.all_tokens()
        for e in engines:
            for t in toks:
                s.wait(e, t)


class Scope:
    uid = 0

    def __init__(s, pb):
        s.pb = pb
        s.es = ExitStack()

    def __enter__(s):
        s.es.__enter__()
        return s

    def __exit__(s, *a):
        s.pb.barrier()
        return s.es.__exit__(*a)

    def sb(s, name, shape, dt):
        Scope.uid += 1
        return Tl(s.es.enter_context(s.pb.nc.sbuf_tensor(f"s{Scope.uid}_" + name, list(shape), dt)))


def bc(ap, shape):
    return ap.broadcast_to(list(shape))


def build(cfg, debug=()):
    c = cfg
    D, S, L, DFF, NC, KC, TG = c.D, c.S, c.L, c.DFF, c.NC, c.KC, c.TG
    NT, NTO, NG, NH, MH = c.NT, c.NTO, c.NG, c.NH, c.MH
    TOK = TG * 128
    nc = bass.Bass("TRN2", target_bir_lowering=False)

    def din(name, shape, dt=F32):
        return nc.dram_tensor(name, list(shape), dt, kind="ExternalInput").ap()

    def dscr(name, shape, dt):
        return nc.dram_tensor(name, list(shape), dt, kind="Internal").ap()

    x_in = din("x", [S, D])
    pos_in = din("pos", [S, 1], I32)
    ownpos_in = din("ownpos", [NTO * 128, 1], I32)
    onehot_in = din("onehot", [128, NC])
    info_all = din("info_all", [NT * 128, 4])
    info_own = din("info_own", [NTO * 128, 4])
    row_all = din("row_all", [NT, 128])
    row_own = din("row_own", [NTO, 128])
    W = {}
    for nm, shp in (("norm_mix", [L, D]), ("norm_mlp", [L, D]), ("w_in", [L, D, c.INW]), ("nsa_gate_b", [L, 48]),
                    ("nsa_q_norm", [L, 128]), ("nsa_kc_norm", [L, 128]), ("nsa_ks_norm", [L, 128]),
                    ("nsa_kw_norm", [L, 128]), ("phi_pe_k", [L, 32, 128]), ("phi_w1_k", [L, 4096, 512]),
                    ("phi_w2_k", [L, 512, 128]), ("phi_pe_v", [L, 32, 128]), ("phi_w1_v", [L, 4096, 512]),
                    ("phi_w2_v", [L, 512, 128]), ("sgu_norm", [L, 1024]), ("sgu_w", [L, 8, 128, 128]),
                    ("sgu_b", [L, 8, 128]), ("moba_q_norm", [L, 128]), ("moba_k_norm", [L, 128]),
                    ("proj_a", [L, 2048, D]), ("proj_b", [L, 1024, D]), ("proj_c", [L, 1024, D]),
                    ("w_out", [L, D, D]), ("mlp_w1", [L, D, DFF]), ("mlp_w2", [L, DFF, D])):
        W[nm] = din(nm, shp)
    c_identb = din("c_identb", [128, 128], BF16)
    c_identf = din("c_identf", [128, 128])
    c_tril = din("c_tril", [128, 128])
    c_ov = din("c_ov", [c.NCC * 128, 256], BF16)
    c_exps = din("c_exps", [128, 64 * 128], BF16)
    c_expm = din("c_expm", [64, 64 * 128], BF16)
    c_invf = din("c_invf", [1, 64])
    c_jrow = din("c_jrow", [1, 256])
    y_out = nc.dram_tensor("y", [NTO * 128, D], F32, kind="ExternalOutput").ap()
    dbg = {}
    for nm, shp, dt in debug:
        dbg[nm] = nc.dram_tensor("dbg_" + nm, list(shp), dt, kind="ExternalOutput").ap()

    wb = {"w_in": dscr("wb_w_in", [D, c.INW], BF16), "phi_w1_k": dscr("wb_p1k", [4096, 512], BF16),
          "phi_w1_v": dscr("wb_p1v", [4096, 512], BF16), "phi_w2_k": dscr("wb_p2k", [512, 128], BF16),
          "phi_w2_v": dscr("wb_p2v", [512, 128], BF16), "proj_a": dscr("wb_pa", [2048, D], BF16),
          "proj_b": dscr("wb_pb", [1024, D], BF16), "proj_c": dscr("wb_pc", [1024, D], BF16),
          "w_out": dscr("wb_wo", [D, D], BF16), "mlp_w1": dscr("wb_w1", [D, DFF], BF16),
          "mlp_w2": dscr("wb_w2", [DFF, D], BF16)}
    wb_tr = {k: Track() for k in wb}
    xs = [None] + [dscr(f"xs{l}", [S, D], F32) for l in range(1, L)]
    xs_tr = [None] + [Track() for _ in range(1, L)]
    kcrawT = dscr("kcrawT", [128, NG, S], BF16)
    vcrawT = dscr("vcrawT", [128, NG, S], BF16)
    ksT = dscr("ksT", [128, NG, S], BF16)
    kwT = dscr("kwT", [128, NG, S], BF16)
    kmT = dscr("kmT", [128, MH, S], BF16)
    vs_s = dscr("vs_s", [S, NG, 128], BF16)
    vw_s = dscr("vw_s", [S, NG, 128], BF16)
    vm_s = dscr("vm_s", [S, MH, 128], BF16)
    kcT_s = dscr("kcT_s", [128, NG, c.NCC * 128], BF16)
    vc_s = dscr("vc_s", [c.NCC * 128, NG, 128], BF16)
    kmeanT_s = dscr("kmeanT_s", [128, MH, c.NMB], BF16)
    kv_tr = Track()
    hT_s = dscr("hT_s", [128, KC, TOK], BF16)
    qnT_s = dscr("qnT_s", [128, NH, TOK], BF16)
    qrT_s = dscr("qrT_s", [128, NH, TOK], BF16)
    qmT_s = dscr("qmT_s", [128, MH, TOK], BF16)
    oT_s = dscr("oT_s", [128, KC, TOK], BF16)
    xg_s = dscr("xg_s", [TG, 128, D], F32)
    grp_tr = Track()

    es = ExitStack()
    with es:
        pb = PB(nc, es)

        def gsb(name, shape, dt):
            return Tl(es.enter_context(nc.sbuf_tensor("g_" + name, list(shape), dt)))

        PS = [Tl(es.enter_context(nc.psum_tensor(f"ps{i}", [128, 512], F32))) for i in range(8)]
        ps_rr = [0]

        def psn():
            t = PS[ps_rr[0] % 8]
            ps_rr[0] += 1
            return t

        identb = gsb("identb", [128, 128], BF16)
        identf = gsb("identf", [128, 128], F32)
        onesf = gsb("onesf", [128, 128], F32)
        invf = gsb("invf", [128, 64], F32)
        pcol = gsb("pcol", [128, 1], F32)
        onehot = gsb("onehot", [128, NC], F32)
        gains = gsb("gains", [128, 6, 128], F32)
        kcg_col = gsb("kcg_col", [128, 1], F32)
        gbias = gsb("gbias", [128, 48], F32)
        sgng = gsb("sgng", [128, 1024], F32)
        sgwT = gsb("sgwT", [128, 8, 128], BF16)
        sgb = gsb("sgb", [128, 8], F32)
        pek = gsb("pek", [128, 32], F32)
        pev = gsb("pev", [128, 32], F32)
        gmix = gsb("gmix", [128, KC], F32)
        gmlp = gsb("gmlp", [128, KC], F32)
        pb.dma("sp", identb[:], c_identb[:, :], writes=[identb])
        pb.dma("sp", identf[:], c_identf[:, :], writes=[identf])
        pb.dma("sp", invf[:], c_invf[0:1, :].partition_broadcast(128), writes=[invf])
        pb.dma("sp", onehot[:], onehot_in[:, :], writes=[onehot])
        pb.op("dve", lambda e: e.memset(onesf[:], 1.0), writes=[onesf])
        pb.op("pool", lambda e: e.iota(pcol[:], [[0, 1]], base=0, channel_multiplier=1,
                                       allow_small_or_imprecise_dtypes=True), writes=[pcol])

        def rope_tables(sc, pos_ap, tag):
            pi_ = sc.sb("rp_i" + tag, [128, 1], I32)
            pf = sc.sb("rp_f" + tag, [128, 1], F32)
            ang = sc.sb("rp_a" + tag, [128, 2, 64], F32)
            kf = sc.sb("rp_k" + tag, [128, 2, 64], F32)
            ki = sc.sb("rp_ki" + tag, [128, 2, 64], I32)
            cs = sc.sb("rp_cs" + tag, [128, 2, 64], F32)
            pb.dma("pool", pi_[:], pos_ap, writes=[pi_])
            pb.op("dve", lambda e: e.tensor_copy(out=pf[:], in_=pi_[:]), reads=[pi_], writes=[pf])
            pb.op("dve", lambda e: e.tensor_scalar(out=ang[:, 0, :], in0=invf[:], scalar1=pf[:, 0:1], scalar2=None,
                                                   op0=ALU.mult), reads=[invf, pf], writes=[ang])
            pb.op("dve", lambda e: e.tensor_scalar(out=ang[:, 1, :], in0=ang[:, 0, :], scalar1=math.pi / 2,
                                                   scalar2=None, op0=ALU.add), reads=[ang], writes=[ang])
            pb.op("dve", lambda e: e.tensor_scalar(out=kf[:], in0=ang[:], scalar1=1.0 / TWO_PI, scalar2=None,
                                                   op0=ALU.mult), reads=[ang], writes=[kf])
            pb.op("dve", lambda e: e.tensor_copy(out=ki[:], in_=kf[:]), reads=[kf], writes=[ki])
            pb.op("dve", lambda e: e.tensor_copy(out=kf[:], in_=ki[:]), reads=[ki], writes=[kf])
            C1 = 6.28125
            C2 = TWO_PI - C1
            pb.op("dve", lambda e: e.scalar_tensor_tensor(out=ang[:], in0=kf[:], scalar=-C1, in1=ang[:],
                                                          op0=ALU.mult, op1=ALU.add), reads=[kf, ang], writes=[ang])
            pb.op("dve", lambda e: e.scalar_tensor_tensor(out=ang[:], in0=kf[:], scalar=-C2, in1=ang[:],
                                                          op0=ALU.mult, op1=ALU.add), reads=[kf, ang], writes=[ang])
            pb.op("dve", lambda e: e.tensor_scalar(out=kf[:], in0=ang[:], scalar1=0.0, scalar2=TWO_PI,
                                                   op0=ALU.is_lt, op1=ALU.mult), reads=[ang], writes=[kf])
            pb.op("dve", lambda e: e.tensor_tensor(out=ang[:], in0=ang[:], in1=kf[:], op=ALU.add),
                  reads=[ang, kf], writes=[ang])
            pb.op("dve", lambda e: e.tensor_scalar(out=kf[:], in0=ang[:], scalar1=TWO_PI, scalar2=-TWO_PI,
                                                   op0=ALU.is_ge, op1=ALU.mult), reads=[ang], writes=[kf])
            pb.op("dve", lambda e: e.tensor_tensor(out=ang[:], in0=ang[:], in1=kf[:], op=ALU.add),
                  reads=[ang, kf], writes=[ang])
            pb.op("dve", lambda e: e.tensor_scalar(out=ang[:], in0=ang[:], scalar1=-1.0, scalar2=math.pi,
                                                   op0=ALU.mult, op1=ALU.add), reads=[ang], writes=[ang])
            pb.op("act", lambda e: e.activation(out=cs[:], in_=ang[:], func=AF.Sin), reads=[ang], writes=[cs])
            return cs

        def norm_T(sc, xt, hT, t, tag, xap=None):
            sq = sc_tmp["sq"]
            ss = sc.sb("nt_ss" + tag, [128, 1], F32)
            xb = sc_tmp["xb"]
            xap = xt[:] if xap is None else xap
            pb.op("act", lambda e: e.activation(out=sq[:], in_=xap, func=AF.Square), reads=[xt], writes=[sq])
            pb.op("dve", lambda e: e.tensor_reduce(out=ss[:], in_=sq[:], axis=AX.X, op=ALU.add), reads=[sq],
                  writes=[ss])
            rstd_from_ss(ss, ss, 1.0 / D)
            pb.op("act", lambda e: e.activation(out=xb[:], in_=xap, func=AF.Copy, scale=ss[:, 0:1]),
                  reads=[xt, ss], writes=[xb])
            transpose_to(xb, KC, lambda k0, n: hT[:, k0:k0 + n, t * 128:(t + 1) * 128], [hT])

        def rstd_from_ss(out, ss, inv_n):
            pb.op("dve", lambda e: e.tensor_scalar(out=out[:], in0=ss[:], scalar1=inv_n, scalar2=EPS, op0=ALU.mult,
                                                   op1=ALU.add), reads=[ss], writes=[out])
            pb.op("act", lambda e: e.activation(out=out[:], in_=out[:], func=AF.Sqrt), reads=[out], writes=[out])
            pb.op("dve", lambda e: e.reciprocal(out=out[:], in_=out[:]), reads=[out], writes=[out])

        def transpose_to(src, nblk, dst_fn, dst_tiles, src_fn=None, bank=None):
            k0 = 0
            i = 0
            while k0 < nblk:
                n = min(8, nblk - k0)
                pt = psn() if bank is None else bank
                ptb = pt[:].bitcast(BF16)
                fns = []
                for j in range(n):
                    sap = src_fn(k0 + j) if src_fn else src[:, (k0 + j) * 128:(k0 + j + 1) * 128]
                    fns.append(lambda e, j=j, sap=sap: e.transpose(out=ptb[:, j * 128:(j + 1) * 128], in_=sap,
                                                                   identity=identb[:]))
                pb.op("pe", fns, reads=[src, identb], writes=[pt])
                dst = dst_fn(k0, n)
                srcv = ptb[:, 0:n * 128].rearrange("p (n f) -> p n f", f=128)
                eng = "act" if (i % 2 == 0) else "dve"
                if eng == "act":
                    pb.op("act", lambda e: e.copy(out=dst, in_=srcv), reads=[pt], writes=dst_tiles)
                else:
                    pb.op("dve", lambda e: e.tensor_copy(out=dst, in_=srcv), reads=[pt], writes=dst_tiles)
                k0 += n
                i += 1

        wbufs = []
        wb_rr = [0]

        def wload(wname, k0, nk, c0, ncols):
            t = wbufs[wb_rr[0] % len(wbufs)]
            wb_rr[0] += 1
            src = wb[wname][k0 * 128:(k0 + nk) * 128, c0:c0 + ncols].rearrange("(k p) n -> p k n", p=128)
            pb.dma("sp", t[:, 0:nk, 0:ncols], src, reads=[wb_tr[wname]], writes=[t])
            return t

        def gemm(wname, c0, ncols, act, kc0, nkc, ntile, mode, epi, wrow0=0):
            nout = ntile if mode == "tok" else ncols // 128
            pss = [psn() for _ in range(nout)]
            kk = 0
            while kk < nkc:
                nk = min(16, nkc - kk)
                wt = wload(wname, wrow0 + kk, nk, c0, ncols)
                for i in range(nout):
                    fns = []
                    for k in range(nk):
                        if mode == "tok":
                            l_ap = act[:, kc0 + kk + k, i * 128:(i + 1) * 128]
                            r_ap = wt[:, k, 0:ncols]
                            o_ap = pss[i][:, 0:ncols]
                        else:
                            l_ap = wt[:, k, i * 128:(i + 1) * 128]
                            r_ap = act[:, kc0 + kk + k, 0:ntile * 128]
                            o_ap = pss[i][:, 0:ntile * 128]
                        fns.append(lambda e, l_ap=l_ap, r_ap=r_ap, o_ap=o_ap, st=(kk + k == 0),
                                   sp_=(kk + k == nkc - 1): e.matmul(o_ap, lhsT=l_ap, rhs=r_ap, start=st, stop=sp_))
                    pb.op("pe", fns, reads=[act, wt], writes=[pss[i]])
                kk += nk
            for i in range(nout):
                epi(i, pss[i])

        def precast(sc, l):
            st = [sc.sb(f"pc_f{i}", [128, 4096], F32) for i in range(NPC)]
            sb_ = [sc.sb(f"pc_b{i}", [128, 4096], BF16) for i in range(NPC)]
            pb.dma("sp", gmix[:], W["norm_mix"][l:l + 1, :].rearrange("o (k p) -> p (o k)", p=128), writes=[gmix], slow=True)
            pb.dma("sp", gmlp[:], W["norm_mlp"][l:l + 1, :].rearrange("o (k p) -> p (o k)", p=128), writes=[gmlp], slow=True)
            i = 0
            for nm in wb:
                src = W[nm][l]
                R, C = src.shape
                g = gmix if nm == "w_in" else (gmlp if nm == "mlp_w1" else None)
                for r in range(R // 128):
                    cc = 0
                    while cc < C:
                        n = min(4096, C - cc)
                        a, b = st[i % NPC], sb_[i % NPC]
                        pb.dma("sp", a[:, 0:n], src[r * 128:(r + 1) * 128, cc:cc + n], writes=[a])
                        eng = ("dve", "act", "dve", "act", "pool")[i % 5]
                        if g is None:
                            if eng == "act":
                                pb.op("act", lambda e, a=a, b=b, n=n: e.copy(out=b[:, 0:n], in_=a[:, 0:n]),
                                      reads=[a], writes=[b])
                            else:
                                pb.op(eng, lambda e, a=a, b=b, n=n: e.tensor_copy(out=b[:, 0:n], in_=a[:, 0:n]),
                                      reads=[a], writes=[b])
                        elif eng == "act":
                            pb.op("act", lambda e, a=a, b=b, n=n, r=r, g=g: e.activation(
                                out=b[:, 0:n], in_=a[:, 0:n], func=AF.Copy, scale=g[:, r:r + 1]),
                                reads=[a, g], writes=[b])
                        else:
                            pb.op(eng, lambda e, a=a, b=b, n=n, r=r, g=g: e.tensor_scalar(
                                out=b[:, 0:n], in0=a[:, 0:n], scalar1=g[:, r:r + 1], scalar2=None, op0=ALU.mult),
                                reads=[a, g], writes=[b])
                        pb.dma("pool", wb[nm][r * 128:(r + 1) * 128, cc:cc + n], b[:, 0:n], reads=[b],
                               writes=[wb_tr[nm]])
                        cc += n
                        i += 1

        def load_layer_params(sc, l):
            for gi, nm in enumerate(("nsa_q_norm", "nsa_kc_norm", "nsa_ks_norm", "nsa_kw_norm", "moba_q_norm",
                                     "moba_k_norm")):
                pb.dma("pool", gains[:, gi, :], W[nm][l:l + 1, :].partition_broadcast(128), writes=[gains])
            pb.op("dve", lambda e: e.tensor_scalar(out=gains[:, 0, :], in0=gains[:, 0, :], scalar1=128 ** -0.5,
                                                   scalar2=None, op0=ALU.mult), reads=[gains], writes=[gains])
            pb.op("dve", lambda e: e.tensor_scalar(out=gains[:, 4, :], in0=gains[:, 4, :], scalar1=128 ** -0.5,
                                                   scalar2=None, op0=ALU.mult), reads=[gains], writes=[gains])
            pb.dma("pool", kcg_col[:], W["nsa_kc_norm"][l:l + 1, :].rearrange("o p -> p o"), writes=[kcg_col], slow=True)
            pb.dma("pool", gbias[:], W["nsa_gate_b"][l:l + 1, :].partition_broadcast(128), writes=[gbias])
            pb.dma("pool", sgng[:], W["sgu_norm"][l:l + 1, :].partition_broadcast(128), writes=[sgng])
            pb.dma("pool", sgb[:], W["sgu_b"][l].rearrange("g t -> t g"), writes=[sgb], slow=True)
            pb.dma("pool", pek[:], W["phi_pe_k"][l].rearrange("l d -> d l"), writes=[pek], slow=True)
            pb.dma("pool", pev[:], W["phi_pe_v"][l].rearrange("l d -> d l"), writes=[pev], slow=True)
            tril = sc.sb("lp_tril", [128, 128], F32)
            wtmp = sc.sb("lp_w", [128, 128], F32)
            pb.dma("pool", tril[:], c_tril[:, :], writes=[tril])
            for g in range(8):
                pb.dma("pool", wtmp[:], W["sgu_w"][l, g], writes=[wtmp])
                pb.op("dve", lambda e: e.tensor_tensor(out=wtmp[:], in0=wtmp[:], in1=tril[:], op=ALU.mult),
                      reads=[wtmp, tril], writes=[wtmp])
                pt = psn()
                pb.op("pe", lambda e: e.transpose(out=pt[:, 0:128], in_=wtmp[:], identity=identf[:]),
                      reads=[wtmp, identf], writes=[pt])
                pb.op("dve", lambda e, g=g: e.tensor_copy(out=sgwT[:, g, :], in_=pt[:, 0:128]), reads=[pt],
                      writes=[sgwT])

        sc_tmp = {}

        def headnorm_rope(sc, ps, nheads, gain_idx, cs, out_rot, out_plain, tag):
            W_ = nheads * 128
            sq = sc_tmp["hn_sq"]
            xn = sc_tmp["hn_xn"]
            ss = sc.sb("hn_ss" + tag, [128, 4], F32)
            pb.op("act", lambda e: e.activation(out=sq[:, 0:W_], in_=ps[:, 0:W_], func=AF.Square), reads=[ps],
                  writes=[sq])
            pb.op("dve", lambda e: e.tensor_reduce(out=ss[:, 0:nheads],
                                                   in_=sq[:, 0:W_].rearrange("p (h d) -> p h d", d=128),
                                                   axis=AX.X, op=ALU.add), reads=[sq], writes=[ss])
            rstd_from_ss(ss, ss, 1.0 / 128)
            xn3 = xn[:, 0:W_].rearrange("p (h d) -> p h d", d=128)
            pb.op("dve", lambda e: e.tensor_tensor(out=xn3, in0=ps[:, 0:W_].rearrange("p (h d) -> p h d", d=128),
                                                   in1=bc(ss[:, 0:nheads].unsqueeze(2), [128, nheads, 128]),
                                                   op=ALU.mult), reads=[ps, ss], writes=[xn])
            gr = bc(gains[:, gain_idx:gain_idx + 1, :], [128, nheads, 128])
            if out_plain is not None:
                pb.op("pool", lambda e: e.tensor_tensor(out=out_plain, in0=xn3, in1=gr, op=ALU.mult),
                      reads=[xn, gains], writes=[sc_tmp["hn_outp"]])
            if out_rot is not None:
                pb.op("dve", lambda e: e.tensor_tensor(out=xn3, in0=xn3, in1=gr, op=ALU.mult), reads=[xn, gains],
                      writes=[xn])
                t1 = sc_tmp["hn_t1"]
                t2 = sc_tmp["hn_t2"]
                x1 = xn3[:, :, 0:64]
                x2 = xn3[:, :, 64:128]
                cosb = bc(cs[:, 1:2, :], [128, nheads, 64])
                sinb = bc(cs[:, 0:1, :], [128, nheads, 64])
                t1v = t1[:, 0:nheads * 64].rearrange("p (h d) -> p h d", d=64)
                t2v = t2[:, 0:nheads * 64].rearrange("p (h d) -> p h d", d=64)
                pb.op("dve", lambda e: e.tensor_tensor(out=t1v, in0=x1, in1=cosb, op=ALU.mult), reads=[xn, cs],
                      writes=[t1])
                pb.op("pool", lambda e: e.tensor_tensor(out=t2v, in0=x2, in1=sinb, op=ALU.mult), reads=[xn, cs],
                      writes=[t2])
                pb.op("dve", lambda e: e.tensor_tensor(out=out_rot[:, :, 0:64], in0=t1v, in1=t2v, op=ALU.subtract),
                      reads=[t1, t2], writes=[sc_tmp["hn_outr"]])
                pb.op("dve", lambda e: e.tensor_tensor(out=t1v, in0=x2, in1=cosb, op=ALU.mult), reads=[xn, cs],
                      writes=[t1])
                pb.op("pool", lambda e: e.tensor_tensor(out=t2v, in0=x1, in1=sinb, op=ALU.mult), reads=[xn, cs],
                      writes=[t2])
                pb.op("dve", lambda e: e.tensor_tensor(out=out_rot[:, :, 64:128], in0=t1v, in1=t2v, op=ALU.add),
                      reads=[t1, t2], writes=[sc_tmp["hn_outr"]])

        def alloc_common_tmps(sc):
            sc_tmp["xb"] = sc.sb("t_xb", [128, D], BF16)
            sc_tmp["sq"] = sc_tmp["xb"]
            sc_tmp["hn_sq"] = sc.sb("t_hnsq", [128, 512], F32)
            sc_tmp["hn_xn"] = sc.sb("t_hnxn", [128, 512], F32)
            sc_tmp["hn_t1"] = sc.sb("t_hnt1", [128, 256], F32)
            sc_tmp["hn_t2"] = sc.sb("t_hnt2", [128, 256], F32)
            sc_tmp["hn_outp"] = sc.sb("t_hnop", [128, 512], BF16)
            sc_tmp["hn_outr"] = sc.sb("t_hnor", [128, 512], BF16)
            wbufs.clear()
            for i in range(NWBUF):
                wbufs.append(sc.sb(f"wbuf{i}", [128, 16, 512], BF16))

        def load_x_tile(sc, l, xt, tile_idx):
            if l == 0:
                pb.dma("sp", xt[:], x_in[tile_idx * 128:(tile_idx + 1) * 128, :], writes=[xt])
            else:
                pb.dma("sp", xt[:], xs[l][tile_idx * 128:(tile_idx + 1) * 128, :], reads=[xs_tr[l]], writes=[xt])

        def kv_pass(l):
            with Scope(pb) as sc:
                alloc_common_tmps(sc)
                hT = sc.sb("kv_hT", [128, KC, TOK], BF16)
                xts = [sc.sb(f"kv_x{i}", [128, D], F32) for i in range(2)]
                stage = [sc.sb(f"kv_st{i}", [128, 4, 128], BF16) for i in range(2)]
                st_i = [0]
                for g0 in range(0, NT, TG):
                    sg_ = Scope(pb)
                    sg_.__enter__()
                    css = []
                    for t in range(TG):
                        xt = xts[t % 2]
                        load_x_tile(sc, l, xt, g0 + t)
                        norm_T(sg_, xt, hT, t, f"kv{t % 2}")
                        css.append(rope_tables(sg_, pos_in[(g0 + t) * 128:(g0 + t + 1) * 128, :], f"kv{t}"))

                    def epi_factory(kind, dstT, dstV, h0, gain_idx):
                        def epi(t, ps):
                            tok0 = (g0 + t) * 128
                            if kind == "v":
                                sg = stage[st_i[0] % 2]
                                st_i[0] += 1
                                pb.op("act", lambda e: e.copy(out=sg[:].rearrange("p h d -> p (h d)"), in_=ps[:]),
                                      reads=[ps], writes=[sg])
                                pb.dma("pool", dstV[tok0:tok0 + 128, h0:h0 + 4, :], sg[:], reads=[sg],
                                       writes=[kv_tr])
                                return
                            src = sc_tmp["hn_outr"]
                            if kind == "kraw":
                                pb.op("act", lambda e: e.copy(out=src[:], in_=ps[:]), reads=[ps], writes=[src])
                            else:
                                headnorm_rope(sg_, ps, 4, gain_idx, css[t],
                                              src[:].rearrange("p (h d) -> p h d", d=128), None, "kv")
                            sg = stage[st_i[0] % 2]
                            st_i[0] += 1
                            transpose_to(src, 4, lambda k0, n: sg[:, k0:k0 + n, :], [sg])
                            pb.dma("pool", dstT[:, h0:h0 + 4, tok0:tok0 + 128], sg[:], reads=[sg], writes=[kv_tr])
                        return epi

                    o = c.off
                    plan = [("kraw", o["kc"], kcrawT, None, 0, 0), ("kraw", o["vc"], vcrawT, None, 0, 0),
                            ("knr", o["ks"], ksT, None, 0, 2), ("v", o["vs"], None, vs_s, 0, 0),
                            ("knr", o["kw"], kwT, None, 0, 3), ("v", o["vw"], None, vw_s, 0, 0),
                            ("knr", o["km"], kmT, None, 0, 5), ("knr", o["km"] + 512, kmT, None, 4, 5),
                            ("v", o["vm"], None, vm_s, 0, 0), ("v", o["vm"] + 512, None, vm_s, 4, 0)]
                    for kind, c0, dT, dV, h0, gi in plan:
                        gemm("w_in", c0, 512, hT, 0, KC, TG, "tok", epi_factory(kind, dT, dV, h0, gi))
                    sg_.__exit__(None, None, None)

        def dump(name, src_ap, tracks):
            if name in dbg:
                pb.dma("pool", dbg[name], src_ap, reads=tracks)

        ga_sb = gsb("ga_sb", [128, TG, 48], F32)
        p16col = gsb("p16col", [128, 1], F32)
        pb.op("dve", lambda e: e.tensor_scalar(out=p16col[:], in0=pcol[:], scalar1=16.0, scalar2=None, op0=ALU.mult),
              reads=[pcol], writes=[p16col])
        yT_s = dscr("yT_s", [128, KC, TOK], BF16)

        def alloc_wbufs(sc):
            wbufs.clear()
            for i in range(NWBUF):
                wbufs.append(sc.sb(f"wbuf{i}", [128, 16, 512], BF16))

        def gelu_from_psum(ps, W_, out_ap, out_tiles, tmp):
            x2, inner, sg = tmp
            pb.op("act", lambda e: e.activation(out=x2[:, 0:W_], in_=ps[:, 0:W_], func=AF.Square), reads=[ps],
                  writes=[x2])
            pb.op("dve", lambda e: e.tensor_scalar(out=inner[:, 0:W_], in0=x2[:, 0:W_], scalar1=0.044715, scalar2=1.0,
                                                   op0=ALU.mult, op1=ALU.add), reads=[x2], writes=[inner])
            pb.op("dve", lambda e: e.tensor_tensor(out=inner[:, 0:W_], in0=inner[:, 0:W_], in1=ps[:, 0:W_],
                                                   op=ALU.mult), reads=[inner, ps], writes=[inner])
            pb.op("act", lambda e: e.activation(out=sg[:, 0:W_], in_=inner[:, 0:W_], func=AF.Sigmoid,
                                                scale=1.5957691216057308), reads=[inner], writes=[sg])
            pb.op("dve", lambda e: e.tensor_tensor(out=out_ap, in0=sg[:, 0:W_], in1=ps[:, 0:W_], op=ALU.mult),
                  reads=[sg, ps], writes=out_tiles)

        def compress(l):
            CW = min(512, c.NCC * 128)
            with Scope(pb) as sc:
                kr = sc.sb("cp_kr", [128, S], BF16)
                klT = sc.sb("cp_kl", [128, 32, CW], BF16)
                w1 = sc.sb("cp_w1", [128, 32, 512], BF16)
                w2 = sc.sb("cp_w2", [128, 4, 128], BF16)
                g1T = sc.sb("cp_g1", [128, 4, CW], BF16)
                tmp = [sc.sb(f"cp_t{i}", [128, 512], F32) for i in range(3)]
                kst = sc.sb("cp_kst", [128, CW], BF16)
                vst = sc.sb("cp_vst", [128, 4, 128], BF16)
                pb.op("pool", lambda e: e.memset(klT[:], 0.0), writes=[klT])
                for which in ("k", "v"):
                    pe = pek if which == "k" else pev
                    n1, n2 = "phi_w1_" + which, "phi_w2_" + which
                    pb.dma("sp", w1[:], wb[n1].rearrange("(l d) h -> d l h", d=128), reads=[wb_tr[n1]], writes=[w1])
                    pb.dma("sp", w2[:], wb[n2].rearrange("(c h) d -> h c d", h=128), reads=[wb_tr[n2]], writes=[w2])
                    raw = kcrawT if which == "k" else vcrawT
                    for g in range(NG):
                        pb.dma("sp", kr[:], raw[:, g, :], reads=[kv_tr], writes=[kr])
                        for n0 in range(0, c.NCC * 128, CW):
                            nn = min(CW, c.NCMP - n0)
                            if nn < CW:
                                pb.op("pool", lambda e: e.memset(klT[:], 0.0), writes=[klT])
                            for lq in range(32):
                                src = kr[:, 16 * n0 + lq: 16 * n0 + lq + 16 * (nn - 1) + 1: 16]
                                pb.op(("dve", "pool")[lq % 2], lambda e, lq=lq, src=src: e.tensor_scalar(
                                    out=klT[:, lq, 0:nn], in0=src, scalar1=pe[:, lq:lq + 1], scalar2=None,
                                    op0=ALU.add), reads=[kr, pe], writes=[klT])
                            for hc in range(4):
                                ps = psn()
                                pb.op("pe", [lambda e, lq=lq, hc=hc, ps=ps: e.matmul(
                                    ps[:, 0:CW], lhsT=w1[:, lq, hc * 128:(hc + 1) * 128], rhs=klT[:, lq, :],
                                    start=(lq == 0), stop=(lq == 31)) for lq in range(32)], reads=[w1, klT], writes=[ps])
                                gelu_from_psum(ps, CW, g1T[:, hc, :], [g1T], tmp)
                            if which == "k":
                                ps = psn()
                                pb.op("pe", [lambda e, hc=hc, ps=ps: e.matmul(ps[:, 0:CW], lhsT=w2[:, hc, :],
                                                                             rhs=g1T[:, hc, :], start=(hc == 0),
                                                                             stop=(hc == 3)) for hc in range(4)],
                                      reads=[w2, g1T], writes=[ps])
                                x2, inner, _ = tmp
                                pb.op("act", lambda e: e.activation(out=x2[:, 0:CW], in_=ps[:, 0:CW], func=AF.Square),
                                      reads=[ps], writes=[x2])
                                ps2 = psn()
                                pb.op("pe", lambda e: e.matmul(ps2[:, 0:CW], lhsT=onesf[:], rhs=x2[:, 0:CW], start=True,
                                                               stop=True), reads=[onesf, x2], writes=[ps2])
                                rstd_from_ss(inner, ps2, 1.0 / 128)
                                pb.op("dve", lambda e: e.scalar_tensor_tensor(
                                    out=kst[:], in0=ps[:, 0:CW], scalar=kcg_col[:, 0:1], in1=inner[:, 0:CW],
                                    op0=ALU.mult, op1=ALU.mult), reads=[ps, kcg_col, inner], writes=[kst])
                                pb.dma("pool", kcT_s[:, g, n0:n0 + CW], kst[:], reads=[kst], writes=[kv_tr])
                            else:
                                for j in range(CW // 128):
                                    ps = psn()
                                    pb.op("pe", [lambda e, hc=hc, ps=ps, j=j: e.matmul(
                                        ps[:, 0:128], lhsT=g1T[:, hc, j * 128:(j + 1) * 128], rhs=w2[:, hc, :],
                                        start=(hc == 0), stop=(hc == 3)) for hc in range(4)], reads=[w2, g1T],
                                        writes=[ps])
                                    pb.op("act", lambda e, j=j, ps=ps: e.copy(out=vst[:, j, :], in_=ps[:, 0:128]),
                                          reads=[ps], writes=[vst])
                                pb.dma("pool", vc_s[n0:n0 + CW, g, :].rearrange("(j p) d -> p j d", p=128),
                                       vst[:, 0:CW // 128, :], reads=[vst], writes=[kv_tr])

        def kmean(l):
            with Scope(pb) as sc:
                kr = sc.sb("km_kr", [128, S], BF16)
                acc = sc.sb("km_acc", [128, c.NMB], F32)
                st = sc.sb("km_st", [128, MH, c.NMB], BF16)
                for h in range(MH):
                    pb.dma("sp", kr[:], kmT[:, h, :], reads=[kv_tr], writes=[kr])
                    pb.op("dve", lambda e: e.tensor_reduce(out=acc[:], in_=kr[:].rearrange("p (b k) -> p b k", k=256),
                                                           axis=AX.X, op=ALU.add), reads=[kr], writes=[acc])
                    pb.op("dve", lambda e, h=h: e.tensor_scalar(out=st[:, h, :], in0=acc[:], scalar1=1.0 / 256,
                                                                scalar2=None, op0=ALU.mult), reads=[acc], writes=[st])
                pb.dma("pool", kmeanT_s[:, :, :], st[:], reads=[st], writes=[kv_tr])

        def phase_a(l, last, qtiles):
            ntile = len(qtiles)
            src_x = x_in if l == 0 else xs[l]
            src_tr = [] if l == 0 else [xs_tr[l]]
            o = c.off
            with Scope(pb) as sc:
                alloc_common_tmps(sc)
                hT = sc.sb("a_hT", [128, KC, TOK], BF16)
                xt = sc.sb("a_x", [128, D], F32)
                xtmp = sc.sb("a_xtmp", [128, D], F32) if last else None
                gu = sc.sb("a_gu", [128, TG, 1024], BF16)
                gvn = sc.sb("a_gvn", [128, TG, 1024], BF16)
                gv = sc.sb("a_gv", [128, 512], F32)
                ob = sc.sb("a_ob", [128, 1024], BF16)
                tmp = [sc.sb(f"a_t{i}", [128, 512], F32) for i in range(3)]
                stg = [sc.sb(f"a_stg{i}", [128, 8, 128], BF16) for i in range(2)]
                ss4 = sc.sb("a_ss4", [128, 4], F32)
                stg_i = [0]
                css = []
                for t, qi in enumerate(qtiles):
                    if not last:
                        pb.dma("sp", xt[:], src_x[qi * 128:(qi + 1) * 128, :], reads=src_tr, writes=[xt])
                    else:
                        for m in range(NC):
                            ti = NC * qi + m
                            pb.dma("sp", xtmp[:], src_x[ti * 128:(ti + 1) * 128, :], reads=src_tr, writes=[xtmp])
                            if m == 0:
                                pb.op("dve", lambda e: e.tensor_scalar(out=xt[:], in0=xtmp[:], scalar1=onehot[:, 0:1],
                                                                       scalar2=None, op0=ALU.mult),
                                      reads=[xtmp, onehot], writes=[xt])
                            else:
                                pb.op("dve", lambda e, m=m: e.scalar_tensor_tensor(
                                    out=xt[:], in0=xtmp[:], scalar=onehot[:, m:m + 1], in1=xt[:], op0=ALU.mult,
                                    op1=ALU.add), reads=[xtmp, onehot, xt], writes=[xt])
                    pb.dma("pool", xg_s[t], xt[:], reads=[xt], writes=[grp_tr])
                    norm_T(sc, xt, hT, t, f"a{t}")
                    pos_ap = (ownpos_in if last else pos_in)[qi * 128:(qi + 1) * 128, :]
                    css.append(rope_tables(sc, pos_ap, f"a{t}"))
                pb.dma("pool", hT_s[:, :, :], hT[:], reads=[hT], writes=[grp_tr])

                def q_epi(dst_plain, dst_rot, h0, gain_idx):
                    def epi(t, ps):
                        outp = sc_tmp["hn_outp"]
                        outr = sc_tmp["hn_outr"]
                        headnorm_rope(sc, ps, 4, gain_idx, css[t], outr[:].rearrange("p (h d) -> p h d", d=128),
                                      outp[:].rearrange("p (h d) -> p h d", d=128) if dst_plain is not None else None,
                                      "a")
                        for srct, dst in ((outp, dst_plain), (outr, dst_rot)):
                            if dst is None:
                                continue
                            sg = stg[stg_i[0] % 2]
                            stg_i[0] += 1
                            transpose_to(srct, 4, lambda k0, n, sg=sg: sg[:, k0:k0 + n, :], [sg])
                            pb.dma("pool", dst[:, h0:h0 + 4, t * 128:(t + 1) * 128], sg[:, 0:4, :], reads=[sg],
                                   writes=[grp_tr])
                    return epi

                for cq in range(4):
                    gemm("w_in", o["qa"] + cq * 512, 512, hT, 0, KC, ntile, "tok", q_epi(qnT_s, qrT_s, cq * 4, 0))
                for cq in range(2):
                    gemm("w_in", o["qm"] + cq * 512, 512, hT, 0, KC, ntile, "tok", q_epi(None, qmT_s, cq * 4, 4))

                def ga_epi(t, ps):
                    pb.op("dve", lambda e: e.tensor_tensor(out=ga_sb[:, t, :], in0=ps[:, 0:48], in1=gbias[:],
                                                           op=ALU.add), reads=[ps, gbias], writes=[ga_sb])
                    pb.op("act", lambda e: e.activation(out=ga_sb[:, t, :], in_=ga_sb[:, t, :], func=AF.Sigmoid),
                          reads=[ga_sb], writes=[ga_sb])
                gemm("w_in", o["ga"], 48, hT, 0, KC, ntile, "tok", ga_epi)

                def u_epi(cu):
                    def epi(t, ps):
                        gelu_from_psum(ps, 512, gu[:, t, cu * 512:(cu + 1) * 512], [gu], tmp)
                    return epi

                def v_epi(cv):
                    def epi(t, ps):
                        gelu_from_psum(ps, 512, gv[:], [gv], tmp)
                        x2 = tmp[0]
                        pb.op("act", lambda e: e.activation(out=x2[:], in_=gv[:], func=AF.Square), reads=[gv],
                              writes=[x2])
                        pb.op("dve", lambda e: e.tensor_reduce(out=ss4[:], in_=x2[:].rearrange("p (h d) -> p h d", d=128),
                                                               axis=AX.X, op=ALU.add), reads=[x2], writes=[ss4])
                        rstd_from_ss(ss4, ss4, 1.0 / 128)
                        gv3 = gv[:].rearrange("p (h d) -> p h d", d=128)
                        pb.op("dve", lambda e: e.tensor_tensor(out=gv3, in0=gv3,
                                                               in1=bc(ss4[:, 0:4].unsqueeze(2), [128, 4, 128]),
                                                               op=ALU.mult), reads=[gv, ss4], writes=[gv])
                        pb.op("pool", lambda e: e.tensor_tensor(out=gvn[:, t, cv * 512:(cv + 1) * 512], in0=gv[:],
                                                                in1=sgng[:, cv * 512:(cv + 1) * 512], op=ALU.mult),
                              reads=[gv, sgng], writes=[gvn])
                    return epi

                for cu in range(2):
                    gemm("w_in", o["ub"] + cu * 512, 512, hT, 0, KC, ntile, "tok", u_epi(cu))
                for cv in range(2):
                    gemm("w_in", o["vb"] + cv * 512, 512, hT, 0, KC, ntile, "tok", v_epi(cv))
                for t in range(ntile):
                    for half in range(2):
                        ps = psn()
                        pb.op("pe", [lambda e, g=g, ps=ps: e.matmul(
                            ps[:, (g % 4) * 128:(g % 4 + 1) * 128], lhsT=sgwT[:, g, :],
                            rhs=gvn[:, t, g * 128:(g + 1) * 128], start=True, stop=True)
                            for g in range(half * 4, half * 4 + 4)], reads=[sgwT, gvn], writes=[ps])
                        for g in range(half * 4, half * 4 + 4):
                            pb.op("dve", lambda e, g=g, ps=ps: e.scalar_tensor_tensor(
                                out=ob[:, g * 128:(g + 1) * 128], in0=ps[:, (g % 4) * 128:(g % 4 + 1) * 128],
                                scalar=sgb[:, g:g + 1], in1=gu[:, t, g * 128:(g + 1) * 128], op0=ALU.add,
                                op1=ALU.mult), reads=[ps, sgb, gu], writes=[ob])
                    sg = stg[stg_i[0] % 2]
                    stg_i[0] += 1
                    transpose_to(ob, 8, lambda k0, n, sg=sg: sg[:, k0:k0 + n, :], [sg])
                    pb.dma("pool", oT_s[:, 16:24, t * 128:(t + 1) * 128], sg[:], reads=[sg], writes=[grp_tr])
        def phase_b(l, last, qtiles):
            NS = c.NSLC
            NMB = c.NMB
            NCC = c.NCC
            SB = [PS[0], PS[1]]
            MB = PS[2]
            OB = [PS[3], PS[4], PS[5], PS[6]]
            XB = PS[7]
            with Scope(pb) as sc:
                exps = sc.sb("b_exps", [128, 64, 128], BF16)
                expm = sc.sb("b_expm", [64, 64, 128], BF16)
                ov = sc.sb("b_ov", [128, NCC, 256], BF16)
                kcT = sc.sb("b_kcT", [128, NG, NCC * 128], BF16)
                vcP = sc.sb("b_vcP", [128, NCC, NG, 129], BF16)
                kmeanT = sc.sb("b_kmean", [128, MH, NMB], BF16)
                jrow = sc.sb("b_jrow", [128, 256], F32)
                pb.dma("sp", exps[:].rearrange("p a b -> p (a b)"), c_exps[:, :], writes=[exps])
                pb.dma("sp", expm[:].rearrange("p a b -> p (a b)"), c_expm[:, :], writes=[expm])
                pb.dma("sp", ov[:], c_ov.rearrange("(c p) j -> p c j", p=128), writes=[ov])
                pb.dma("sp", kcT[:], kcT_s[:, :, :], reads=[kv_tr], writes=[kcT])
                pb.op("dve", lambda e: e.memset(vcP[:], 1.0), writes=[vcP])
                for cc in range(NCC):
                    pb.dma("sp", vcP[:, cc, :, 0:128], vc_s[cc * 128:(cc + 1) * 128, :, :], reads=[kv_tr], writes=[vcP])
                pb.dma("sp", kmeanT[:], kmeanT_s[:, :, :], reads=[kv_tr], writes=[kmeanT])
                pb.dma("sp", jrow[:], c_jrow[0:1, :].partition_broadcast(128), writes=[jrow])
                qn = sc.sb("b_qn", [128, NH * 128], BF16)
                qr = sc.sb("b_qr", [128, NH * 128], BF16)
                qm = sc.sb("b_qm", [128, MH, 128], BF16)
                info = sc.sb("b_info", [128, 4], F32)
                trow = sc.sb("b_trow", [128, 128], F32)
                kbuf = [sc.sb(f"b_k{i}", [128, 4, 512], BF16) for i in range(2)]
                vbuf = [sc.sb(f"b_v{i}", [128, 4, 4, 129], BF16) for i in range(2)]
                for vb_ in vbuf:
                    pb.op("dve", lambda e, vb_=vb_: e.memset(vb_[:], 1.0), writes=[vb_])
                e_sb = [sc.sb(f"b_e{i}", [128, 512], F32) for i in range(2)]
                em = [sc.sb(f"b_em{i}", [128, 512], BF16) for i in range(2)]
                cmd = sc.sb("b_cmd", [128, 8, 128], F32)
                msk = sc.sb("b_msk", [128, 512], F32)
                wm = sc.sb("b_wm", [128, 128], F32)
                wm2 = sc.sb("b_wm2", [128, 128], F32)
                ecmp = sc.sb("b_ecmp", [128, NCC, 512], BF16)
                imp = sc.sb("b_imp", [128, 256], F32)
                dd = sc.sb("b_dd", [128, 256], F32)
                ff = sc.sb("b_ff", [128, 256], F32)
                al = sc.sb("b_al", [128, 256], F32)
                pen = sc.sb("b_pen", [128, 256], F32)
                score = sc.sb("b_score", [128, 256], F32)
                sc2 = sc.sb("b_sc2", [128, 256], F32)
                m8a = sc.sb("b_m8a", [128, 8], F32)
                m8b = sc.sb("b_m8b", [128, 8], F32)
                sel = sc.sb("b_sel", [128, 256], BF16)
                selT = sc.sb("b_selT", [128, 2, 128], BF16)
                pastm = sc.sb("b_pastm", [128, NMB], F32)
                ownm = sc.sb("b_ownm", [128, NMB], F32)
                penm = sc.sb("b_penm", [128, NMB], F32)
                msc = sc.sb("b_msc", [128, MH, NMB], F32)
                m8m = sc.sb("b_m8m", [128, MH, 8], F32)
                selm = sc.sb("b_selm", [128, MH, NMB], F32)
                selmb = sc.sb("b_selmb", [128, MH, 64], BF16)
                selTm = sc.sb("b_selTm", [64, MH, 128], BF16)
                oa = sc.sb("b_oa", [128, NH * 128], F32)
                oab = sc.sb("b_oab", [128, NH * 128], BF16)
                om = sc.sb("b_om", [128, MH * 128], BF16)
                sm = sc.sb("b_sm", [128, 4], F32)
                ostg = sc.sb("b_ostg", [128, 8, 128], BF16)
                pb.op("pool", lambda e: e.memset(sel[:], 0.0), writes=[sel])
                pb.op("pool", lambda e: e.memset(selmb[:], 0.0), writes=[selmb])
                ctr = [0]

                def finish_head(O, ga_col, dst_ap, dst_tile, first):
                    pb.op("dve", lambda e: e.tensor_scalar(out=sm[:, 0:1], in0=O[:, 128:129], scalar1=1e-30,
                                                           scalar2=None, op0=ALU.max), reads=[O], writes=[sm])
                    pb.op("dve", lambda e: e.reciprocal(out=sm[:, 1:2], in_=sm[:, 0:1]), reads=[sm], writes=[sm])
                    if ga_col is not None:
                        pb.op("dve", lambda e: e.tensor_tensor(out=sm[:, 2:3], in0=sm[:, 1:2], in1=ga_col, op=ALU.mult),
                              reads=[sm, ga_sb], writes=[sm])
                        coef = sm[:, 2:3]
                    else:
                        coef = sm[:, 1:2]
                    if first:
                        pb.op("dve", lambda e: e.tensor_scalar(out=dst_ap, in0=O[:, 0:128], scalar1=coef, scalar2=None,
                                                               op0=ALU.mult), reads=[O, sm], writes=[dst_tile])
                    else:
                        pb.op("dve", lambda e: e.scalar_tensor_tensor(out=dst_ap, in0=O[:, 0:128], scalar=coef,
                                                                      in1=dst_ap, op0=ALU.mult, op1=ALU.add),
                              reads=[O, sm, dst_tile], writes=[dst_tile])

                for t, qi in enumerate(qtiles):
                    if not last:
                        imax = imin = qi
                        ndiag = 1
                    else:
                        imax = NC * qi + NC - 1
                        imin = NC * qi
                        ndiag = NC
                    nk = imax + 1
                    w0 = max(0, imin - 4)
                    ncc = min(NCC, (8 * imax + 7 + 127) // 128)
                    inf_src = info_own if last else info_all
                    row_src = row_own if last else row_all
                    pb.dma("sp", qn[:].rearrange("p (h q) -> p h q", q=128), qnT_s[:, :, t * 128:(t + 1) * 128],
                           reads=[grp_tr], writes=[qn])
                    pb.dma("sp", qr[:].rearrange("p (h q) -> p h q", q=128), qrT_s[:, :, t * 128:(t + 1) * 128],
                           reads=[grp_tr], writes=[qr])
                    pb.dma("sp", qm[:], qmT_s[:, :, t * 128:(t + 1) * 128], reads=[grp_tr], writes=[qm])
                    pb.dma("pool", info[:], inf_src[qi * 128:(qi + 1) * 128, :], writes=[info])
                    pb.dma("pool", trow[:], row_src[qi:qi + 1, :].partition_broadcast(128), writes=[trow])
                    for i in range(ndiag):
                        kt = nk - ndiag + i
                        pb.op("pool", lambda e, i=i, kt=kt: e.tensor_scalar(
                            out=cmd[:, i, :], in0=trow[:], scalar1=float(-kt * 128), scalar2=pcol[:, 0:1], op0=ALU.add,
                            op1=ALU.is_ge), reads=[trow, pcol], writes=[cmd])

                    for g in range(NG):
                        qng = qn[:, g * 512:(g + 1) * 512]
                        qrg = qr[:, g * 512:(g + 1) * 512]
                        for cc in range(ncc):
                            Sb = SB[ctr[0] % 2]
                            eb = e_sb[ctr[0] % 2]
                            ctr[0] += 1
                            pb.op("pe", lambda e, cc=cc, Sb=Sb: e.matmul(Sb[:, 0:512], lhsT=kcT[:, g, cc * 128:(cc + 1) * 128],
                                                                       rhs=qng, start=True, stop=True),
                                  reads=[kcT, qn], writes=[Sb])
                            pb.op("act", lambda e, Sb=Sb, eb=eb: e.activation(out=eb[:], in_=Sb[:], func=AF.Exp),
                                  reads=[Sb], writes=[eb])
                            pb.op("pool", lambda e, cc=cc: e.tensor_scalar(
                                out=wm[:], in0=trow[:], scalar1=float(-(31 + 2048 * cc)), scalar2=p16col[:, 0:1],
                                op0=ALU.add, op1=ALU.is_ge), reads=[trow, p16col], writes=[wm])
                            pb.op("dve", lambda e, cc=cc, eb=eb: e.tensor_tensor(
                                out=ecmp[:, cc, :].rearrange("p (r q) -> p r q", q=128),
                                in0=eb[:].rearrange("p (r q) -> p r q", q=128),
                                in1=bc(wm[:].unsqueeze(1), [128, 4, 128]), op=ALU.mult), reads=[eb, wm], writes=[ecmp])
                        for r in range(4):
                            head = g * 4 + r
                            A = OB[r]
                            Bk = MB if r % 2 == 0 else XB
                            pb.op("pe", [lambda e, cc=cc, A=A: e.matmul(
                                A[:, 0:129], lhsT=ecmp[:, cc, r * 128:(r + 1) * 128], rhs=vcP[:, cc, g, :],
                                start=(cc == 0), stop=(cc == ncc - 1)) for cc in range(ncc)], reads=[ecmp, vcP],
                                writes=[A])
                            pb.op("pe", [lambda e, cc=cc, Bk=Bk: e.matmul(
                                Bk[:, 0:256], lhsT=ecmp[:, cc, r * 128:(r + 1) * 128], rhs=ov[:, cc, :],
                                start=(cc == 0), stop=(cc == ncc - 1)) for cc in range(ncc)], reads=[ecmp, ov],
                                writes=[Bk])
                            finish_head(A, ga_sb[:, t, head:head + 1], oa[:, head * 128:(head + 1) * 128], oa, True)
                            if r == 0:
                                pb.op("dve", lambda e, Bk=Bk: e.tensor_scalar(out=imp[:], in0=Bk[:, 0:256],
                                                                            scalar1=sm[:, 1:2], scalar2=None,
                                                                            op0=ALU.mult), reads=[Bk, sm], writes=[imp])
                            else:
                                pb.op("dve", lambda e, Bk=Bk: e.scalar_tensor_tensor(
                                    out=imp[:], in0=Bk[:, 0:256], scalar=sm[:, 1:2], in1=imp[:], op0=ALU.mult,
                                    op1=ALU.add), reads=[Bk, sm, imp], writes=[imp])
                        pb.op("dve", lambda e: e.tensor_scalar(out=dd[:, 0:NS], in0=jrow[:, 0:NS], scalar1=info[:, 1:2],
                                                               scalar2=None, op0=ALU.subtract), reads=[jrow, info],
                              writes=[dd])
                        pb.op("dve", lambda e: e.tensor_scalar(out=ff[:, 0:NS], in0=dd[:, 0:NS], scalar1=-1.0,
                                                               scalar2=None, op0=ALU.is_ge), reads=[dd], writes=[ff])
                        pb.op("dve", lambda e: e.scalar_tensor_tensor(out=ff[:, 0:NS], in0=dd[:, 0:NS], scalar=0.0,
                                                                      in1=ff[:, 0:NS], op0=ALU.is_le, op1=ALU.mult),
                              reads=[dd, ff], writes=[ff])
                        pb.op("dve", lambda e: e.memset(ff[:, 0:1], 1.0), reads=[ff], writes=[ff])
                        pb.op("dve", lambda e: e.tensor_single_scalar(out=al[:, 0:NS], in_=dd[:, 0:NS], scalar=0.0,
                                                                      op=ALU.is_le), reads=[dd], writes=[al])
                        pb.op("dve", lambda e: e.scalar_tensor_tensor(out=score[:, 0:NS], in0=ff[:, 0:NS], scalar=1e9,
                                                                      in1=imp[:, 0:NS], op0=ALU.mult, op1=ALU.add),
                              reads=[ff, imp], writes=[score])
                        pb.op("dve", lambda e: e.tensor_scalar(out=pen[:, 0:NS], in0=al[:, 0:NS], scalar1=1.0,
                                                               scalar2=1e30, op0=ALU.subtract, op1=ALU.mult),
                              reads=[al], writes=[pen])
                        pb.op("dve", lambda e: e.tensor_tensor(out=score[:, 0:NS], in0=score[:, 0:NS], in1=al[:, 0:NS],
                                                               op=ALU.mult), reads=[score, al], writes=[score])
                        pb.op("dve", lambda e: e.tensor_tensor(out=score[:, 0:NS], in0=score[:, 0:NS], in1=pen[:, 0:NS],
                                                               op=ALU.add), reads=[score, pen], writes=[score])
                        pb.op("dve", lambda e: e.max(out=m8a[:], in_=score[:, 0:NS]), reads=[score], writes=[m8a])
                        pb.op("dve", lambda e: e.match_replace(out=sc2[:, 0:NS], in_to_replace=m8a[:],
                                                               in_values=score[:, 0:NS], imm_value=-3.0e38),
                              reads=[score, m8a], writes=[sc2])
                        pb.op("dve", lambda e: e.max(out=m8b[:], in_=sc2[:, 0:NS]), reads=[sc2], writes=[m8b])
                        pb.op("dve", lambda e: e.scalar_tensor_tensor(out=sel[:, 0:NS], in0=score[:, 0:NS],
                                                                      scalar=m8b[:, 7:8], in1=al[:, 0:NS],
                                                                      op0=ALU.is_ge, op1=ALU.mult),
                              reads=[score, m8b, al], writes=[sel])
                        transpose_to(sel, 2, lambda k0, n: selT[:, k0:k0 + n, :], [selT], bank=XB)
                        for br in ("slc", "win"):
                            kT_src, v_src = (ksT, vs_s) if br == "slc" else (kwT, vw_s)
                            k_lo = 0 if br == "slc" else w0
                            for ktb in range(k_lo, nk, 4):
                                nkb = min(4, nk - ktb)
                                kb = kbuf[ctr[0] % 2]
                                vb = vbuf[ctr[0] % 2]
                                pb.dma("sp", kb[:, 0, 0:nkb * 128], kT_src[:, g, ktb * 128:(ktb + nkb) * 128],
                                       reads=[kv_tr], writes=[kb])
                                pb.dma("sp", vb[:, 0:nkb, 0, 0:128],
                                       v_src[ktb * 128:(ktb + nkb) * 128, g, :].rearrange("(k p) d -> p k d", p=128),
                                       reads=[kv_tr], writes=[vb])
                                for kk in range(nkb):
                                    kt = ktb + kk
                                    Sb = SB[ctr[0] % 2]
                                    eb = e_sb[ctr[0] % 2]
                                    emb = em[ctr[0] % 2]
                                    ctr[0] += 1
                                    pb.op("pe", lambda e, kk=kk, Sb=Sb, kb=kb: e.matmul(
                                        Sb[:, 0:512], lhsT=kb[:, 0, kk * 128:(kk + 1) * 128], rhs=qrg, start=True,
                                        stop=True), reads=[kb, qr], writes=[Sb])
                                    pb.op("act", lambda e, Sb=Sb, eb=eb: e.activation(out=eb[:], in_=Sb[:], func=AF.Exp),
                                          reads=[Sb], writes=[eb])
                                    if br == "slc":
                                        pb.op("pe", lambda e, kt=kt: e.matmul(MB[:, 0:128], lhsT=exps[:, kt % 64, :],
                                                                              rhs=selT[:, kt // 64, :], start=True,
                                                                              stop=True), reads=[exps, selT], writes=[MB])
                                        if kt >= nk - ndiag:
                                            di = kt - (nk - ndiag)
                                            pb.op("dve", lambda e, di=di: e.tensor_tensor(
                                                out=wm[:], in0=cmd[:, di, :], in1=MB[:, 0:128], op=ALU.mult),
                                                reads=[cmd, MB], writes=[wm])
                                            mask_ap, mask_t = wm[:], wm
                                        else:
                                            mask_ap, mask_t = MB[:, 0:128], MB
                                    else:
                                        pb.op("pool", lambda e, kt=kt: e.tensor_scalar(
                                            out=wm[:], in0=trow[:], scalar1=float(-kt * 128), scalar2=pcol[:, 0:1],
                                            op0=ALU.add, op1=ALU.subtract), reads=[trow, pcol], writes=[wm])
                                        pb.op("pool", lambda e: e.tensor_scalar(
                                            out=wm2[:], in0=wm[:], scalar1=0.0, scalar2=None, op0=ALU.is_ge),
                                            reads=[wm], writes=[wm2])
                                        pb.op("pool", lambda e: e.tensor_scalar(
                                            out=wm[:], in0=wm[:], scalar1=511.0, scalar2=None, op0=ALU.is_le),
                                            reads=[wm], writes=[wm])
                                        pb.op("pool", lambda e: e.tensor_tensor(
                                            out=wm[:], in0=wm[:], in1=wm2[:], op=ALU.mult), reads=[wm, wm2],
                                            writes=[wm])
                                        mask_ap, mask_t = wm[:], wm
                                    pb.op("dve", lambda e, eb=eb, emb=emb, mask_ap=mask_ap: e.tensor_tensor(
                                        out=emb[:].rearrange("p (r q) -> p r q", q=128),
                                        in0=eb[:].rearrange("p (r q) -> p r q", q=128),
                                        in1=bc(mask_ap.unsqueeze(1), [128, 4, 128]), op=ALU.mult),
                                        reads=[eb, mask_t], writes=[emb])
                                    for r in range(4):
                                        pb.op("pe", lambda e, r=r, emb=emb, vb=vb, kk=kk, kt=kt: e.matmul(
                                            OB[r][:, 0:129], lhsT=emb[:, r * 128:(r + 1) * 128], rhs=vb[:, kk, 0, :],
                                            start=(kt == k_lo), stop=(kt == nk - 1)), reads=[emb, vb], writes=[OB[r]])
                            gofs = 16 if br == "slc" else 32
                            for r in range(4):
                                head = g * 4 + r
                                finish_head(OB[r], ga_sb[:, t, gofs + head:gofs + head + 1],
                                            oa[:, head * 128:(head + 1) * 128], oa, False)

                    pb.op("pe", [lambda e, h=h: e.matmul(XB[:, h * NMB:(h + 1) * NMB], lhsT=qm[:, h, :],
                                                         rhs=kmeanT[:, h, :], start=True, stop=True)
                                 for h in range(MH)], reads=[qm, kmeanT], writes=[XB])
                    pb.op("dve", lambda e: e.tensor_scalar(out=pastm[:], in0=jrow[:, 0:NMB], scalar1=info[:, 2:3],
                                                           scalar2=None, op0=ALU.is_lt), reads=[jrow, info],
                          writes=[pastm])
                    pb.op("dve", lambda e: e.tensor_scalar(out=ownm[:], in0=jrow[:, 0:NMB], scalar1=info[:, 2:3],
                                                           scalar2=None, op0=ALU.is_equal), reads=[jrow, info],
                          writes=[ownm])
                    pb.op("dve", lambda e: e.tensor_scalar(out=penm[:], in0=pastm[:], scalar1=1.0, scalar2=1e30,
                                                           op0=ALU.subtract, op1=ALU.mult), reads=[pastm], writes=[penm])
                    pb.op("dve", lambda e: e.tensor_tensor(out=msc[:], in0=XB[:, 0:MH * NMB].rearrange(
                        "p (h b) -> p h b", b=NMB), in1=bc(pastm[:].unsqueeze(1), [128, MH, NMB]), op=ALU.mult),
                        reads=[XB, pastm], writes=[msc])
                    pb.op("dve", lambda e: e.tensor_tensor(out=msc[:], in0=msc[:],
                                                           in1=bc(penm[:].unsqueeze(1), [128, MH, NMB]), op=ALU.add),
                          reads=[msc, penm], writes=[msc])
                    for h in range(MH):
                        pb.op("dve", lambda e, h=h: e.max(out=m8m[:, h, :], in_=msc[:, h, :]), reads=[msc],
                              writes=[m8m])
                    for h in range(MH):
                        pb.op("dve", lambda e, h=h: e.scalar_tensor_tensor(
                            out=selm[:, h, :], in0=msc[:, h, :], scalar=m8m[:, h, 2:3], in1=pastm[:], op0=ALU.is_ge,
                            op1=ALU.mult), reads=[msc, m8m, pastm], writes=[selm])
                    pb.op("dve", lambda e: e.tensor_tensor(out=selmb[:, :, 0:NMB], in0=selm[:],
                                                           in1=bc(ownm[:].unsqueeze(1), [128, MH, NMB]), op=ALU.add),
                          reads=[selm, ownm], writes=[selmb])
                    xbb = XB[:].bitcast(BF16)
                    pb.op("pe", [lambda e, h=h: e.transpose(out=xbb[0:64, h * 128:(h + 1) * 128], in_=selmb[:, h, :],
                                                            identity=identb[:]) for h in range(MH)],
                          reads=[selmb, identb], writes=[XB])
                    pb.op("act", lambda e: e.copy(out=selTm[:].rearrange("p h q -> p (h q)"), in_=xbb[0:64, 0:MH * 128]),
                          reads=[XB], writes=[selTm])
                    for hb in range(MH // 4):
                        for ktb in range(0, nk, 4):
                            nkb = min(4, nk - ktb)
                            kb = kbuf[ctr[0] % 2]
                            vb = vbuf[ctr[0] % 2]
                            pb.dma("sp", kb[:, :, 0:nkb * 128], kmT[:, hb * 4:(hb + 1) * 4, ktb * 128:(ktb + nkb) * 128],
                                   reads=[kv_tr], writes=[kb])
                            for kk in range(nkb):
                                pb.dma("sp", vb[:, kk, :, 0:128],
                                       vm_s[(ktb + kk) * 128:(ktb + kk + 1) * 128, hb * 4:(hb + 1) * 4, :],
                                       reads=[kv_tr], writes=[vb])
                            for kk in range(nkb):
                                kt = ktb + kk
                                Sb = SB[ctr[0] % 2]
                                eb = e_sb[ctr[0] % 2]
                                emb = em[ctr[0] % 2]
                                ctr[0] += 1
                                pb.op("pe", [lambda e, h=h, kk=kk, Sb=Sb, kb=kb: e.matmul(
                                    Sb[:, h * 128:(h + 1) * 128], lhsT=kb[:, h, kk * 128:(kk + 1) * 128],
                                    rhs=qm[:, hb * 4 + h, :], start=True, stop=True) for h in range(4)],
                                    reads=[kb, qm], writes=[Sb])
                                pb.op("pe", [lambda e, h=h, kt=kt: e.matmul(
                                    MB[:, h * 128:(h + 1) * 128], lhsT=expm[:, kt // 2, :], rhs=selTm[:, hb * 4 + h, :],
                                    start=True, stop=True) for h in range(4)], reads=[expm, selTm], writes=[MB])
                                pb.op("act", lambda e, Sb=Sb, eb=eb: e.activation(out=eb[:], in_=Sb[:], func=AF.Exp),
                                      reads=[Sb], writes=[eb])
                                if kt >= nk - ndiag:
                                    di = kt - (nk - ndiag)
                                    pb.op("dve", lambda e, di=di: e.tensor_tensor(
                                        out=msk[:].rearrange("p (r q) -> p r q", q=128),
                                        in0=MB[:].rearrange("p (r q) -> p r q", q=128),
                                        in1=bc(cmd[:, di:di + 1, :], [128, 4, 128]), op=ALU.mult), reads=[MB, cmd],
                                        writes=[msk])
                                    mask_ap, mask_t = msk[:], msk
                                else:
                                    mask_ap, mask_t = MB[:], MB
                                pb.op("dve", lambda e, eb=eb, emb=emb, mask_ap=mask_ap: e.tensor_tensor(
                                    out=emb[:], in0=eb[:], in1=mask_ap, op=ALU.mult), reads=[eb, mask_t], writes=[emb])
                                for h in range(4):
                                    pb.op("pe", lambda e, h=h, emb=emb, vb=vb, kk=kk, kt=kt: e.matmul(
                                        OB[h][:, 0:129], lhsT=emb[:, h * 128:(h + 1) * 128], rhs=vb[:, kk, h, :],
                                        start=(kt == 0), stop=(kt == nk - 1)), reads=[emb, vb], writes=[OB[h]])
                        for h in range(4):
                            hh = hb * 4 + h
                            finish_head(OB[h], None, om[:, hh * 128:(hh + 1) * 128], om, True)
                    pb.op("act", lambda e: e.copy(out=oab[:], in_=oa[:]), reads=[oa], writes=[oab])
                    for half in range(2):
                        transpose_to(oab, 8, lambda k0, n: ostg[:, k0:k0 + n, :], [ostg],
                                     src_fn=lambda k, half=half: oab[:, (half * 8 + k) * 128:(half * 8 + k + 1) * 128],
                                     bank=XB)
                        pb.dma("pool", oT_s[:, half * 8:(half + 1) * 8, t * 128:(t + 1) * 128], ostg[:], reads=[ostg],
                               writes=[grp_tr])
                    transpose_to(om, 8, lambda k0, n: ostg[:, k0:k0 + n, :], [ostg], bank=XB)
                    pb.dma("pool", oT_s[:, 24:32, t * 128:(t + 1) * 128], ostg[:], reads=[ostg], writes=[grp_tr])

        def phase_c1(l, ntile):
            o = c.off
            NTK = ntile * 128
            with Scope(pb) as sc:
                alloc_wbufs(sc)
                hT = sc.sb("c_hT", [128, KC, TOK], BF16)
                oT = sc.sb("c_oT", [128, KC, TOK], BF16)
                sig = [sc.sb(f"c_sig{j}", [128, TOK], F32) for j in range(4)]
                yacc = [sc.sb(f"c_yacc{j}", [128, TOK], F32) for j in range(4)]
                tmp = sc.sb("c_tmp", [128, TOK], F32)
                ystg = sc.sb("c_ystg", [128, 4, TOK], BF16)
                pb.dma("sp", hT[:], hT_s[:, :, :], reads=[grp_tr], writes=[hT])
                pb.dma("sp", oT[:], oT_s[:, :, :], reads=[grp_tr], writes=[oT])
                for cc4 in range(D // 512):
                    col0 = cc4 * 512
                    for bi, (gname, pname, kofs, nkp) in enumerate((("gma", "proj_a", 0, 16), ("gmb", "proj_b", 16, 8),
                                                                    ("gmc", "proj_c", 24, 8))):
                        def epi_gate(j, ps):
                            pb.op("act", lambda e: e.activation(out=sig[j][:, 0:NTK], in_=ps[:, 0:NTK], func=AF.Sigmoid),
                                  reads=[ps], writes=[sig[j]])

                        def epi_proj(j, ps, bi=bi):
                            if bi == 0:
                                pb.op("dve", lambda e: e.tensor_tensor(out=yacc[j][:, 0:NTK], in0=sig[j][:, 0:NTK],
                                                                       in1=ps[:, 0:NTK], op=ALU.mult),
                                      reads=[sig[j], ps], writes=[yacc[j]])
                                return
                            pb.op("dve", lambda e: e.tensor_tensor(out=tmp[:, 0:NTK], in0=sig[j][:, 0:NTK],
                                                                   in1=ps[:, 0:NTK], op=ALU.mult), reads=[sig[j], ps],
                                  writes=[tmp])
                            if bi == 1:
                                pb.op("pool", lambda e: e.tensor_tensor(out=yacc[j][:, 0:NTK], in0=yacc[j][:, 0:NTK],
                                                                        in1=tmp[:, 0:NTK], op=ALU.add),
                                      reads=[yacc[j], tmp], writes=[yacc[j]])
                            else:
                                pb.op("pool", lambda e: e.tensor_tensor(out=ystg[:, j, 0:NTK], in0=yacc[j][:, 0:NTK],
                                                                        in1=tmp[:, 0:NTK], op=ALU.add),
                                      reads=[yacc[j], tmp], writes=[ystg])
                        gemm("w_in", o[gname] + col0, 512, hT, 0, KC, ntile, "feat", epi_gate)
                        gemm(pname, col0, 512, oT, kofs, nkp, ntile, "feat", epi_proj)
                    pb.dma("pool", yT_s[:, cc4 * 4:(cc4 + 1) * 4, :], ystg[:], reads=[ystg], writes=[grp_tr])

        def phase_c2d(l, last, qtiles):
            ntile = len(qtiles)
            NTK = ntile * 128
            with Scope(pb) as scx:
                xacc = scx.sb("x_acc", [128, TG, D], F32)
                for t in range(ntile):
                    pb.dma("sp", xacc[:, t, :], xg_s[t], reads=[grp_tr], writes=[xacc])
                with Scope(pb) as sc:
                    alloc_wbufs(sc)
                    yT = sc.sb("c2_yT", [128, KC, TOK], BF16)
                    pb.dma("sp", yT[:], yT_s[:, :, :], reads=[grp_tr], writes=[yT])
                    for oc in range(D // 512):
                        def epi(t, ps, oc=oc):
                            pb.op("dve", lambda e: e.tensor_tensor(out=xacc[:, t, oc * 512:(oc + 1) * 512],
                                                                   in0=xacc[:, t, oc * 512:(oc + 1) * 512], in1=ps[:],
                                                                   op=ALU.add), reads=[xacc, ps], writes=[xacc])
                        gemm("w_out", oc * 512, 512, yT, 0, KC, ntile, "tok", epi)
                h2T = scx.sb("d_h2T", [128, KC, TOK], BF16)
                with Scope(pb) as sc:
                    sc_tmp["xb"] = sc.sb("t_xb", [128, D], BF16)
                    sc_tmp["sq"] = sc_tmp["xb"]
                    for t in range(ntile):
                        norm_T(sc, xacc, h2T, t, f"d{t}", xap=xacc[:, t, :])
                with Scope(pb) as sc:
                    alloc_wbufs(sc)
                    hid = sc.sb("d_hid", [128, 16, TOK], BF16)
                    rl = [sc.sb(f"d_rl{i}", [128, TOK], F32) for i in range(2)]
                    rli = [0]
                    for part in range(DFF // 2048):
                        for c4 in range(4):
                            def epi1(j, ps, c4=c4):
                                r_ = rl[rli[0] % 2]
                                rli[0] += 1
                                pb.op("act", lambda e: e.activation(out=r_[:, 0:NTK], in_=ps[:, 0:NTK], func=AF.Relu),
                                      reads=[ps], writes=[r_])
                                pb.op("pool", lambda e: e.tensor_tensor(out=hid[:, c4 * 4 + j, 0:NTK], in0=r_[:, 0:NTK],
                                                                        in1=r_[:, 0:NTK], op=ALU.mult), reads=[r_],
                                      writes=[hid])
                            gemm("mlp_w1", part * 2048 + c4 * 512, 512, h2T, 0, KC, ntile, "feat", epi1)
                        for oc in range(D // 512):
                            def epi2(t, ps, oc=oc):
                                pb.op("dve", lambda e: e.tensor_tensor(out=xacc[:, t, oc * 512:(oc + 1) * 512],
                                                                       in0=xacc[:, t, oc * 512:(oc + 1) * 512],
                                                                       in1=ps[:], op=ALU.add), reads=[xacc, ps],
                                      writes=[xacc])
                            gemm("mlp_w2", oc * 512, 512, hid, 0, 16, ntile, "tok", epi2, wrow0=part * 16)
                for t, qi in enumerate(qtiles):
                    if last:
                        pb.dma("pool", y_out[qi * 128:(qi + 1) * 128, :], xacc[:, t, :], reads=[xacc])
                    else:
                        pb.dma("pool", xs[l + 1][qi * 128:(qi + 1) * 128, :], xacc[:, t, :], reads=[xacc],
                               writes=[xs_tr[l + 1]])
                if "xmid" in dbg and False:
                    pass

        for l in range(L):
            last = (l == L - 1)
            with Scope(pb) as sc:
                precast(sc, l)
            with Scope(pb) as sc:
                load_layer_params(sc, l)
            kv_pass(l)
            compress(l)
            kmean(l)
            ntl = NTO if last else NT
            assert ntl % TG == 0
            for g0 in range(0, ntl, TG):
                qtiles = list(range(g0, g0 + TG))
                phase_a(l, last, qtiles)
                phase_b(l, last, qtiles)
                phase_c1(l, TG)
                phase_c2d(l, last, qtiles)
        pb.barrier(engines=("sp",))
        print("instructions:", pb.nins)
    return nc


def host_constants(cfg):
    c = cfg
    bf = ml_dtypes.bfloat16
    ident = np.eye(128, dtype=np.float32)
    tril = np.tril(np.ones((128, 128), np.float32))
    n = np.arange(c.NCC * 128)
    j = np.arange(256)
    ov = ((n[:, None] * 16 <= j[None, :] * 64 + 63) & (n[:, None] * 16 + 31 >= j[None, :] * 64)).astype(np.float32)
    ov[c.NCMP:, :] = 0
    exps = np.zeros((128, 64, 128), np.float32)
    for m in range(64):
        for k in range(128):
            exps[2 * m + k // 64, m, k] = 1.0
    expm = np.zeros((64, 64, 128), np.float32)
    for m in range(64):
        expm[m, m, :] = 1.0
    invf = (10000.0 ** (-np.arange(0, 128, 2, dtype=np.float32) / 128)).astype(np.float32)[None, :]
    jrow = np.arange(256, dtype=np.float32)[None, :]
    return {"c_identb": ident.astype(bf), "c_identf": ident, "c_tril": tril, "c_ov": ov.astype(bf),
            "c_exps": exps.reshape(128, -1).astype(bf), "c_expm": expm.reshape(64, -1).astype(bf), "c_invf": invf,
            "c_jrow": jrow}


def host_inputs(cfg, inputs):
    c = cfg
    consts = host_constants(c)
    x = np.ascontiguousarray(np.asarray(inputs["x"], np.float32).reshape(c.S, c.D))
    pos = np.ascontiguousarray(np.asarray(inputs["positions"], np.int32).reshape(c.S, 1))
    idx_all = np.arange(c.S)
    info_all = np.stack([idx_all, idx_all // 64, idx_all // 256, np.zeros_like(idx_all)], 1).astype(np.float32)
    row_all = idx_all.reshape(c.NT, 128).astype(np.float32)
    maps = []
    for core in range(c.NC):
        tiles = [c.NC * j + core for j in range(c.NTO)]
        own = np.concatenate([np.arange(t * 128, (t + 1) * 128) for t in tiles])
        m = dict(consts)
        m["x"] = x
        m["pos"] = pos
        m["ownpos"] = np.ascontiguousarray(pos[own])
        oh = np.zeros((128, c.NC), np.float32)
        oh[:, core] = 1.0
        m["onehot"] = oh
        m["info_all"] = info_all
        m["info_own"] = np.ascontiguousarray(info_all[own])
        m["row_all"] = row_all
        m["row_own"] = np.ascontiguousarray(own.reshape(c.NTO, 128).astype(np.float32))
        for k, v in inputs.items():
            if k in ("x", "positions"):
                continue
            m[k] = np.ascontiguousarray(np.asarray(v, np.float32))
        maps.append(m)
    return maps


def run(cfg, inputs, debug=(), trace=False):
    nc = build(cfg, debug)
    maps = host_inputs(cfg, inputs)
    res = run_bass_kernel_spmd(nc, maps, core_ids=list(range(cfg.NC)), trace=trace)
    return res


N_CORES_USED = 8


def kernel(**inputs):
    cfg = Cfg(L=1, NC=N_CORES_USED)
    depth = int(np.asarray(inputs["w_in"]).shape[0])
    nc = build(cfg)
    x = np.ascontiguousarray(np.asarray(inputs["x"], np.float32).reshape(cfg.S, cfg.D))
    for l in range(depth):
        inp = {}
        for k, v in inputs.items():
            if k == "x":
                inp[k] = x
            elif k == "positions":
                inp[k] = v
            else:
                inp[k] = np.asarray(v)[l:l + 1]
        maps = host_inputs(cfg, inp)
        res = run_bass_kernel_spmd(nc, maps, core_ids=list(range(cfg.NC)))
        out = np.empty((cfg.S, cfg.D), np.float32)
        for core in range(cfg.NC):
            y = res.results[core]["y"]
            for j in range(cfg.NTO):
                t = cfg.NC * j + core
                out[t * 128:(t + 1) * 128] = y[j * 128:(j + 1) * 128]
        x = out
    return x.reshape(1, cfg.S, cfg.D)
```

```python
import math
import numpy as np
import ml_dtypes
from contextlib import ExitStack
import concourse.bass as bass
import concourse.mybir as mybir
from concourse.bass_utils import run_bass_kernel_spmd

F32 = mybir.dt.float32
BF16 = mybir.dt.bfloat16
I32 = mybir.dt.int32
AF = mybir.ActivationFunctionType
ALU = mybir.AluOpType
AX = mybir.AxisListType
EPS = 1e-6
NEGBIG = -1e30
TWO_PI = 2.0 * math.pi


class Cfg:
    def __init__(s, S=16384, L=2, DFF=16384, NC=8):
        s.D = 4096
        s.S, s.L, s.DFF, s.NC = S, L, DFF, NC
        s.HD = 128
        s.NH, s.NG, s.MH, s.SW = 16, 4, 8, 1024
        s.KC = s.D // 128
        s.NT = S // 128
        s.NTO = s.NT // NC
        s.NCMP = (S - 32) // 16 + 1
        s.NCC = (s.NCMP + 127) // 128
        s.NSLC = S // 64
        s.NMB = S // 256
        s.TG = 4
        o = {}
        off = 0
        for name, w in (("qa", 2048), ("kc", 512), ("vc", 512), ("ks", 512), ("vs", 512), ("kw", 512),
                        ("vw", 512), ("ga", 48), ("ub", 1024), ("vb", 1024), ("qm", 1024), ("km", 1024),
                        ("vm", 1024), ("gma", 4096), ("gmb", 4096), ("gmc", 4096)):
            o[name] = off
            off += w
        s.off = o
        s.INW = off
        assert s.INW == 22576
        assert s.NT % (NC * 1) == 0 and s.NMB >= 8 and s.NSLC >= 16


class Track:
    __slots__ = ("w", "r")

    def __init__(s):
        s.w = None
        s.r = {}


class Tl:
    def __init__(s, h):
        s.h = h
        s.tr = Track()

    def __getitem__(s, i):
        return s.h[i]


NDS = 8
NWBUF = 3
NPC = 5
SAME_ENGINE_SYNC = True


class PB:
    def __init__(s, nc, es):
        s.nc, s.es = nc, es
        s.E = {"pe": nc.tensor, "act": nc.scalar, "dve": nc.vector, "pool": nc.gpsimd, "sp": nc.sync}
        s.sems = []
        s.cidx, s.ccnt = {}, {}
        for e in ("pe", "act", "dve", "pool"):
            s.cidx[e] = len(s.sems)
            s.sems.append(es.enter_context(nc.semaphore("cs_" + e)))
            s.ccnt[e] = 0
        s.dq = {}
        for q in ("sp", "pool", "act"):
            idx = []
            for i in range(NDS):
                idx.append(len(s.sems))
                s.sems.append(es.enter_context(nc.semaphore(f"ds_{q}{i}")))
            s.dq[q] = {"idx": idx, "val": [0] * NDS, "nxt": 0}
        s.seen = {}
        s.nins = 0

    def wait(s, e, tok):
        k = (e, tok[0])
        if e == "pe" and tok[0] == s.cidx["pe"]:
            return
        if not SAME_ENGINE_SYNC and e in s.cidx and tok[0] == s.cidx[e]:
            return
        if s.seen.get(k, 0) >= tok[1]:
            return
        s.E[e].wait_ge(s.sems[tok[0]], tok[1])
        s.seen[k] = tok[1]

    def _deps(s, e, reads, writes):
        for t in reads:
            if t.w is not None:
                s.wait(e, t.w)
        for t in writes:
            if t.w is not None:
                s.wait(e, t.w)
            for tok in t.r.values():
                s.wait(e, tok)

    def _mark(s, tok, reads, writes):
        for t in reads:
            t.r[tok[0]] = tok
        for t in writes:
            t.w = tok
            t.r = {}

    @staticmethod
    def _tr(lst):
        return [x.tr if isinstance(x, Tl) else x for x in lst]

    def op(s, e, fns, reads=(), writes=()):
        reads, writes = s._tr(reads), s._tr(writes)
        s._deps(e, reads, writes)
        if not isinstance(fns, (list, tuple)):
            fns = [fns]
        ins = None
        for f in fns:
            ins = f(s.E[e])
            s.nins += 1
        s.ccnt[e] += 1
        tok = (s.cidx[e], s.ccnt[e])
        ins.then_inc(s.sems[tok[0]], 1)
        s._mark(tok, reads, writes)
        return tok

    def dma(s, q, out, in_, reads=(), writes=(), slow=False):
        reads, writes = s._tr(reads), s._tr(writes)
        d = s.dq[q]
        i = d["nxt"]
        d["nxt"] = (i + 1) % NDS
        si = d["idx"][i]
        if d["val"][i] > 0:
            s.wait(q, (si, d["val"][i]))
        s._deps(q, reads, writes)
        ins = s.E[q].dma_start(out=out, in_=in_, allow_slow_non_contiguous=True) if slow else s.E[q].dma_start(out=out, in_=in_)
        s.nins += 1
        d["val"][i] += 16
        tok = (si, d["val"][i])
        ins.then_inc(s.sems[si], 16)
        s._mark(tok, reads, writes)
        return tok

    def all_tokens(s):
        toks = [(s.cidx[e], s.ccnt[e]) for e in s.cidx if s.ccnt[e] > 0]
        for q in s.dq.values():
            for si, v in zip(q["idx"], q["val"]):
                if v > 0:
                    toks.append((si, v))
        return toks

    def barrier(s, engines=("pe", "act", "dve", "pool", "sp")):
        toks = s.all_tokens()
        for e in engines:
            for t in toks:
                s.wait(e, t)


class Scope:
    uid = 0

    def __init__(s, pb):
        s.pb = pb
        s.es = ExitStack()

    def __enter__(s):
        s.es.__enter__()
        return s

    def __exit__(s, *a):
        s.pb.barrier()
        return s.es.__exit__(*a)

    def sb(s, name, shape, dt):
        Scope.uid += 1
        return Tl(s.es.enter_context(s.pb.nc.sbuf_tensor(f"s{Scope.uid}_" + name, list(shape), dt)))


def bc(ap, shape):
    return ap.broadcast_to(list(shape))


def build(cfg, debug=()):
    c = cfg
    D, S, L, DFF, NC, KC, TG = c.D, c.S, c.L, c.DFF, c.NC, c.KC, c.TG
    NT, NTO, NG, NH, MH = c.NT, c.NTO, c.NG, c.NH, c.MH
    TOK = TG * 128
    nc = bass.Bass("TRN2", target_bir_lowering=False)

    def din(name, shape, dt=F32):
        return nc.dram_tensor(name, list(shape), dt, kind="ExternalInput").ap()

    def dscr(name, shape, dt):
        return nc.dram_tensor(name, list(shape), dt, kind="Internal").ap()

    x_in = din("x", [S, D])
    pos_in = din("pos", [S, 1], I32)
    ownpos_in = din("ownpos", [NTO * 128, 1], I32)
    onehot_in = din("onehot", [128, NC])
    info_all = din("info_all", [NT * 128, 4])
    info_own = din("info_own", [NTO * 128, 4])
    row_all = din("row_all", [NT, 128])
    row_own = din("row_own", [NTO, 128])
    W = {}
    for nm, shp in (("norm_mix", [L, D]), ("norm_mlp", [L, D]), ("w_in", [L, D, c.INW]), ("nsa_gate_b", [L, 48]),
                    ("nsa_q_norm", [L, 128]), ("nsa_kc_norm", [L, 128]), ("nsa_ks_norm", [L, 128]),
                    ("nsa_kw_norm", [L, 128]), ("phi_pe_k", [L, 32, 128]), ("phi_w1_k", [L, 4096, 512]),
                    ("phi_w2_k", [L, 512, 128]), ("phi_pe_v", [L, 32, 128]), ("phi_w1_v", [L, 4096, 512]),
                    ("phi_w2_v", [L, 512, 128]), ("sgu_norm", [L, 1024]), ("sgu_w", [L, 8, 128, 128]),
                    ("sgu_b", [L, 8, 128]), ("moba_q_norm", [L, 128]), ("moba_k_norm", [L, 128]),
                    ("proj_a", [L, 2048, D]), ("proj_b", [L, 1024, D]), ("proj_c", [L, 1024, D]),
                    ("w_out", [L, D, D]), ("mlp_w1", [L, D, DFF]), ("mlp_w2", [L, DFF, D])):
        W[nm] = din(nm, shp)
    c_identb = din("c_identb", [128, 128], BF16)
    c_identf = din("c_identf", [128, 128])
    c_tril = din("c_tril", [128, 128])
    c_ov = din("c_ov", [c.NCC * 128, 256], BF16)
    c_exps = din("c_exps", [128, 64 * 128], BF16)
    c_expm = din("c_expm", [64, 64 * 128], BF16)
    c_invf = din("c_invf", [1, 64])
    c_jrow = din("c_jrow", [1, 256])
    y_out = nc.dram_tensor("y", [NTO * 128, D], F32, kind="ExternalOutput").ap()
    dbg = {}
    for nm, shp, dt in debug:
        dbg[nm] = nc.dram_tensor("dbg_" + nm, list(shp), dt, kind="ExternalOutput").ap()

    wb = {"w_in": dscr("wb_w_in", [D, c.INW], BF16), "phi_w1_k": dscr("wb_p1k", [4096, 512], BF16),
          "phi_w1_v": dscr("wb_p1v", [4096, 512], BF16), "phi_w2_k": dscr("wb_p2k", [512, 128], BF16),
          "phi_w2_v": dscr("wb_p2v", [512, 128], BF16), "proj_a": dscr("wb_pa", [2048, D], BF16),
          "proj_b": dscr("wb_pb", [1024, D], BF16), "proj_c": dscr("wb_pc", [1024, D], BF16),
          "w_out": dscr("wb_wo", [D, D], BF16), "mlp_w1": dscr("wb_w1", [D, DFF], BF16),
          "mlp_w2": dscr("wb_w2", [DFF, D], BF16)}
    wb_tr = {k: Track() for k in wb}
    xs = [None] + [dscr(f"xs{l}", [S, D], F32) for l in range(1, L)]
    xs_tr = [None] + [Track() for _ in range(1, L)]
    kcrawT = dscr("kcrawT", [128, NG, S], BF16)
    vcrawT = dscr("vcrawT", [128, NG, S], BF16)
    ksT = dscr("ksT", [128, NG, S], BF16)
    kwT = dscr("kwT", [128, NG, S], BF16)
    kmT = dscr("kmT", [128, MH, S], BF16)
    vs_s = dscr("vs_s", [S, NG, 128], BF16)
    vw_s = dscr("vw_s", [S, NG, 128], BF16)
    vm_s = dscr("vm_s", [S, MH, 128], BF16)
    kcT_s = dscr("kcT_s", [128, NG, c.NCC * 128], BF16)
    vc_s = dscr("vc_s", [c.NCC * 128, NG, 128], BF16)
    kmeanT_s = dscr("kmeanT_s", [128, MH, c.NMB], BF16)
    kv_tr = Track()
    hT_s = dscr("hT_s", [128, KC, TOK], BF16)
    qnT_s = dscr("qnT_s", [128, NH, TOK], BF16)
    qrT_s = dscr("qrT_s", [128, NH, TOK], BF16)
    qmT_s = dscr("qmT_s", [128, MH, TOK], BF16)
    oT_s = dscr("oT_s", [128, KC, TOK], BF16)
    xg_s = dscr("xg_s", [TG, 128, D], F32)
    grp_tr = Track()

    es = ExitStack()
    with es:
        pb = PB(nc, es)

        def gsb(name, shape, dt):
            return Tl(es.enter_context(nc.sbuf_tensor("g_" + name, list(shape), dt)))

        PS = [Tl(es.enter_context(nc.psum_tensor(f"ps{i}", [128, 512], F32))) for i in range(8)]
        ps_rr = [0]

        def psn():
            t = PS[ps_rr[0] % 8]
            ps_rr[0] += 1
            return t

        identb = gsb("identb", [128, 128], BF16)
        identf = gsb("identf", [128, 128], F32)
        onesf = gsb("onesf", [128, 128], F32)
        invf = gsb("invf", [128, 64], F32)
        pcol = gsb("pcol", [128, 1], F32)
        onehot = gsb("onehot", [128, NC], F32)
        gains = gsb("gains", [128, 6, 128], F32)
        kcg_col = gsb("kcg_col", [128, 1], F32)
        gbias = gsb("gbias", [128, 48], F32)
        sgng = gsb("sgng", [128, 1024], F32)
        sgwT = gsb("sgwT", [128, 8, 128], BF16)
        sgb = gsb("sgb", [128, 8], F32)
        pek = gsb("pek", [128, 32], F32)
        pev = gsb("pev", [128, 32], F32)
        gmix = gsb("gmix", [128, KC], F32)
        gmlp = gsb("gmlp", [128, KC], F32)
        pb.dma("sp", identb[:], c_identb[:, :], writes=[identb])
        pb.dma("sp", identf[:], c_identf[:, :], writes=[identf])
        pb.dma("sp", invf[:], c_invf[0:1, :].partition_broadcast(128), writes=[invf])
        pb.dma("sp", onehot[:], onehot_in[:, :], writes=[onehot])
        pb.op("dve", lambda e: e.memset(onesf[:], 1.0), writes=[onesf])
        pb.op("pool", lambda e: e.iota(pcol[:], [[0, 1]], base=0, channel_multiplier=1,
                                       allow_small_or_imprecise_dtypes=True), writes=[pcol])

        def rope_tables(sc, pos_ap, tag):
            pi_ = sc.sb("rp_i" + tag, [128, 1], I32)
            pf = sc.sb("rp_f" + tag, [128, 1], F32)
            ang = sc.sb("rp_a" + tag, [128, 2, 64], F32)
            kf = sc.sb("rp_k" + tag, [128, 2, 64], F32)
            ki = sc.sb("rp_ki" + tag, [128, 2, 64], I32)
            cs = sc.sb("rp_cs" + tag, [128, 2, 64], F32)
            pb.dma("pool", pi_[:], pos_ap, writes=[pi_])
            pb.op("dve", lambda e: e.tensor_copy(out=pf[:], in_=pi_[:]), reads=[pi_], writes=[pf])
            pb.op("dve", lambda e: e.tensor_scalar(out=ang[:, 0, :], in0=invf[:], scalar1=pf[:, 0:1], scalar2=None,
                                                   op0=ALU.mult), reads=[invf, pf], writes=[ang])
            pb.op("dve", lambda e: e.tensor_scalar(out=ang[:, 1, :], in0=ang[:, 0, :], scalar1=math.pi / 2,
                                                   scalar2=None, op0=ALU.add), reads=[ang], writes=[ang])
            pb.op("dve", lambda e: e.tensor_scalar(out=kf[:], in0=ang[:], scalar1=1.0 / TWO_PI, scalar2=None,
                                                   op0=ALU.mult), reads=[ang], writes=[kf])
            pb.op("dve", lambda e: e.tensor_copy(out=ki[:], in_=kf[:]), reads=[kf], writes=[ki])
            pb.op("dve", lambda e: e.tensor_copy(out=kf[:], in_=ki[:]), reads=[ki], writes=[kf])
            C1 = 6.28125
            C2 = TWO_PI - C1
            pb.op("dve", lambda e: e.scalar_tensor_tensor(out=ang[:], in0=kf[:], scalar=-C1, in1=ang[:],
                                                          op0=ALU.mult, op1=ALU.add), reads=[kf, ang], writes=[ang])
            pb.op("dve", lambda e: e.scalar_tensor_tensor(out=ang[:], in0=kf[:], scalar=-C2, in1=ang[:],
                                                          op0=ALU.mult, op1=ALU.add), reads=[kf, ang], writes=[ang])
            pb.op("dve", lambda e: e.tensor_scalar(out=kf[:], in0=ang[:], scalar1=0.0, scalar2=TWO_PI,
                                                   op0=ALU.is_lt, op1=ALU.mult), reads=[ang], writes=[kf])
            pb.op("dve", lambda e: e.tensor_tensor(out=ang[:], in0=ang[:], in1=kf[:], op=ALU.add),
                  reads=[ang, kf], writes=[ang])
            pb.op("dve", lambda e: e.tensor_scalar(out=kf[:], in0=ang[:], scalar1=TWO_PI, scalar2=-TWO_PI,
                                                   op0=ALU.is_ge, op1=ALU.mult), reads=[ang], writes=[kf])
            pb.op("dve", lambda e: e.tensor_tensor(out=ang[:], in0=ang[:], in1=kf[:], op=ALU.add),
                  reads=[ang, kf], writes=[ang])
            pb.op("dve", lambda e: e.tensor_scalar(out=ang[:], in0=ang[:], scalar1=-1.0, scalar2=math.pi,
                                                   op0=ALU.mult, op1=ALU.add), reads=[ang], writes=[ang])
            pb.op("act", lambda e: e.activation(out=cs[:], in_=ang[:], func=AF.Sin), reads=[ang], writes=[cs])
            return cs

        def norm_T(sc, xt, hT, t, tag, xap=None):
            sq = sc_tmp["sq"]
            ss = sc.sb("nt_ss" + tag, [128, 1], F32)
            xb = sc_tmp["xb"]
            xap = xt[:] if xap is None else xap
            pb.op("act", lambda e: e.activation(out=sq[:], in_=xap, func=AF.Square), reads=[xt], writes=[sq])
            pb.op("dve", lambda e: e.tensor_reduce(out=ss[:], in_=sq[:], axis=AX.X, op=ALU.add), reads=[sq],
                  writes=[ss])
            rstd_from_ss(ss, ss, 1.0 / D)
            pb.op("act", lambda e: e.activation(out=xb[:], in_=xap, func=AF.Copy, scale=ss[:, 0:1]),
                  reads=[xt, ss], writes=[xb])
            transpose_to(xb, KC, lambda k0, n: hT[:, k0:k0 + n, t * 128:(t + 1) * 128], [hT])

        def rstd_from_ss(out, ss, inv_n):
            pb.op("dve", lambda e: e.tensor_scalar(out=out[:], in0=ss[:], scalar1=inv_n, scalar2=EPS, op0=ALU.mult,
                                                   op1=ALU.add), reads=[ss], writes=[out])
            pb.op("act", lambda e: e.activation(out=out[:], in_=out[:], func=AF.Sqrt), reads=[out], writes=[out])
            pb.op("dve", lambda e: e.reciprocal(out=out[:], in_=out[:]), reads=[out], writes=[out])

        def transpose_to(src, nblk, dst_fn, dst_tiles, src_fn=None, bank=None):
            k0 = 0
            i = 0
            while k0 < nblk:
                n = min(8, nblk - k0)
                pt = psn() if bank is None else bank
                ptb = pt[:].bitcast(BF16)
                fns = []
                for j in range(n):
                    sap = src_fn(k0 + j) if src_fn else src[:, (k0 + j) * 128:(k0 + j + 1) * 128]
                    fns.append(lambda e, j=j, sap=sap: e.transpose(out=ptb[:, j * 128:(j + 1) * 128], in_=sap,
                                                                   identity=identb[:]))
                pb.op("pe", fns, reads=[src, identb], writes=[pt])
                dst = dst_fn(k0, n)
                srcv = ptb[:, 0:n * 128].rearrange("p (n f) -> p n f", f=128)
                eng = "act" if (i % 2 == 0) else "dve"
                if eng == "act":
                    pb.op("act", lambda e: e.copy(out=dst, in_=srcv), reads=[pt], writes=dst_tiles)
                else:
                    pb.op("dve", lambda e: e.tensor_copy(out=dst, in_=srcv), reads=[pt], writes=dst_tiles)
                k0 += n
                i += 1

        wbufs = []
        wb_rr = [0]

        def wload(wname, k0, nk, c0, ncols):
            t = wbufs[wb_rr[0] % len(wbufs)]
            wb_rr[0] += 1
            src = wb[wname][k0 * 128:(k0 + nk) * 128, c0:c0 + ncols].rearrange("(k p) n -> p k n", p=128)
            pb.dma("sp", t[:, 0:nk, 0:ncols], src, reads=[wb_tr[wname]], writes=[t])
            return t

        def gemm(wname, c0, ncols, act, kc0, nkc, ntile, mode, epi, wrow0=0):
            nout = ntile if mode == "tok" else ncols // 128
            pss = [psn() for _ in range(nout)]
            kk = 0
            while kk < nkc:
                nk = min(16, nkc - kk)
                wt = wload(wname, wrow0 + kk, nk, c0, ncols)
                for i in range(nout):
                    fns = []
                    for k in range(nk):
                        if mode == "tok":
                            l_ap = act[:, kc0 + kk + k, i * 128:(i + 1) * 128]
                            r_ap = wt[:, k, 0:ncols]
                            o_ap = pss[i][:, 0:ncols]
                        else:
                            l_ap = wt[:, k, i * 128:(i + 1) * 128]
                            r_ap = act[:, kc0 + kk + k, 0:ntile * 128]
                            o_ap = pss[i][:, 0:ntile * 128]
                        fns.append(lambda e, l_ap=l_ap, r_ap=r_ap, o_ap=o_ap, st=(kk + k == 0),
                                   sp_=(kk + k == nkc - 1): e.matmul(o_ap, lhsT=l_ap, rhs=r_ap, start=st, stop=sp_))
                    pb.op("pe", fns, reads=[act, wt], writes=[pss[i]])
                kk += nk
            for i in range(nout):
                epi(i, pss[i])

        def precast(sc, l):
            st = [sc.sb(f"pc_f{i}", [128, 4096], F32) for i in range(NPC)]
            sb_ = [sc.sb(f"pc_b{i}", [128, 4096], BF16) for i in range(NPC)]
            pb.dma("sp", gmix[:], W["norm_mix"][l:l + 1, :].rearrange("o (k p) -> p (o k)", p=128), writes=[gmix], slow=True)
            pb.dma("sp", gmlp[:], W["norm_mlp"][l:l + 1, :].rearrange("o (k p) -> p (o k)", p=128), writes=[gmlp], slow=True)
            i = 0
            for nm in wb:
                src = W[nm][l]
                R, C = src.shape
                g = gmix if nm == "w_in" else (gmlp if nm == "mlp_w1" else None)
                for r in range(R // 128):
                    cc = 0
                    while cc < C:
                        n = min(4096, C - cc)
                        a, b = st[i % NPC], sb_[i % NPC]
                        pb.dma("sp", a[:, 0:n], src[r * 128:(r + 1) * 128, cc:cc + n], writes=[a])
                        eng = ("dve", "act", "dve", "act", "pool")[i % 5]
                        if g is None:
                            if eng == "act":
                                pb.op("act", lambda e, a=a, b=b, n=n: e.copy(out=b[:, 0:n], in_=a[:, 0:n]),
                                      reads=[a], writes=[b])
                            else:
                                pb.op(eng, lambda e, a=a, b=b, n=n: e.tensor_copy(out=b[:, 0:n], in_=a[:, 0:n]),
                                      reads=[a], writes=[b])
                        elif eng == "act":
                            pb.op("act", lambda e, a=a, b=b, n=n, r=r, g=g: e.activation(
                                out=b[:, 0:n], in_=a[:, 0:n], func=AF.Copy, scale=g[:, r:r + 1]),
                                reads=[a, g], writes=[b])
                        else:
                            pb.op(eng, lambda e, a=a, b=b, n=n, r=r, g=g: e.tensor_scalar(
                                out=b[:, 0:n], in0=a[:, 0:n], scalar1=g[:, r:r + 1], scalar2=None, op0=ALU.mult),
                                reads=[a, g], writes=[b])
                        pb.dma("pool", wb[nm][r * 128:(r + 1) * 128, cc:cc + n], b[:, 0:n], reads=[b],
                               writes=[wb_tr[nm]])
                        cc += n
                        i += 1

        def load_layer_params(sc, l):
            for gi, nm in enumerate(("nsa_q_norm", "nsa_kc_norm", "nsa_ks_norm", "nsa_kw_norm", "moba_q_norm",
                                     "moba_k_norm")):
                pb.dma("pool", gains[:, gi, :], W[nm][l:l + 1, :].partition_broadcast(128), writes=[gains])
            pb.op("dve", lambda e: e.tensor_scalar(out=gains[:, 0, :], in0=gains[:, 0, :], scalar1=128 ** -0.5,
                                                   scalar2=None, op0=ALU.mult), reads=[gains], writes=[gains])
            pb.op("dve", lambda e: e.tensor_scalar(out=gains[:, 4, :], in0=gains[:, 4, :], scalar1=128 ** -0.5,
                                                   scalar2=None, op0=ALU.mult), reads=[gains], writes=[gains])
            pb.dma("pool", kcg_col[:], W["nsa_kc_norm"][l:l + 1, :].rearrange("o p -> p o"), writes=[kcg_col], slow=True)
            pb.dma("pool", gbias[:], W["nsa_gate_b"][l:l + 1, :].partition_broadcast(128), writes=[gbias])
            pb.dma("pool", sgng[:], W["sgu_norm"][l:l + 1, :].partition_broadcast(128), writes=[sgng])
            pb.dma("pool", sgb[:], W["sgu_b"][l].rearrange("g t -> t g"), writes=[sgb], slow=True)
            pb.dma("pool", pek[:], W["phi_pe_k"][l].rearrange("l d -> d l"), writes=[pek], slow=True)
            pb.dma("pool", pev[:], W["phi_pe_v"][l].rearrange("l d -> d l"), writes=[pev], slow=True)
            tril = sc.sb("lp_tril", [128, 128], F32)
            wtmp = sc.sb("lp_w", [128, 128], F32)
            pb.dma("pool", tril[:], c_tril[:, :], writes=[tril])
            for g in range(8):
                pb.dma("pool", wtmp[:], W["sgu_w"][l, g], writes=[wtmp])
                pb.op("dve", lambda e: e.tensor_tensor(out=wtmp[:], in0=wtmp[:], in1=tril[:], op=ALU.mult),
                      reads=[wtmp, tril], writes=[wtmp])
                pt = psn()
                pb.op("pe", lambda e: e.transpose(out=pt[:, 0:128], in_=wtmp[:], identity=identf[:]),
                      reads=[wtmp, identf], writes=[pt])
                pb.op("dve", lambda e, g=g: e.tensor_copy(out=sgwT[:, g, :], in_=pt[:, 0:128]), reads=[pt],
                      writes=[sgwT])

        sc_tmp = {}

        def headnorm_rope(sc, ps, nheads, gain_idx, cs, out_rot, out_plain, tag):
            W_ = nheads * 128
            sq = sc_tmp["hn_sq"]
            xn = sc_tmp["hn_xn"]
            ss = sc.sb("hn_ss" + tag, [128, 4], F32)
            pb.op("act", lambda e: e.activation(out=sq[:, 0:W_], in_=ps[:, 0:W_], func=AF.Square), reads=[ps],
                  writes=[sq])
            pb.op("dve", lambda e: e.tensor_reduce(out=ss[:, 0:nheads],
                                                   in_=sq[:, 0:W_].rearrange("p (h d) -> p h d", d=128),
                                                   axis=AX.X, op=ALU.add), reads=[sq], writes=[ss])
            rstd_from_ss(ss, ss, 1.0 / 128)
            xn3 = xn[:, 0:W_].rearrange("p (h d) -> p h d", d=128)
            pb.op("dve", lambda e: e.tensor_tensor(out=xn3, in0=ps[:, 0:W_].rearrange("p (h d) -> p h d", d=128),
                                                   in1=bc(ss[:, 0:nheads].unsqueeze(2), [128, nheads, 128]),
                                                   op=ALU.mult), reads=[ps, ss], writes=[xn])
            gr = bc(gains[:, gain_idx:gain_idx + 1, :], [128, nheads, 128])
            if out_plain is not None:
                pb.op("pool", lambda e: e.tensor_tensor(out=out_plain, in0=xn3, in1=gr, op=ALU.mult),
                      reads=[xn, gains], writes=[sc_tmp["hn_outp"]])
            if out_rot is not None:
                pb.op("dve", lambda e: e.tensor_tensor(out=xn3, in0=xn3, in1=gr, op=ALU.mult), reads=[xn, gains],
                      writes=[xn])
                t1 = sc_tmp["hn_t1"]
                t2 = sc_tmp["hn_t2"]
                x1 = xn3[:, :, 0:64]
                x2 = xn3[:, :, 64:128]
                cosb = bc(cs[:, 1:2, :], [128, nheads, 64])
                sinb = bc(cs[:, 0:1, :], [128, nheads, 64])
                t1v = t1[:, 0:nheads * 64].rearrange("p (h d) -> p h d", d=64)
                t2v = t2[:, 0:nheads * 64].rearrange("p (h d) -> p h d", d=64)
                pb.op("dve", lambda e: e.tensor_tensor(out=t1v, in0=x1, in1=cosb, op=ALU.mult), reads=[xn, cs],
                      writes=[t1])
                pb.op("pool", lambda e: e.tensor_tensor(out=t2v, in0=x2, in1=sinb, op=ALU.mult), reads=[xn, cs],
                      writes=[t2])
                pb.op("dve", lambda e: e.tensor_tensor(out=out_rot[:, :, 0:64], in0=t1v, in1=t2v, op=ALU.subtract),
                      reads=[t1, t2], writes=[sc_tmp["hn_outr"]])
                pb.op("dve", lambda e: e.tensor_tensor(out=t1v, in0=x2, in1=cosb, op=ALU.mult), reads=[xn, cs],
                      writes=[t1])
                pb.op("pool", lambda e: e.tensor_tensor(out=t2v, in0=x1, in1=sinb, op=ALU.mult), reads=[xn, cs],
                      writes=[t2])
                pb.op("dve", lambda e: e.tensor_tensor(out=out_rot[:, :, 64:128], in0=t1v, in1=t2v, op=ALU.add),
                      reads=[t1, t2], writes=[sc_tmp["hn_outr"]])

        def alloc_common_tmps(sc):
            sc_tmp["xb"] = sc.sb("t_xb", [128, D], BF16)
            sc_tmp["sq"] = sc_tmp["xb"]
            sc_tmp["hn_sq"] = sc.sb("t_hnsq", [128, 512], F32)
            sc_tmp["hn_xn"] = sc.sb("t_hnxn", [128, 512], F32)
            sc_tmp["hn_t1"] = sc.sb("t_hnt1", [128, 256], F32)
            sc_tmp["hn_t2"] = sc.sb("t_hnt2", [128, 256], F32)
            sc_tmp["hn_outp"] = sc.sb("t_hnop", [128, 512], BF16)
            sc_tmp["hn_outr"] = sc.sb("t_hnor", [128, 512], BF16)
            wbufs.clear()
            for i in range(NWBUF):
                wbufs.append(sc.sb(f"wbuf{i}", [128, 16, 512], BF16))

        def load_x_tile(sc, l, xt, tile_idx):
            if l == 0:
                pb.dma("sp", xt[:], x_in[tile_idx * 128:(tile_idx + 1) * 128, :], writes=[xt])
            else:
                pb.dma("sp", xt[:], xs[l][tile_idx * 128:(tile_idx + 1) * 128, :], reads=[xs_tr[l]], writes=[xt])

        def kv_pass(l):
            with Scope(pb) as sc:
                alloc_common_tmps(sc)
                hT = sc.sb("kv_hT", [128, KC, TOK], BF16)
                xts = [sc.sb(f"kv_x{i}", [128, D], F32) for i in range(2)]
                stage = [sc.sb(f"kv_st{i}", [128, 4, 128], BF16) for i in range(2)]
                st_i = [0]
                for g0 in range(0, NT, TG):
                    sg_ = Scope(pb)
                    sg_.__enter__()
                    css = []
                    for t in range(TG):
                        xt = xts[t % 2]
                        load_x_tile(sc, l, xt, g0 + t)
                        norm_T(sg_, xt, hT, t, f"kv{t % 2}")
                        css.append(rope_tables(sg_, pos_in[(g0 + t) * 128:(g0 + t + 1) * 128, :], f"kv{t}"))

                    def epi_factory(kind, dstT, dstV, h0, gain_idx):
                        def epi(t, ps):
                            tok0 = (g0 + t) * 128
                            if kind == "v":
                                sg = stage[st_i[0] % 2]
                                st_i[0] += 1
                                pb.op("act", lambda e: e.copy(out=sg[:].rearrange("p h d -> p (h d)"), in_=ps[:]),
                                      reads=[ps], writes=[sg])
                                pb.dma("pool", dstV[tok0:tok0 + 128, h0:h0 + 4, :], sg[:], reads=[sg],
                                       writes=[kv_tr])
                                return
                            src = sc_tmp["hn_outr"]
                            if kind == "kraw":
                                pb.op("act", lambda e: e.copy(out=src[:], in_=ps[:]), reads=[ps], writes=[src])
                            else:
                                headnorm_rope(sg_, ps, 4, gain_idx, css[t],
                                              src[:].rearrange("p (h d) -> p h d", d=128), None, "kv")
                            sg = stage[st_i[0] % 2]
                            st_i[0] += 1
                            transpose_to(src, 4, lambda k0, n: sg[:, k0:k0 + n, :], [sg])
                            pb.dma("pool", dstT[:, h0:h0 + 4, tok0:tok0 + 128], sg[:], reads=[sg], writes=[kv_tr])
                        return epi

                    o = c.off
                    plan = [("kraw", o["kc"], kcrawT, None, 0, 0), ("kraw", o["vc"], vcrawT, None, 0, 0),
                            ("knr", o["ks"], ksT, None, 0, 2), ("v", o["vs"], None, vs_s, 0, 0),
                            ("knr", o["kw"], kwT, None, 0, 3), ("v", o["vw"], None, vw_s, 0, 0),
                            ("knr", o["km"], kmT, None, 0, 5), ("knr", o["km"] + 512, kmT, None, 4, 5),
                            ("v", o["vm"], None, vm_s, 0, 0), ("v", o["vm"] + 512, None, vm_s, 4, 0)]
                    for kind, c0, dT, dV, h0, gi in plan:
                        gemm("w_in", c0, 512, hT, 0, KC, TG, "tok", epi_factory(kind, dT, dV, h0, gi))
                    sg_.__exit__(None, None, None)

        def dump(name, src_ap, tracks):
            if name in dbg:
                pb.dma("pool", dbg[name], src_ap, reads=tracks)

        ga_sb = gsb("ga_sb", [128, TG, 48], F32)
        p16col = gsb("p16col", [128, 1], F32)
        pb.op("dve", lambda e: e.tensor_scalar(out=p16col[:], in0=pcol[:], scalar1=16.0, scalar2=None, op0=ALU.mult),
              reads=[pcol], writes=[p16col])
        yT_s = dscr("yT_s", [128, KC, TOK], BF16)

        def alloc_wbufs(sc):
            wbufs.clear()
            for i in range(NWBUF):
                wbufs.append(sc.sb(f"wbuf{i}", [128, 16, 512], BF16))

        def gelu_from_psum(ps, W_, out_ap, out_tiles, tmp):
            x2, inner, sg = tmp
            pb.op("act", lambda e: e.activation(out=x2[:, 0:W_], in_=ps[:, 0:W_], func=AF.Square), reads=[ps],
                  writes=[x2])
            pb.op("dve", lambda e: e.tensor_scalar(out=inner[:, 0:W_], in0=x2[:, 0:W_], scalar1=0.044715, scalar2=1.0,
                                                   op0=ALU.mult, op1=ALU.add), reads=[x2], writes=[inner])
            pb.op("dve", lambda e: e.tensor_tensor(out=inner[:, 0:W_], in0=inner[:, 0:W_], in1=ps[:, 0:W_],
                                                   op=ALU.mult), reads=[inner, ps], writes=[inner])
            pb.op("act", lambda e: e.activation(out=sg[:, 0:W_], in_=inner[:, 0:W_], func=AF.Sigmoid,
                                                scale=1.5957691216057308), reads=[inner], writes=[sg])
            pb.op("dve", lambda e: e.tensor_tensor(out=out_ap, in0=sg[:, 0:W_], in1=ps[:, 0:W_], op=ALU.mult),
                  reads=[sg, ps], writes=out_tiles)

        def compress(l):
            CW = min(512, c.NCC * 128)
            with Scope(pb) as sc:
                kr = sc.sb("cp_kr", [128, S], BF16)
                klT = sc.sb("cp_kl", [128, 32, CW], BF16)
                w1 = sc.sb("cp_w1", [128, 32, 512], BF16)
                w2 = sc.sb("cp_w2", [128, 4, 128], BF16)
                g1T = sc.sb("cp_g1", [128, 4, CW], BF16)
                tmp = [sc.sb(f"cp_t{i}", [128, 512], F32) for i in range(3)]
                kst = sc.sb("cp_kst", [128, CW], BF16)
                vst = sc.sb("cp_vst", [128, 4, 128], BF16)
                pb.op("pool", lambda e: e.memset(klT[:], 0.0), writes=[klT])
                for which in ("k", "v"):
                    pe = pek if which == "k" else pev
                    n1, n2 = "phi_w1_" + which, "phi_w2_" + which
                    pb.dma("sp", w1[:], wb[n1].rearrange("(l d) h -> d l h", d=128), reads=[wb_tr[n1]], writes=[w1])
                    pb.dma("sp", w2[:], wb[n2].rearrange("(c h) d -> h c d", h=128), reads=[wb_tr[n2]], writes=[w2])
                    raw = kcrawT if which == "k" else vcrawT
                    for g in range(NG):
                        pb.dma("sp", kr[:], raw[:, g, :], reads=[kv_tr], writes=[kr])
                        for n0 in range(0, c.NCC * 128, CW):
                            nn = min(CW, c.NCMP - n0)
                            if nn < CW:
                                pb.op("pool", lambda e: e.memset(klT[:], 0.0), writes=[klT])
                            for lq in range(32):
                                src = kr[:, 16 * n0 + lq: 16 * n0 + lq + 16 * (nn - 1) + 1: 16]
                                pb.op(("dve", "pool")[lq % 2], lambda e, lq=lq, src=src: e.tensor_scalar(
                                    out=klT[:, lq, 0:nn], in0=src, scalar1=pe[:, lq:lq + 1], scalar2=None,
                                    op0=ALU.add), reads=[kr, pe], writes=[klT])
                            for hc in range(4):
                                ps = psn()
                                pb.op("pe", [lambda e, lq=lq, hc=hc, ps=ps: e.matmul(
                                    ps[:, 0:CW], lhsT=w1[:, lq, hc * 128:(hc + 1) * 128], rhs=klT[:, lq, :],
                                    start=(lq == 0), stop=(lq == 31)) for lq in range(32)], reads=[w1, klT], writes=[ps])
                                gelu_from_psum(ps, CW, g1T[:, hc, :], [g1T], tmp)
                            if which == "k":
                                ps = psn()
                                pb.op("pe", [lambda e, hc=hc, ps=ps: e.matmul(ps[:, 0:CW], lhsT=w2[:, hc, :],
                                                                             rhs=g1T[:, hc, :], start=(hc == 0),
                                                                             stop=(hc == 3)) for hc in range(4)],
                                      reads=[w2, g1T], writes=[ps])
                                x2, inner, _ = tmp
                                pb.op("act", lambda e: e.activation(out=x2[:, 0:CW], in_=ps[:, 0:CW], func=AF.Square),
                                      reads=[ps], writes=[x2])
                                ps2 = psn()
                                pb.op("pe", lambda e: e.matmul(ps2[:, 0:CW], lhsT=onesf[:], rhs=x2[:, 0:CW], start=True,
                                                               stop=True), reads=[onesf, x2], writes=[ps2])
                                rstd_from_ss(inner, ps2, 1.0 / 128)
                                pb.op("dve", lambda e: e.scalar_tensor_tensor(
                                    out=kst[:], in0=ps[:, 0:CW], scalar=kcg_col[:, 0:1], in1=inner[:, 0:CW],
                                    op0=ALU.mult, op1=ALU.mult), reads=[ps, kcg_col, inner], writes=[kst])
                                pb.dma("pool", kcT_s[:, g, n0:n0 + CW], kst[:], reads=[kst], writes=[kv_tr])
                            else:
                                for j in range(CW // 128):
                                    ps = psn()
                                    pb.op("pe", [lambda e, hc=hc, ps=ps, j=j: e.matmul(
                                        ps[:, 0:128], lhsT=g1T[:, hc, j * 128:(j + 1) * 128], rhs=w2[:, hc, :],
                                        start=(hc == 0), stop=(hc == 3)) for hc in range(4)], reads=[w2, g1T],
                                        writes=[ps])
                                    pb.op("act", lambda e, j=j, ps=ps: e.copy(out=vst[:, j, :], in_=ps[:, 0:128]),
                                          reads=[ps], writes=[vst])
                                pb.dma("pool", vc_s[n0:n0 + CW, g, :].rearrange("(j p) d -> p j d", p=128),
                                       vst[:, 0:CW // 128, :], reads=[vst], writes=[kv_tr])

        def kmean(l):
            with Scope(pb) as sc:
                kr = sc.sb("km_kr", [128, S], BF16)
                acc = sc.sb("km_acc", [128, c.NMB], F32)
                st = sc.sb("km_st", [128, MH, c.NMB], BF16)
                for h in range(MH):
                    pb.dma("sp", kr[:], kmT[:, h, :], reads=[kv_tr], writes=[kr])
                    pb.op("dve", lambda e: e.tensor_reduce(out=acc[:], in_=kr[:].rearrange("p (b k) -> p b k", k=256),
                                                           axis=AX.X, op=ALU.add), reads=[kr], writes=[acc])
                    pb.op("dve", lambda e, h=h: e.tensor_scalar(out=st[:, h, :], in0=acc[:], scalar1=1.0 / 256,
                                                                scalar2=None, op0=ALU.mult), reads=[acc], writes=[st])
                pb.dma("pool", kmeanT_s[:, :, :], st[:], reads=[st], writes=[kv_tr])

        def phase_a(l, last, qtiles):
            ntile = len(qtiles)
            src_x = x_in if l == 0 else xs[l]
            src_tr = [] if l == 0 else [xs_tr[l]]
            o = c.off
            with Scope(pb) as sc:
                alloc_common_tmps(sc)
                hT = sc.sb("a_hT", [128, KC, TOK], BF16)
                xt = sc.sb("a_x", [128, D], F32)
                xtmp = sc.sb("a_xtmp", [128, D], F32) if last else None
                gu = sc.sb("a_gu", [128, TG, 1024], BF16)
                gvn = sc.sb("a_gvn", [128, TG, 1024], BF16)
                gv = sc.sb("a_gv", [128, 512], F32)
                ob = sc.sb("a_ob", [128, 1024], BF16)
                tmp = [sc.sb(f"a_t{i}", [128, 512], F32) for i in range(3)]
                stg = [sc.sb(f"a_stg{i}", [128, 8, 128], BF16) for i in range(2)]
                ss4 = sc.sb("a_ss4", [128, 4], F32)
                stg_i = [0]
                css = []
                for t, qi in enumerate(qtiles):
                    if not last:
                        pb.dma("sp", xt[:], src_x[qi * 128:(qi + 1) * 128, :], reads=src_tr, writes=[xt])
                    else:
                        for m in range(NC):
                            ti = NC * qi + m
                            pb.dma("sp", xtmp[:], src_x[ti * 128:(ti + 1) * 128, :], reads=src_tr, writes=[xtmp])
                            if m == 0:
                                pb.op("dve", lambda e: e.tensor_scalar(out=xt[:], in0=xtmp[:], scalar1=onehot[:, 0:1],
                                                                       scalar2=None, op0=ALU.mult),
                                      reads=[xtmp, onehot], writes=[xt])
                            else:
                                pb.op("dve", lambda e, m=m: e.scalar_tensor_tensor(
                                    out=xt[:], in0=xtmp[:], scalar=onehot[:, m:m + 1], in1=xt[:], op0=ALU.mult,
                                    op1=ALU.add), reads=[xtmp, onehot, xt], writes=[xt])
                    pb.dma("pool", xg_s[t], xt[:], reads=[xt], writes=[grp_tr])
                    norm_T(sc, xt, hT, t, f"a{t}")
                    pos_ap = (ownpos_in if last else pos_in)[qi * 128:(qi + 1) * 128, :]
                    css.append(rope_tables(sc, pos_ap, f"a{t}"))
                pb.dma("pool", hT_s[:, :, :], hT[:], reads=[hT], writes=[grp_tr])

                def q_epi(dst_plain, dst_rot, h0, gain_idx):
                    def epi(t, ps):
                        outp = sc_tmp["hn_outp"]
                        outr = sc_tmp["hn_outr"]
                        headnorm_rope(sc, ps, 4, gain_idx, css[t], outr[:].rearrange("p (h d) -> p h d", d=128),
                                      outp[:].rearrange("p (h d) -> p h d", d=128) if dst_plain is not None else None,
                                      "a")
                        for srct, dst in ((outp, dst_plain), (outr, dst_rot)):
                            if dst is None:
                                continue
                            sg = stg[stg_i[0] % 2]
                            stg_i[0] += 1
                            transpose_to(srct, 4, lambda k0, n, sg=sg: sg[:, k0:k0 + n, :], [sg])
                            pb.dma("pool", dst[:, h0:h0 + 4, t * 128:(t + 1) * 128], sg[:, 0:4, :], reads=[sg],
                                   writes=[grp_tr])
                    return epi

                for cq in range(4):
                    gemm("w_in", o["qa"] + cq * 512, 512, hT, 0, KC, ntile, "tok", q_epi(qnT_s, qrT_s, cq * 4, 0))
                for cq in range(2):
                    gemm("w_in", o["qm"] + cq * 512, 512, hT, 0, KC, ntile, "tok", q_epi(None, qmT_s, cq * 4, 4))

                def ga_epi(t, ps):
                    pb.op("dve", lambda e: e.tensor_tensor(out=ga_sb[:, t, :], in0=ps[:, 0:48], in1=gbias[:],
                                                           op=ALU.add), reads=[ps, gbias], writes=[ga_sb])
                    pb.op("act", lambda e: e.activation(out=ga_sb[:, t, :], in_=ga_sb[:, t, :], func=AF.Sigmoid),
                          reads=[ga_sb], writes=[ga_sb])
                gemm("w_in", o["ga"], 48, hT, 0, KC, ntile, "tok", ga_epi)

                def u_epi(cu):
                    def epi(t, ps):
                        gelu_from_psum(ps, 512, gu[:, t, cu * 512:(cu + 1) * 512], [gu], tmp)
                    return epi

                def v_epi(cv):
                    def epi(t, ps):
                        gelu_from_psum(ps, 512, gv[:], [gv], tmp)
                        x2 = tmp[0]
                        pb.op("act", lambda e: e.activation(out=x2[:], in_=gv[:], func=AF.Square), reads=[gv],
                              writes=[x2])
                        pb.op("dve", lambda e: e.tensor_reduce(out=ss4[:], in_=x2[:].rearrange("p (h d) -> p h d", d=128),
                                                               axis=AX.X, op=ALU.add), reads=[x2], writes=[ss4])
                        rstd_from_ss(ss4, ss4, 1.0 / 128)
                        gv3 = gv[:].rearrange("p (h d) -> p h d", d=128)
                        pb.op("dve", lambda e: e.tensor_tensor(out=gv3, in0=gv3,
                                                               in1=bc(ss4[:, 0:4].unsqueeze(2), [128, 4, 128]),
                                                               op=ALU.mult), reads=[gv, ss4], writes=[gv])
                        pb.op("pool", lambda e: e.tensor_tensor(out=gvn[:, t, cv * 512:(cv + 1) * 512], in0=gv[:],
                                                                in1=sgng[:, cv * 512:(cv + 1) * 512], op=ALU.mult),
                              reads=[gv, sgng], writes=[gvn])
                    return epi

                for cu in range(2):
                    gemm("w_in", o["ub"] + cu * 512, 512, hT, 0, KC, ntile, "tok", u_epi(cu))
                for cv in range(2):
                    gemm("w_in", o["vb"] + cv * 512, 512, hT, 0, KC, ntile, "tok", v_epi(cv))
                for t in range(ntile):
                    for half in range(2):
                        ps = psn()
                        pb.op("pe", [lambda e, g=g, ps=ps: e.matmul(
                            ps[:, (g % 4) * 128:(g % 4 + 1) * 128], lhsT=sgwT[:, g, :],
                            rhs=gvn[:, t, g * 128:(g + 1) * 128], start=True, stop=True)
                            for g in range(half * 4, half * 4 + 4)], reads=[sgwT, gvn], writes=[ps])
                        for g in range(half * 4, half * 4 + 4):
                            pb.op("dve", lambda e, g=g, ps=ps: e.scalar_tensor_tensor(
                                out=ob[:, g * 128:(g + 1) * 128], in0=ps[:, (g % 4) * 128:(g % 4 + 1) * 128],
                                scalar=sgb[:, g:g + 1], in1=gu[:, t, g * 128:(g + 1) * 128], op0=ALU.add,
                                op1=ALU.mult), reads=[ps, sgb, gu], writes=[ob])
                    sg = stg[stg_i[0] % 2]
                    stg_i[0] += 1
                    transpose_to(ob, 8, lambda k0, n, sg=sg: sg[:, k0:k0 + n, :], [sg])
                    pb.dma("pool", oT_s[:, 16:24, t * 128:(t + 1) * 128], sg[:], reads=[sg], writes=[grp_tr])
        def phase_b(l, last, qtiles):
            NS = c.NSLC
            NMB = c.NMB
            NCC = c.NCC
            SB = [PS[0], PS[1]]
            MB = PS[2]
            OB = [PS[3], PS[4], PS[5], PS[6]]
            XB = PS[7]
            with Scope(pb) as sc:
                exps = sc.sb("b_exps", [128, 64, 128], BF16)
                expm = sc.sb("b_expm", [64, 64, 128], BF16)
                ov = sc.sb("b_ov", [128, NCC, 256], BF16)
                kcT = sc.sb("b_kcT", [128, NG, NCC * 128], BF16)
                vcP = sc.sb("b_vcP", [128, NCC, NG, 129], BF16)
                kmeanT = sc.sb("b_kmean", [128, MH, NMB], BF16)
                jrow = sc.sb("b_jrow", [128, 256], F32)
                pb.dma("sp", exps[:].rearrange("p a b -> p (a b)"), c_exps[:, :], writes=[exps])
                pb.dma("sp", expm[:].rearrange("p a b -> p (a b)"), c_expm[:, :], writes=[expm])
                pb.dma("sp", ov[:], c_ov.rearrange("(c p) j -> p c j", p=128), writes=[ov])
                pb.dma("sp", kcT[:], kcT_s[:, :, :], reads=[kv_tr], writes=[kcT])
                pb.op("dve", lambda e: e.memset(vcP[:], 1.0), writes=[vcP])
                for cc in range(NCC):
                    pb.dma("sp", vcP[:, cc, :, 0:128], vc_s[cc * 128:(cc + 1) * 128, :, :], reads=[kv_tr], writes=[vcP])
                pb.dma("sp", kmeanT[:], kmeanT_s[:, :, :], reads=[kv_tr], writes=[kmeanT])
                pb.dma("sp", jrow[:], c_jrow[0:1, :].partition_broadcast(128), writes=[jrow])
                qn = sc.sb("b_qn", [128, NH * 128], BF16)
                qr = sc.sb("b_qr", [128, NH * 128], BF16)
                qm = sc.sb("b_qm", [128, MH, 128], BF16)
                info = sc.sb("b_info", [128, 4], F32)
                trow = sc.sb("b_trow", [128, 128], F32)
                kbuf = [sc.sb(f"b_k{i}", [128, 4, 512], BF16) for i in range(2)]
                vbuf = [sc.sb(f"b_v{i}", [128, 4, 4, 129], BF16) for i in range(2)]
                for vb_ in vbuf:
                    pb.op("dve", lambda e, vb_=vb_: e.memset(vb_[:], 1.0), writes=[vb_])
                e_sb = [sc.sb(f"b_e{i}", [128, 512], F32) for i in range(2)]
                em = [sc.sb(f"b_em{i}", [128, 512], BF16) for i in range(2)]
                cmd = sc.sb("b_cmd", [128, 8, 128], F32)
                msk = sc.sb("b_msk", [128, 512], F32)
                wm = sc.sb("b_wm", [128, 128], F32)
                wm2 = sc.sb("b_wm2", [128, 128], F32)
                ecmp = sc.sb("b_ecmp", [128, NCC, 512], BF16)
                imp = sc.sb("b_imp", [128, 256], F32)
                dd = sc.sb("b_dd", [128, 256], F32)
                ff = sc.sb("b_ff", [128, 256], F32)
                al = sc.sb("b_al", [128, 256], F32)
                pen = sc.sb("b_pen", [128, 256], F32)
                score = sc.sb("b_score", [128, 256], F32)
                sc2 = sc.sb("b_sc2", [128, 256], F32)
                m8a = sc.sb("b_m8a", [128, 8], F32)
                m8b = sc.sb("b_m8b", [128, 8], F32)
                sel = sc.sb("b_sel", [128, 256], BF16)
                selT = sc.sb("b_selT", [128, 2, 128], BF16)
                pastm = sc.sb("b_pastm", [128, NMB], F32)
                ownm = sc.sb("b_ownm", [128, NMB], F32)
                penm = sc.sb("b_penm", [128, NMB], F32)
                msc = sc.sb("b_msc", [128, MH, NMB], F32)
                m8m = sc.sb("b_m8m", [128, MH, 8], F32)
                selm = sc.sb("b_selm", [128, MH, NMB], F32)
                selmb = sc.sb("b_selmb", [128, MH, 64], BF16)
                selTm = sc.sb("b_selTm", [64, MH, 128], BF16)
                oa = sc.sb("b_oa", [128, NH * 128], F32)
                oab = sc.sb("b_oab", [128, NH * 128], BF16)
                om = sc.sb("b_om", [128, MH * 128], BF16)
                sm = sc.sb("b_sm", [128, 4], F32)
                ostg = sc.sb("b_ostg", [128, 8, 128], BF16)
                pb.op("pool", lambda e: e.memset(sel[:], 0.0), writes=[sel])
                pb.op("pool", lambda e: e.memset(selmb[:], 0.0), writes=[selmb])
                ctr = [0]

                def finish_head(O, ga_col, dst_ap, dst_tile, first):
                    pb.op("dve", lambda e: e.tensor_scalar(out=sm[:, 0:1], in0=O[:, 128:129], scalar1=1e-30,
                                                           scalar2=None, op0=ALU.max), reads=[O], writes=[sm])
                    pb.op("dve", lambda e: e.reciprocal(out=sm[:, 1:2], in_=sm[:, 0:1]), reads=[sm], writes=[sm])
                    if ga_col is not None:
                        pb.op("dve", lambda e: e.tensor_tensor(out=sm[:, 2:3], in0=sm[:, 1:2], in1=ga_col, op=ALU.mult),
                              reads=[sm, ga_sb], writes=[sm])
                        coef = sm[:, 2:3]
                    else:
                        coef = sm[:, 1:2]
                    if first:
                        pb.op("dve", lambda e: e.tensor_scalar(out=dst_ap, in0=O[:, 0:128], scalar1=coef, scalar2=None,
                                                               op0=ALU.mult), reads=[O, sm], writes=[dst_tile])
                    else:
                        pb.op("dve", lambda e: e.scalar_tensor_tensor(out=dst_ap, in0=O[:, 0:128], scalar=coef,
                                                                      in1=dst_ap, op0=ALU.mult, op1=ALU.add),
                              reads=[O, sm, dst_tile], writes=[dst_tile])

                for t, qi in enumerate(qtiles):
                    if not last:
                        imax = imin = qi
                        ndiag = 1
                    else:
                        imax = NC * qi + NC - 1
                        imin = NC * qi
                        ndiag = NC
                    nk = imax + 1
                    w0 = max(0, imin - 4)
                    ncc = min(NCC, (8 * imax + 7 + 127) // 128)
                    inf_src = info_own if last else info_all
                    row_src = row_own if last else row_all
                    pb.dma("sp", qn[:].rearrange("p (h q) -> p h q", q=128), qnT_s[:, :, t * 128:(t + 1) * 128],
                           reads=[grp_tr], writes=[qn])
                    pb.dma("sp", qr[:].rearrange("p (h q) -> p h q", q=128), qrT_s[:, :, t * 128:(t + 1) * 128],
                           reads=[grp_tr], writes=[qr])
                    pb.dma("sp", qm[:], qmT_s[:, :, t * 128:(t + 1) * 128], reads=[grp_tr], writes=[qm])
                    pb.dma("pool", info[:], inf_src[qi * 128:(qi + 1) * 128, :], writes=[info])
                    pb.dma("pool", trow[:], row_src[qi:qi + 1, :].partition_broadcast(128), writes=[trow])
                    for i in range(ndiag):
                        kt = nk - ndiag + i
                        pb.op("pool", lambda e, i=i, kt=kt: e.tensor_scalar(
                            out=cmd[:, i, :], in0=trow[:], scalar1=float(-kt * 128), scalar2=pcol[:, 0:1], op0=ALU.add,
                            op1=ALU.is_ge), reads=[trow, pcol], writes=[cmd])

                    for g in range(NG):
                        qng = qn[:, g * 512:(g + 1) * 512]
                        qrg = qr[:, g * 512:(g + 1) * 512]
                        for cc in range(ncc):
                            Sb = SB[ctr[0] % 2]
                            eb = e_sb[ctr[0] % 2]
                            ctr[0] += 1
                            pb.op("pe", lambda e, cc=cc, Sb=Sb: e.matmul(Sb[:, 0:512], lhsT=kcT[:, g, cc * 128:(cc + 1) * 128],
                                                                       rhs=qng, start=True, stop=True),
                                  reads=[kcT, qn], writes=[Sb])
                            pb.op("act", lambda e, Sb=Sb, eb=eb: e.activation(out=eb[:], in_=Sb[:], func=AF.Exp),
                                  reads=[Sb], writes=[eb])
                            pb.op("pool", lambda e, cc=cc: e.tensor_scalar(
                                out=wm[:], in0=trow[:], scalar1=float(-(31 + 2048 * cc)), scalar2=p16col[:, 0:1],
                                op0=ALU.add, op1=ALU.is_ge), reads=[trow, p16col], writes=[wm])
                            pb.op("dve", lambda e, cc=cc, eb=eb: e.tensor_tensor(
                                out=ecmp[:, cc, :].rearrange("p (r q) -> p r q", q=128),
                                in0=eb[:].rearrange("p (r q) -> p r q", q=128),
                                in1=bc(wm[:].unsqueeze(1), [128, 4, 128]), op=ALU.mult), reads=[eb, wm], writes=[ecmp])
                        for r in range(4):
                            head = g * 4 + r
                            A = OB[r]
                            Bk = MB if r % 2 == 0 else XB
                            pb.op("pe", [lambda e, cc=cc, A=A: e.matmul(
                                A[:, 0:129], lhsT=ecmp[:, cc, r * 128:(r + 1) * 128], rhs=vcP[:, cc, g, :],
                                start=(cc == 0), stop=(cc == ncc - 1)) for cc in range(ncc)], reads=[ecmp, vcP],
                                writes=[A])
                            pb.op("pe", [lambda e, cc=cc, Bk=Bk: e.matmul(
                                Bk[:, 0:256], lhsT=ecmp[:, cc, r * 128:(r + 1) * 128], rhs=ov[:, cc, :],
                                start=(cc == 0), stop=(cc == ncc - 1)) for cc in range(ncc)], reads=[ecmp, ov],
                                writes=[Bk])
                            finish_head(A, ga_sb[:, t, head:head + 1], oa[:, head * 128:(head + 1) * 128], oa, True)
                            if r == 0:
                                pb.op("dve", lambda e, Bk=Bk: e.tensor_scalar(out=imp[:], in0=Bk[:, 0:256],
                                                                            scalar1=sm[:, 1:2], scalar2=None,
                                                                            op0=ALU.mult), reads=[Bk, sm], writes=[imp])
                            else:
                                pb.op("dve", lambda e, Bk=Bk: e.scalar_tensor_tensor(
                                    out=imp[:], in0=Bk[:, 0:256], scalar=sm[:, 1:2], in1=imp[:], op0=ALU.mult,
                                    op1=ALU.add), reads=[Bk, sm, imp], writes=[imp])
                        pb.op("dve", lambda e: e.tensor_scalar(out=dd[:, 0:NS], in0=jrow[:, 0:NS], scalar1=info[:, 1:2],
                                                               scalar2=None, op0=ALU.subtract), reads=[jrow, info],
                              writes=[dd])
                        pb.op("dve", lambda e: e.tensor_scalar(out=ff[:, 0:NS], in0=dd[:, 0:NS], scalar1=-1.0,
                                                               scalar2=None, op0=ALU.is_ge), reads=[dd], writes=[ff])
                        pb.op("dve", lambda e: e.scalar_tensor_tensor(out=ff[:, 0:NS], in0=dd[:, 0:NS], scalar=0.0,
                                                                      in1=ff[:, 0:NS], op0=ALU.is_le, op1=ALU.mult),
                              reads=[dd, ff], writes=[ff])
                        pb.op("dve", lambda e: e.memset(ff[:, 0:1], 1.0), reads=[ff], writes=[ff])
                        pb.op("dve", lambda e: e.tensor_single_scalar(out=al[:, 0:NS], in_=dd[:, 0:NS], scalar=0.0,
                                                                      op=ALU.is_le), reads=[dd], writes=[al])
                        pb.op("dve", lambda e: e.scalar_tensor_tensor(out=score[:, 0:NS], in0=ff[:, 0:NS], scalar=1e9,
                                                                      in1=imp[:, 0:NS], op0=ALU.mult, op1=ALU.add),
                              reads=[ff, imp], writes=[score])
                        pb.op("dve", lambda e: e.tensor_scalar(out=pen[:, 0:NS], in0=al[:, 0:NS], scalar1=1.0,
                                                               scalar2=1e30, op0=ALU.subtract, op1=ALU.mult),
                              reads=[al], writes=[pen])
                        pb.op("dve", lambda e: e.tensor_tensor(out=score[:, 0:NS], in0=score[:, 0:NS], in1=al[:, 0:NS],
                                                               op=ALU.mult), reads=[score, al], writes=[score])
                        pb.op("dve", lambda e: e.tensor_tensor(out=score[:, 0:NS], in0=score[:, 0:NS], in1=pen[:, 0:NS],
                                                               op=ALU.add), reads=[score, pen], writes=[score])
                        pb.op("dve", lambda e: e.max(out=m8a[:], in_=score[:, 0:NS]), reads=[score], writes=[m8a])
                        pb.op("dve", lambda e: e.match_replace(out=sc2[:, 0:NS], in_to_replace=m8a[:],
                                                               in_values=score[:, 0:NS], imm_value=-3.0e38),
                              reads=[score, m8a], writes=[sc2])
                        pb.op("dve", lambda e: e.max(out=m8b[:], in_=sc2[:, 0:NS]), reads=[sc2], writes=[m8b])
                        pb.op("dve", lambda e: e.scalar_tensor_tensor(out=sel[:, 0:NS], in0=score[:, 0:NS],
                                                                      scalar=m8b[:, 7:8], in1=al[:, 0:NS],
                                                                      op0=ALU.is_ge, op1=ALU.mult),
                              reads=[score, m8b, al], writes=[sel])
                        transpose_to(sel, 2, lambda k0, n: selT[:, k0:k0 + n, :], [selT], bank=XB)
                        for br in ("slc", "win"):
                            kT_src, v_src = (ksT, vs_s) if br == "slc" else (kwT, vw_s)
                            k_lo = 0 if br == "slc" else w0
                            for ktb in range(k_lo, nk, 4):
                                nkb = min(4, nk - ktb)
                                kb = kbuf[ctr[0] % 2]
                                vb = vbuf[ctr[0] % 2]
                                pb.dma("sp", kb[:, 0, 0:nkb * 128], kT_src[:, g, ktb * 128:(ktb + nkb) * 128],
                                       reads=[kv_tr], writes=[kb])
                                pb.dma("sp", vb[:, 0:nkb, 0, 0:128],
                                       v_src[ktb * 128:(ktb + nkb) * 128, g, :].rearrange("(k p) d -> p k d", p=128),
                                       reads=[kv_tr], writes=[vb])
                                for kk in range(nkb):
                                    kt = ktb + kk
                                    Sb = SB[ctr[0] % 2]
                                    eb = e_sb[ctr[0] % 2]
                                    emb = em[ctr[0] % 2]
                                    ctr[0] += 1
                                    pb.op("pe", lambda e, kk=kk, Sb=Sb, kb=kb: e.matmul(
                                        Sb[:, 0:512], lhsT=kb[:, 0, kk * 128:(kk + 1) * 128], rhs=qrg, start=True,
                                        stop=True), reads=[kb, qr], writes=[Sb])
                                    pb.op("act", lambda e, Sb=Sb, eb=eb: e.activation(out=eb[:], in_=Sb[:], func=AF.Exp),
                                          reads=[Sb], writes=[eb])
                                    if br == "slc":
                                        Mk = MB if ctr[0] % 2 == 0 else XB
                                        pb.op("pe", lambda e, kt=kt, Mk=Mk: e.matmul(Mk[:, 0:128], lhsT=exps[:, kt % 64, :],
                                                                              rhs=selT[:, kt // 64, :], start=True,
                                                                              stop=True), reads=[exps, selT], writes=[Mk])
                                        if kt >= nk - ndiag:
                                            di = kt - (nk - ndiag)
                                            pb.op("dve", lambda e, di=di, Mk=Mk: e.tensor_tensor(
                                                out=wm[:], in0=cmd[:, di, :], in1=Mk[:, 0:128], op=ALU.mult),
                                                reads=[cmd, Mk], writes=[wm])
                                            mask_ap, mask_t = wm[:], wm
                                        else:
                                            mask_ap, mask_t = Mk[:, 0:128], Mk
                                    else:
                                        pb.op("pool", lambda e, kt=kt: e.tensor_scalar(
                                            out=wm[:], in0=trow[:], scalar1=float(-kt * 128), scalar2=pcol[:, 0:1],
                                            op0=ALU.add, op1=ALU.subtract), reads=[trow, pcol], writes=[wm])
                                        pb.op("pool", lambda e: e.tensor_scalar(
                                            out=wm2[:], in0=wm[:], scalar1=0.0, scalar2=None, op0=ALU.is_ge),
                                            reads=[wm], writes=[wm2])
                                        pb.op("pool", lambda e: e.tensor_scalar(
                                            out=wm[:], in0=wm[:], scalar1=511.0, scalar2=None, op0=ALU.is_le),
                                            reads=[wm], writes=[wm])
                                        pb.op("pool", lambda e: e.tensor_tensor(
                                            out=wm[:], in0=wm[:], in1=wm2[:], op=ALU.mult), reads=[wm, wm2],
                                            writes=[wm])
                                        mask_ap, mask_t = wm[:], wm
                                    pb.op("dve", lambda e, eb=eb, emb=emb, mask_ap=mask_ap: e.tensor_tensor(
                                        out=emb[:].rearrange("p (r q) -> p r q", q=128),
                                        in0=eb[:].rearrange("p (r q) -> p r q", q=128),
                                        in1=bc(mask_ap.unsqueeze(1), [128, 4, 128]), op=ALU.mult),
                                        reads=[eb, mask_t], writes=[emb])
                                    for r in range(4):
                                        pb.op("pe", lambda e, r=r, emb=emb, vb=vb, kk=kk, kt=kt: e.matmul(
                                            OB[r][:, 0:129], lhsT=emb[:, r * 128:(r + 1) * 128], rhs=vb[:, kk, 0, :],
                                            start=(kt == k_lo), stop=(kt == nk - 1)), reads=[emb, vb], writes=[OB[r]])
                            gofs = 16 if br == "slc" else 32
                            for r in range(4):
                                head = g * 4 + r
                                finish_head(OB[r], ga_sb[:, t, gofs + head:gofs + head + 1],
                                            oa[:, head * 128:(head + 1) * 128], oa, False)

                    pb.op("pe", [lambda e, h=h: e.matmul(XB[:, h * NMB:(h + 1) * NMB], lhsT=qm[:, h, :],
                                                         rhs=kmeanT[:, h, :], start=True, stop=True)
                                 for h in range(MH)], reads=[qm, kmeanT], writes=[XB])
                    pb.op("dve", lambda e: e.tensor_scalar(out=pastm[:], in0=jrow[:, 0:NMB], scalar1=info[:, 2:3],
                                                           scalar2=None, op0=ALU.is_lt), reads=[jrow, info],
                          writes=[pastm])
                    pb.op("dve", lambda e: e.tensor_scalar(out=ownm[:], in0=jrow[:, 0:NMB], scalar1=info[:, 2:3],
                                                           scalar2=None, op0=ALU.is_equal), reads=[jrow, info],
                          writes=[ownm])
                    pb.op("dve", lambda e: e.tensor_scalar(out=penm[:], in0=pastm[:], scalar1=1.0, scalar2=1e30,
                                                           op0=ALU.subtract, op1=ALU.mult), reads=[pastm], writes=[penm])
                    pb.op("dve", lambda e: e.tensor_tensor(out=msc[:], in0=XB[:, 0:MH * NMB].rearrange(
                        "p (h b) -> p h b", b=NMB), in1=bc(pastm[:].unsqueeze(1), [128, MH, NMB]), op=ALU.mult),
                        reads=[XB, pastm], writes=[msc])
                    pb.op("dve", lambda e: e.tensor_tensor(out=msc[:], in0=msc[:],
                                                           in1=bc(penm[:].unsqueeze(1), [128, MH, NMB]), op=ALU.add),
                          reads=[msc, penm], writes=[msc])
                    for h in range(MH):
                        pb.op("dve", lambda e, h=h: e.max(out=m8m[:, h, :], in_=msc[:, h, :]), reads=[msc],
                              writes=[m8m])
                    for h in range(MH):
                        pb.op("dve", lambda e, h=h: e.scalar_tensor_tensor(
                            out=selm[:, h, :], in0=msc[:, h, :], scalar=m8m[:, h, 2:3], in1=pastm[:], op0=ALU.is_ge,
                            op1=ALU.mult), reads=[msc, m8m, pastm], writes=[selm])
                    pb.op("dve", lambda e: e.tensor_tensor(out=selmb[:, :, 0:NMB], in0=selm[:],
                                                           in1=bc(ownm[:].unsqueeze(1), [128, MH, NMB]), op=ALU.add),
                          reads=[selm, ownm], writes=[selmb])
                    xbb = XB[:].bitcast(BF16)
                    pb.op("pe", [lambda e, h=h: e.transpose(out=xbb[0:64, h * 128:(h + 1) * 128], in_=selmb[:, h, :],
                                                            identity=identb[:]) for h in range(MH)],
                          reads=[selmb, identb], writes=[XB])
                    pb.op("act", lambda e: e.copy(out=selTm[:].rearrange("p h q -> p (h q)"), in_=xbb[0:64, 0:MH * 128]),
                          reads=[XB], writes=[selTm])
                    for hb in range(MH // 4):
                        for ktb in range(0, nk, 4):
                            nkb = min(4, nk - ktb)
                            kb = kbuf[ctr[0] % 2]
                            vb = vbuf[ctr[0] % 2]
                            pb.dma("sp", kb[:, :, 0:nkb * 128], kmT[:, hb * 4:(hb + 1) * 4, ktb * 128:(ktb + nkb) * 128],
                                   reads=[kv_tr], writes=[kb])
                            for kk in range(nkb):
                                pb.dma("sp", vb[:, kk, :, 0:128],
                                       vm_s[(ktb + kk) * 128:(ktb + kk + 1) * 128, hb * 4:(hb + 1) * 4, :],
                                       reads=[kv_tr], writes=[vb])
                            for kk in range(nkb):
                                kt = ktb + kk
                                Sb = SB[ctr[0] % 2]
                                eb = e_sb[ctr[0] % 2]
                                emb = em[ctr[0] % 2]
                                ctr[0] += 1
                                pb.op("pe", [lambda e, h=h, kk=kk, Sb=Sb, kb=kb: e.matmul(
                                    Sb[:, h * 128:(h + 1) * 128], lhsT=kb[:, h, kk * 128:(kk + 1) * 128],
                                    rhs=qm[:, hb * 4 + h, :], start=True, stop=True) for h in range(4)],
                                    reads=[kb, qm], writes=[Sb])
                                Mk = MB if ctr[0] % 2 == 0 else XB
                                pb.op("pe", [lambda e, h=h, kt=kt, Mk=Mk: e.matmul(
                                    Mk[:, h * 128:(h + 1) * 128], lhsT=expm[:, kt // 2, :], rhs=selTm[:, hb * 4 + h, :],
                                    start=True, stop=True) for h in range(4)], reads=[expm, selTm], writes=[Mk])
                                pb.op("act", lambda e, Sb=Sb, eb=eb: e.activation(out=eb[:], in_=Sb[:], func=AF.Exp),
                                      reads=[Sb], writes=[eb])
                                if kt >= nk - ndiag:
                                    di = kt - (nk - ndiag)
                                    pb.op("dve", lambda e, di=di, Mk=Mk: e.tensor_tensor(
                                        out=msk[:].rearrange("p (r q) -> p r q", q=128),
                                        in0=Mk[:].rearrange("p (r q) -> p r q", q=128),
                                        in1=bc(cmd[:, di:di + 1, :], [128, 4, 128]), op=ALU.mult), reads=[Mk, cmd],
                                        writes=[msk])
                                    mask_ap, mask_t = msk[:], msk
                                else:
                                    mask_ap, mask_t = Mk[:], Mk
                                pb.op("dve", lambda e, eb=eb, emb=emb, mask_ap=mask_ap: e.tensor_tensor(
                                    out=emb[:], in0=eb[:], in1=mask_ap, op=ALU.mult), reads=[eb, mask_t], writes=[emb])
                                for h in range(4):
                                    pb.op("pe", lambda e, h=h, emb=emb, vb=vb, kk=kk, kt=kt: e.matmul(
                                        OB[h][:, 0:129], lhsT=emb[:, h * 128:(h + 1) * 128], rhs=vb[:, kk, h, :],
                                        start=(kt == 0), stop=(kt == nk - 1)), reads=[emb, vb], writes=[OB[h]])
                        for h in range(4):
                            hh = hb * 4 + h
                            finish_head(OB[h], None, om[:, hh * 128:(hh + 1) * 128], om, True)
                    pb.op("act", lambda e: e.copy(out=oab[:], in_=oa[:]), reads=[oa], writes=[oab])
                    for half in range(2):
                        transpose_to(oab, 8, lambda k0, n: ostg[:, k0:k0 + n, :], [ostg],
                                     src_fn=lambda k, half=half: oab[:, (half * 8 + k) * 128:(half * 8 + k + 1) * 128],
                                     bank=XB)
                        pb.dma("pool", oT_s[:, half * 8:(half + 1) * 8, t * 128:(t + 1) * 128], ostg[:], reads=[ostg],
                               writes=[grp_tr])
                    transpose_to(om, 8, lambda k0, n: ostg[:, k0:k0 + n, :], [ostg], bank=XB)
                    pb.dma("pool", oT_s[:, 24:32, t * 128:(t + 1) * 128], ostg[:], reads=[ostg], writes=[grp_tr])

        def phase_c1(l, ntile):
            o = c.off
            NTK = ntile * 128
            with Scope(pb) as sc:
                alloc_wbufs(sc)
                hT = sc.sb("c_hT", [128, KC, TOK], BF16)
                oT = sc.sb("c_oT", [128, KC, TOK], BF16)
                sig = [sc.sb(f"c_sig{j}", [128, TOK], F32) for j in range(4)]
                yacc = [sc.sb(f"c_yacc{j}", [128, TOK], F32) for j in range(4)]
                tmp = sc.sb("c_tmp", [128, TOK], F32)
                ystg = sc.sb("c_ystg", [128, 4, TOK], BF16)
                pb.dma("sp", hT[:], hT_s[:, :, :], reads=[grp_tr], writes=[hT])
                pb.dma("sp", oT[:], oT_s[:, :, :], reads=[grp_tr], writes=[oT])
                for cc4 in range(D // 512):
                    col0 = cc4 * 512
                    for bi, (gname, pname, kofs, nkp) in enumerate((("gma", "proj_a", 0, 16), ("gmb", "proj_b", 16, 8),
                                                                    ("gmc", "proj_c", 24, 8))):
                        def epi_gate(j, ps):
                            pb.op("act", lambda e: e.activation(out=sig[j][:, 0:NTK], in_=ps[:, 0:NTK], func=AF.Sigmoid),
                                  reads=[ps], writes=[sig[j]])

                        def epi_proj(j, ps, bi=bi):
                            if bi == 0:
                                pb.op("dve", lambda e: e.tensor_tensor(out=yacc[j][:, 0:NTK], in0=sig[j][:, 0:NTK],
                                                                       in1=ps[:, 0:NTK], op=ALU.mult),
                                      reads=[sig[j], ps], writes=[yacc[j]])
                                return
                            pb.op("dve", lambda e: e.tensor_tensor(out=tmp[:, 0:NTK], in0=sig[j][:, 0:NTK],
                                                                   in1=ps[:, 0:NTK], op=ALU.mult), reads=[sig[j], ps],
                                  writes=[tmp])
                            if bi == 1:
                                pb.op("pool", lambda e: e.tensor_tensor(out=yacc[j][:, 0:NTK], in0=yacc[j][:, 0:NTK],
                                                                        in1=tmp[:, 0:NTK], op=ALU.add),
                                      reads=[yacc[j], tmp], writes=[yacc[j]])
                            else:
                                pb.op("pool", lambda e: e.tensor_tensor(out=ystg[:, j, 0:NTK], in0=yacc[j][:, 0:NTK],
                                                                        in1=tmp[:, 0:NTK], op=ALU.add),
                                      reads=[yacc[j], tmp], writes=[ystg])
                        gemm("w_in", o[gname] + col0, 512, hT, 0, KC, ntile, "feat", epi_gate)
                        gemm(pname, col0, 512, oT, kofs, nkp, ntile, "feat", epi_proj)
                    pb.dma("pool", yT_s[:, cc4 * 4:(cc4 + 1) * 4, :], ystg[:], reads=[ystg], writes=[grp_tr])

        def phase_c2d(l, last, qtiles):
            ntile = len(qtiles)
            NTK = ntile * 128
            with Scope(pb) as scx:
                xacc = scx.sb("x_acc", [128, TG, D], F32)
                for t in range(ntile):
                    pb.dma("sp", xacc[:, t, :], xg_s[t], reads=[grp_tr], writes=[xacc])
                with Scope(pb) as sc:
                    alloc_wbufs(sc)
                    yT = sc.sb("c2_yT", [128, KC, TOK], BF16)
                    pb.dma("sp", yT[:], yT_s[:, :, :], reads=[grp_tr], writes=[yT])
                    for oc in range(D // 512):
                        def epi(t, ps, oc=oc):
                            pb.op("dve", lambda e: e.tensor_tensor(out=xacc[:, t, oc * 512:(oc + 1) * 512],
                                                                   in0=xacc[:, t, oc * 512:(oc + 1) * 512], in1=ps[:],
                                                                   op=ALU.add), reads=[xacc, ps], writes=[xacc])
                        gemm("w_out", oc * 512, 512, yT, 0, KC, ntile, "tok", epi)
                h2T = scx.sb("d_h2T", [128, KC, TOK], BF16)
                with Scope(pb) as sc:
                    sc_tmp["xb"] = sc.sb("t_xb", [128, D], BF16)
                    sc_tmp["sq"] = sc_tmp["xb"]
                    for t in range(ntile):
                        norm_T(sc, xacc, h2T, t, f"d{t}", xap=xacc[:, t, :])
                with Scope(pb) as sc:
                    alloc_wbufs(sc)
                    hid = sc.sb("d_hid", [128, 16, TOK], BF16)
                    rl = [sc.sb(f"d_rl{i}", [128, TOK], F32) for i in range(2)]
                    rli = [0]
                    for part in range(DFF // 2048):
                        for c4 in range(4):
                            def epi1(j, ps, c4=c4):
                                r_ = rl[rli[0] % 2]
                                rli[0] += 1
                                pb.op("act", lambda e: e.activation(out=r_[:, 0:NTK], in_=ps[:, 0:NTK], func=AF.Relu),
                                      reads=[ps], writes=[r_])
                                pb.op("pool", lambda e: e.tensor_tensor(out=hid[:, c4 * 4 + j, 0:NTK], in0=r_[:, 0:NTK],
                                                                        in1=r_[:, 0:NTK], op=ALU.mult), reads=[r_],
                                      writes=[hid])
                            gemm("mlp_w1", part * 2048 + c4 * 512, 512, h2T, 0, KC, ntile, "feat", epi1)
                        for oc in range(D // 512):
                            def epi2(t, ps, oc=oc):
                                pb.op("dve", lambda e: e.tensor_tensor(out=xacc[:, t, oc * 512:(oc + 1) * 512],
                                                                       in0=xacc[:, t, oc * 512:(oc + 1) * 512],
                                                                       in1=ps[:], op=ALU.add), reads=[xacc, ps],
                                      writes=[xacc])
                            gemm("mlp_w2", oc * 512, 512, hid, 0, 16, ntile, "tok", epi2, wrow0=part * 16)
                for t, qi in enumerate(qtiles):
                    if last:
                        pb.dma("pool", y_out[qi * 128:(qi + 1) * 128, :], xacc[:, t, :], reads=[xacc])
                    else:
                        pb.dma("pool", xs[l + 1][qi * 128:(qi + 1) * 128, :], xacc[:, t, :], reads=[xacc],
                               writes=[xs_tr[l + 1]])
                if "xmid" in dbg and False:
                    pass

        for l in range(L):
            last = (l == L - 1)
            with Scope(pb) as sc:
                precast(sc, l)
            with Scope(pb) as sc:
                load_layer_params(sc, l)
            kv_pass(l)
            compress(l)
            kmean(l)
            ntl = NTO if last else NT
            assert ntl % TG == 0
            for g0 in range(0, ntl, TG):
                qtiles = list(range(g0, g0 + TG))
                phase_a(l, last, qtiles)
                phase_b(l, last, qtiles)
                phase_c1(l, TG)
                phase_c2d(l, last, qtiles)
        pb.barrier(engines=("sp",))
        print("instructions:", pb.nins)
    return nc


def host_constants(cfg):
    c = cfg
    bf = ml_dtypes.bfloat16
    ident = np.eye(128, dtype=np.float32)
    tril = np.tril(np.ones((128, 128), np.float32))
    n = np.arange(c.NCC * 128)
    j = np.arange(256)
    ov = ((n[:, None] * 16 <= j[None, :] * 64 + 63) & (n[:, None] * 16 + 31 >= j[None, :] * 64)).astype(np.float32)
    ov[c.NCMP:, :] = 0
    exps = np.zeros((128, 64, 128), np.float32)
    for m in range(64):
        for k in range(128):
            exps[2 * m + k // 64, m, k] = 1.0
    expm = np.zeros((64, 64, 128), np.float32)
    for m in range(64):
        expm[m, m, :] = 1.0
    invf = (10000.0 ** (-np.arange(0, 128, 2, dtype=np.float32) / 128)).astype(np.float32)[None, :]
    jrow = np.arange(256, dtype=np.float32)[None, :]
    return {"c_identb": ident.astype(bf), "c_identf": ident, "c_tril": tril, "c_ov": ov.astype(bf),
            "c_exps": exps.reshape(128, -1).astype(bf), "c_expm": expm.reshape(64, -1).astype(bf), "c_invf": invf,
            "c_jrow": jrow}


def host_inputs(cfg, inputs):
    c = cfg
    consts = host_constants(c)
    x = np.ascontiguousarray(np.asarray(inputs["x"], np.float32).reshape(c.S, c.D))
    pos = np.ascontiguousarray(np.asarray(inputs["positions"], np.int32).reshape(c.S, 1))
    idx_all = np.arange(c.S)
    info_all = np.stack([idx_all, idx_all // 64, idx_all // 256, np.zeros_like(idx_all)], 1).astype(np.float32)
    row_all = idx_all.reshape(c.NT, 128).astype(np.float32)
    maps = []
    for core in range(c.NC):
        tiles = [c.NC * j + core for j in range(c.NTO)]
        own = np.concatenate([np.arange(t * 128, (t + 1) * 128) for t in tiles])
        m = dict(consts)
        m["x"] = x
        m["pos"] = pos
        m["ownpos"] = np.ascontiguousarray(pos[own])
        oh = np.zeros((128, c.NC), np.float32)
        oh[:, core] = 1.0
        m["onehot"] = oh
        m["info_all"] = info_all
        m["info_own"] = np.ascontiguousarray(info_all[own])
        m["row_all"] = row_all
        m["row_own"] = np.ascontiguousarray(own.reshape(c.NTO, 128).astype(np.float32))
        for k, v in inputs.items():
            if k in ("x", "positions"):
                continue
            m[k] = np.ascontiguousarray(np.asarray(v, np.float32))
        maps.append(m)
    return maps


def run(cfg, inputs, debug=(), trace=False):
    nc = build(cfg, debug)
    maps = host_inputs(cfg, inputs)
    res = run_bass_kernel_spmd(nc, maps, core_ids=list(range(cfg.NC)), trace=trace)
    return res


N_CORES_USED = 8


def kernel(**inputs):
    cfg = Cfg(L=1, NC=N_CORES_USED)
    depth = int(np.asarray(inputs["w_in"]).shape[0])
    nc = build(cfg)
    x = np.ascontiguousarray(np.asarray(inputs["x"], np.float32).reshape(cfg.S, cfg.D))
    for l in range(depth):
        inp = {}
        for k, v in inputs.items():
            if k == "x":
                inp[k] = x
            elif k == "positions":
                inp[k] = v
            else:
                inp[k] = np.asarray(v)[l:l + 1]
        maps = host_inputs(cfg, inp)
        res = run_bass_kernel_spmd(nc, maps, core_ids=list(range(cfg.NC)))
        out = np.empty((cfg.S, cfg.D), np.float32)
        for core in range(cfg.NC):
            y = res.results[core]["y"]
            for j in range(cfg.NTO):
                t = cfg.NC * j + core
                out[t * 128:(t + 1) * 128] = y[j * 128:(j + 1) * 128]
        x = out
    return x.reshape(1, cfg.S, cfg.D)
```
